# Optimizing a Trainium2 kernel written in Bass

```python
import math
import jax, jax.numpy as jnp
from jax import lax
import numpy as np

D_MODEL = 1024
BATCH = 32
SEQ = 2048
DEPTH = 2

GRID_W = 64
CTX_LEN = 256
EPS = 1e-6

D_MIX = D_MODEL
A_WIDTH = D_MIX // 4
A_HEADS = 4
A_HEAD_DIM = A_WIDTH // A_HEADS
CHUNK = 128
B_WIDTH = D_MIX // 2
SSM_GROUP = 16
SSM_GROUPS = B_WIDTH // SSM_GROUP
SSM_STATE = 64
C_WIDTH = D_MIX - A_WIDTH - B_WIDTH
POOL_WINDOWS = (2, 4, 8, 16)
POOL_GROUP = C_WIDTH // len(POOL_WINDOWS)
D_IN = 2 * A_WIDTH + B_WIDTH + C_WIDTH
D_FF = ((-(-8 * D_MODEL // 3) + 255) // 256) * 256

kernel_name = "hybrid_gmlp_s5_pool_dit_prefix"


def rms_norm(x, g):
    xf = x.astype(jnp.float32)
    y = xf * lax.rsqrt(jnp.mean(xf * xf, axis=-1, keepdims=True) + EPS)
    return (y * g.astype(jnp.float32)).astype(x.dtype)


def layer_norm(x):
    xf = x.astype(jnp.float32)
    mu = jnp.mean(xf, axis=-1, keepdims=True)
    var = jnp.mean(jnp.square(xf - mu), axis=-1, keepdims=True)
    return ((xf - mu) * lax.rsqrt(var + EPS)).astype(x.dtype)


def modulate(h, shift, scale):
    return h * (1 + scale) + shift


def sincos_2d(rows, cols, dim):
    quarter = dim // 4
    omega = 1.0 / (10000.0 ** (jnp.arange(quarter, dtype=jnp.float32) / quarter))
    r = jnp.arange(rows, dtype=jnp.float32)[:, None] * omega
    cc = jnp.arange(cols, dtype=jnp.float32)[:, None] * omega
    er = jnp.concatenate([jnp.sin(r), jnp.cos(r)], axis=-1)
    ec = jnp.concatenate([jnp.sin(cc), jnp.cos(cc)], axis=-1)
    pe = jnp.concatenate([jnp.broadcast_to(er[:, None, :], (rows, cols, dim // 2)),
                          jnp.broadcast_to(ec[None, :, :], (rows, cols, dim // 2))], axis=-1)
    return pe.reshape(rows * cols, dim)


def spatial_gating(z, w_s, b_s):
    bsz, n, _ = z.shape
    z = jax.nn.gelu(z)
    u, v = jnp.split(z, 2, axis=-1)
    v = layer_norm(v.reshape(bsz, n // CHUNK, CHUNK, A_HEADS, A_HEAD_DIM))
    s = jnp.einsum('hpq,bkqhd->bkphd', w_s, v) + b_s.T[None, None, :, :, None]
    return u * s.reshape(bsz, n, A_WIDTH)


def ssm_discretize(lam_re, lam_im, log_dt, b_re, b_im):
    lam = lax.complex(lam_re.astype(jnp.float32), lam_im.astype(jnp.float32))
    dt = jnp.exp(log_dt.astype(jnp.float32))[:, None]
    lam_bar = jnp.exp(lam * dt)
    b = lax.complex(b_re.astype(jnp.float32), b_im.astype(jnp.float32))
    b_bar = ((lam_bar - 1.0) / lam)[..., None] * b
    return lam_bar, b_bar


def diag_scan(lam_bar, bu, h0, reverse):
    if h0 is not None:
        edge = bu.shape[1] - 1 if reverse else 0
        bu = bu.at[:, edge].add(lam_bar * h0)
    a = jnp.broadcast_to(lam_bar, bu.shape)

    def combine(e1, e2):
        a1, b1 = e1
        a2, b2 = e2
        return a1 * a2, a2 * b1 + b2

    _, h = lax.associative_scan(combine, (a, bu), reverse=reverse, axis=1)
    return h


def ssm_mixer(u_lat, u_ctx, lam_re, lam_im, log_dt, b_re, b_im, c_re, c_im, d, glu_w, glu_b, need_ctx):
    def groups(u):
        return u.reshape(u.shape[0], u.shape[1], SSM_GROUPS, SSM_GROUP).astype(jnp.float32)

    g_lat, g_ctx = groups(u_lat), groups(u_ctx)
    df = d.astype(jnp.float32)
    y_lat = df * g_lat
    y_ctx = df * g_ctx if need_ctx else None
    for k, reverse in enumerate((False, True)):
        lam_bar, b_bar = ssm_discretize(lam_re[k], lam_im[k], log_dt[k], b_re[k], b_im[k])
        cm = lax.complex(c_re[k].astype(jnp.float32), c_im[k].astype(jnp.float32))
        bu_ctx = jnp.einsum('blgh,gph->blgp', g_ctx.astype(jnp.complex64), b_bar)
        h_ctx = diag_scan(lam_bar, bu_ctx, None, reverse)
        h_end = h_ctx[:, 0] if reverse else h_ctx[:, -1]
        bu_lat = jnp.einsum('blgh,gph->blgp', g_lat.astype(jnp.complex64), b_bar)
        h_lat = diag_scan(lam_bar, bu_lat, h_end, reverse)
        y_lat = y_lat + jnp.einsum('ghp,blgp->blgh', cm, h_lat).real
        if need_ctx:
            y_ctx = y_ctx + jnp.einsum('ghp,blgp->blgh', cm, h_ctx).real

    def glu(y, dtype):
        g = jax.nn.gelu(y.reshape(y.shape[0], y.shape[1], B_WIDTH)).astype(dtype)
        return g * jax.nn.sigmoid(g @ glu_w + glu_b)

    out_lat = glu(y_lat, u_lat.dtype)
    out_ctx = glu(y_ctx, u_ctx.dtype) if need_ctx else None
    return out_lat, out_ctx


def window_mean(x, w):
    n = x.shape[-2]
    cs = jnp.cumsum(x.astype(jnp.float32), axis=-2)
    cs = jnp.concatenate([jnp.zeros_like(cs[..., :1, :]), cs], axis=-2)
    t = np.arange(n)
    lo = np.clip(t - w // 2, 0, n)
    hi = np.clip(t - w // 2 + w, 0, n)
    cnt = (hi - lo).astype(np.float32)[:, None]
    return ((jnp.take(cs, hi, axis=-2) - jnp.take(cs, lo, axis=-2)) / cnt).astype(x.dtype)


def pool_mixer(p, pool_w, pool_scale, rows):
    bsz, n, _ = p.shape
    outs = []
    for i, w in enumerate(POOL_WINDOWS):
        pg = p[..., i * POOL_GROUP:(i + 1) * POOL_GROUP]
        if rows is None:
            m = window_mean(pg, w)
        else:
            m = window_mean(pg.reshape(bsz, rows, GRID_W, POOL_GROUP), w).reshape(bsz, n, POOL_GROUP)
        outs.append((m - pg) @ pool_w[i])
    return jnp.concatenate(outs, axis=-1) * pool_scale


def mixing_sublayer(h_lat, h_ctx, rows, need_ctx, w_in, w_out, sgu_w, sgu_b,
                    lam_re, lam_im, log_dt, b_re, b_im, c_re, c_im, d, glu_w, glu_b,
                    pool_w, pool_scale):
    b_lo, b_hi = 2 * A_WIDTH, 2 * A_WIDTH + B_WIDTH
    z_lat = h_lat @ w_in
    if need_ctx:
        z_ctx = h_ctx @ w_in
        u_ctx = z_ctx[..., b_lo:b_hi]
    else:
        u_ctx = h_ctx @ w_in[:, b_lo:b_hi]
    a_lat = spatial_gating(z_lat[..., :b_lo], sgu_w, sgu_b)
    s_lat, s_ctx = ssm_mixer(z_lat[..., b_lo:b_hi], u_ctx, lam_re, lam_im, log_dt, b_re, b_im,
                             c_re, c_im, d, glu_w, glu_b, need_ctx)
    p_lat = pool_mixer(z_lat[..., b_hi:], pool_w, pool_scale, rows)
    m_lat = jnp.concatenate([a_lat, s_lat, p_lat], axis=-1) @ w_out
    m_ctx = None
    if need_ctx:
        a_ctx = spatial_gating(z_ctx[..., :b_lo], sgu_w, sgu_b)
        p_ctx = pool_mixer(z_ctx[..., b_hi:], pool_w, pool_scale, None)
        m_ctx = jnp.concatenate([a_ctx, s_ctx, p_ctx], axis=-1) @ w_out
    return m_lat, m_ctx


def swiglu(h, w_gate, w_up, w_down):
    return (jax.nn.silu(h @ w_gate) * (h @ w_up)) @ w_down


def setup_inputs(seed: int = 0) -> dict:
    key = jax.random.key(seed)
    ks = jax.random.split(key, 32)
    f32 = jnp.float32

    def nrm(k, shape, scale):
        return jax.random.normal(k, shape, f32) * scale

    lam_im0 = math.pi * jnp.arange(SSM_STATE, dtype=f32)
    return {
        "x": nrm(ks[0], (BATCH, SEQ, D_MODEL), 1.0),
        "c": nrm(ks[1], (BATCH, D_MODEL), 1.0),
        "ctx": nrm(ks[2], (BATCH, CTX_LEN, D_MODEL), 1.0),
        "c_ctx": nrm(ks[3], (D_MODEL,), 1.0),
        "w_mod": nrm(ks[4], (DEPTH, D_MODEL, 6 * D_MODEL), 0.5 * D_MODEL ** -0.5),
        "b_mod": nrm(ks[5], (DEPTH, 6 * D_MODEL), 0.02),
        "norm_mix_pre": 1.0 + nrm(ks[6], (DEPTH, D_MODEL), 0.1),
        "norm_mix_post": 1.0 + nrm(ks[7], (DEPTH, D_MODEL), 0.1),
        "norm_ffn_pre": 1.0 + nrm(ks[8], (DEPTH, D_MODEL), 0.1),
        "norm_ffn_post": 1.0 + nrm(ks[9], (DEPTH, D_MODEL), 0.1),
        "w_in": nrm(ks[10], (DEPTH, D_MODEL, D_IN), D_MODEL ** -0.5),
        "w_out": nrm(ks[11], (DEPTH, D_MIX, D_MODEL), D_MIX ** -0.5),
        "sgu_w": nrm(ks[12], (DEPTH, A_HEADS, CHUNK, CHUNK), CHUNK ** -0.5),
        "sgu_b": 1.0 + nrm(ks[13], (DEPTH, A_HEADS, CHUNK), 0.1),
        "ssm_lam_re": -0.5 + nrm(ks[14], (DEPTH, 2, SSM_GROUPS, SSM_STATE), 0.01),
        "ssm_lam_im": lam_im0 + nrm(ks[15], (DEPTH, 2, SSM_GROUPS, SSM_STATE), 0.01),
        "ssm_log_dt": jax.random.uniform(ks[16], (DEPTH, 2, SSM_GROUPS), f32,
                                         minval=math.log(1e-3), maxval=math.log(1e-1)),
        "ssm_b_re": nrm(ks[17], (DEPTH, 2, SSM_GROUPS, SSM_STATE, SSM_GROUP), (2 * SSM_GROUP) ** -0.5),
        "ssm_b_im": nrm(ks[18], (DEPTH, 2, SSM_GROUPS, SSM_STATE, SSM_GROUP), (2 * SSM_GROUP) ** -0.5),
        "ssm_c_re": nrm(ks[19], (DEPTH, 2, SSM_GROUPS, SSM_GROUP, SSM_STATE), SSM_STATE ** -0.5),
        "ssm_c_im": nrm(ks[20], (DEPTH, 2, SSM_GROUPS, SSM_GROUP, SSM_STATE), SSM_STATE ** -0.5),
        "ssm_d": nrm(ks[21], (DEPTH, SSM_GROUPS, SSM_GROUP), 1.0),
        "glu_w": nrm(ks[22], (DEPTH, B_WIDTH, B_WIDTH), B_WIDTH ** -0.5),
        "glu_b": nrm(ks[23], (DEPTH, B_WIDTH), 0.02),
        "pool_w": nrm(ks[24], (DEPTH, len(POOL_WINDOWS), POOL_GROUP, POOL_GROUP), POOL_GROUP ** -0.5),
        "pool_scale": 1.0 + nrm(ks[25], (DEPTH, C_WIDTH), 0.1),
        "ffn_w_gate": nrm(ks[26], (DEPTH, D_MODEL, D_FF), D_MODEL ** -0.5),
        "ffn_w_up": nrm(ks[27], (DEPTH, D_MODEL, D_FF), D_MODEL ** -0.5),
        "ffn_w_down": nrm(ks[28], (DEPTH, D_FF, D_MODEL), D_FF ** -0.5),
    }


def reference(x, c, ctx, c_ctx, w_mod, b_mod, norm_mix_pre, norm_mix_post, norm_ffn_pre, norm_ffn_post,
              w_in, w_out, sgu_w, sgu_b, ssm_lam_re, ssm_lam_im, ssm_log_dt, ssm_b_re, ssm_b_im,
              ssm_c_re, ssm_c_im, ssm_d, glu_w, glu_b, pool_w, pool_scale,
              ffn_w_gate, ffn_w_up, ffn_w_down):
    n_lat = x.shape[1]
    ROWS = n_lat // GRID_W
    x_lat = x + sincos_2d(ROWS, GRID_W, x.shape[-1]).astype(x.dtype)[None]
    x_ctx = ctx
    for i in range(DEPTH):
        need_ctx = i < DEPTH - 1
        mod_lat = jax.nn.silu(c) @ w_mod[i] + b_mod[i]
        mod_ctx = jax.nn.silu(c_ctx) @ w_mod[i] + b_mod[i]
        sh1, sc1, g1, sh2, sc2, g2 = [m[:, None, :] for m in jnp.split(mod_lat, 6, axis=-1)]
        csh1, csc1, cg1, csh2, csc2, cg2 = jnp.split(mod_ctx, 6, axis=-1)

        h_lat = modulate(rms_norm(x_lat, norm_mix_pre[i]), sh1, sc1)
        h_ctx = modulate(rms_norm(x_ctx, norm_mix_pre[i]), csh1, csc1)
        m_lat, m_ctx = mixing_sublayer(h_lat, h_ctx, ROWS, need_ctx, w_in[i], w_out[i], sgu_w[i], sgu_b[i],
                                       ssm_lam_re[i], ssm_lam_im[i], ssm_log_dt[i], ssm_b_re[i], ssm_b_im[i],
                                       ssm_c_re[i], ssm_c_im[i], ssm_d[i], glu_w[i], glu_b[i],
                                       pool_w[i], pool_scale[i])
        x_lat = x_lat + g1 * rms_norm(m_lat, norm_mix_post[i])
        f_lat = swiglu(modulate(rms_norm(x_lat, norm_ffn_pre[i]), sh2, sc2),
                       ffn_w_gate[i], ffn_w_up[i], ffn_w_down[i])
        x_lat = x_lat + g2 * rms_norm(f_lat, norm_ffn_post[i])
        if need_ctx:
            x_ctx = x_ctx + cg1 * rms_norm(m_ctx, norm_mix_post[i])
            f_ctx = swiglu(modulate(rms_norm(x_ctx, norm_ffn_pre[i]), csh2, csc2),
                           ffn_w_gate[i], ffn_w_up[i], ffn_w_down[i])
            x_ctx = x_ctx + cg2 * rms_norm(f_ctx, norm_ffn_post[i])
    return x_lat
```

```python
import math
from contextlib import ExitStack

import numpy as np
import ml_dtypes

import concourse.bass as bass
import concourse.mybir as mybir
from concourse.bass_utils import run_bass_kernel_spmd

F32 = mybir.dt.float32
BF16 = mybir.dt.bfloat16
AF = mybir.ActivationFunctionType
ALU = mybir.AluOpType
AX = mybir.AxisListType

D = 1024
KT = 8
DIN = 1280
DFF = 2816
JT = 22
NCTX = 256
NLAT = 2048
NT = 2304
NCH = 288
NG = 32
NPAIR = 16
EPS = 1e-6
NCORES = 8
TWO_PI = 6.283185307179586
CW1 = 6.28125
CW2 = TWO_PI - 6.28125
MAGIC = 12582912.0
SSMW_COLS = 2304
TAB_COLS = 1152

SEM_CAP = 20000


class T:
    __slots__ = ("w", "r", "x")

    def __init__(self, x=False):
        self.w = None
        self.r = []
        self.x = x


class DmaSem:
    def __init__(self, fw, name):
        self.sem = fw.new_sem(name)
        self.count = 0


class FW:
    ENG = ("pe", "act", "dve", "pool", "sp")

    def __init__(self, nc, stack):
        self.nc = nc
        self.stack = stack
        self.lists = {e: [] for e in self.ENG}
        self.cnt = {e: 0 for e in self.ENG}
        self.sems = {e: [] for e in self.ENG}
        self.known = {e: {} for e in self.ENG}
        self.epoch_known = {e: {} for e in self.ENG}
        self.nops = 0

    def new_sem(self, name):
        return self.stack.enter_context(self.nc.semaphore(name))

    def _eng_sem(self, e, n):
        idx = (n - 1) // SEM_CAP
        while len(self.sems[e]) <= idx:
            self.sems[e].append(self.new_sem(f"s_{e}{len(self.sems[e])}"))
        return ("c", e, idx), self.sems[e][idx], (n - 1) % SEM_CAP + 1

    def _need(self, eng, dep, waits):
        if dep is None:
            return
        if dep[0] == "c":
            _, e, n = dep
            if e == eng and n > self.cnt[e]:
                return
            key, sem, val = self._eng_sem(e, n)
            if self.epoch_known[eng].get(e, -1) > key[2]:
                return
        else:
            _, ds, val = dep
            key, sem = ("d", id(ds)), ds.sem
        if self.known[eng].get(key, 0) >= val:
            return
        cur = waits.get(key)
        if cur is None or cur[1] < val:
            waits[key] = (sem, val)

    def _commit(self, eng, waits):
        for key, (sem, val) in waits.items():
            self.known[eng][key] = val
            if key[0] == "c":
                if self.epoch_known[eng].get(key[1], -1) < key[2]:
                    self.epoch_known[eng][key[1]] = key[2]

    def op(self, eng, fn, reads=(), writes=(), inc=True, dma=None):
        ex = [t for t in reads if t.x]
        if ex:
            reads = [t for t in reads if not t.x]
            writes = list(writes) + ex
        waits = {}
        for t in reads:
            self._need(eng, t.w, waits)
        for t in writes:
            self._need(eng, t.w, waits)
            for r in t.r:
                self._need(eng, r, waits)
        self._commit(eng, waits)
        if dma is not None:
            dma.count += 16
            ticket = ("d", dma, dma.count)
            incinfo = (dma.sem, 16)
        elif inc:
            self.cnt[eng] += 1
            n = self.cnt[eng]
            _, sem, _ = self._eng_sem(eng, n)
            ticket = ("c", eng, n)
            incinfo = (sem, 1)
        else:
            ticket = ("c", eng, self.cnt[eng] + 1)
            incinfo = None
        self.lists[eng].append((list(waits.values()), fn, incinfo))
        self.nops += 1
        for t in reads:
            t.r.append(ticket)
        for t in writes:
            t.w = ticket
            t.r = []
        return ticket

    def barrier(self, dsems=()):
        for e in self.ENG:
            waits = {}
            for e2 in self.ENG:
                if e2 != e and self.cnt[e2] > 0:
                    self._need(e, ("c", e2, self.cnt[e2]), waits)
            for ds in dsems:
                if ds.count:
                    self._need(e, ("d", ds, ds.count), waits)
            self._commit(e, waits)
            if waits:
                self.lists[e].append((list(waits.values()), None, None))

    def wait_dma(self, eng, dsems):
        waits = {}
        for ds in dsems:
            if ds.count:
                self._need(eng, ("d", ds, ds.count), waits)
        self._commit(eng, waits)
        if waits:
            self.lists[eng].append((list(waits.values()), None, None))

    def emit(self):
        nc = self.nc
        lists = self.lists
        self.lists = {e: [] for e in self.ENG}

        def run(h, lst):
            for waits, fn, incinfo in lst:
                for sem, val in waits:
                    h.wait_ge(sem, val)
                if fn is None:
                    continue
                ins = fn(h)
                if incinfo is not None:
                    ins.then_inc(incinfo[0], incinfo[1])

        with nc.Block() as block:
            @block.tensor
            def _(h):
                run(h, lists["pe"])

            @block.scalar
            def _(h):
                run(h, lists["act"])

            @block.vector
            def _(h):
                run(h, lists["dve"])

            @block.gpsimd
            def _(h):
                run(h, lists["pool"])

            @block.sync
            def _(h):
                run(h, lists["sp"])


def fr(ap, dims):
    return bass.AP(ap.tensor, ap.offset, [list(ap.ap[0])] + [list(d) for d in dims])


class Buf:
    def __init__(self, t, J, N, blk):
        self.t, self.J, self.N, self.blk = t, J, N, blk
        self.T = [[T() for _ in range((N + blk - 1) // blk)] for _ in range(J)]

    def ts(self, j0, j1, a, b):
        return [self.T[j][k] for j in range(j0, j1) for k in range(a // self.blk, (b - 1) // self.blk + 1)]


def _pe_tables():
    quarter = D // 4
    omega = (1.0 / (10000.0 ** (np.arange(quarter, dtype=np.float32) / np.float32(quarter)))).astype(np.float32)
    r = np.arange(NLAT // 64, dtype=np.float32)[:, None] * omega
    c = np.arange(64, dtype=np.float32)[:, None] * omega
    er = np.concatenate([np.sin(r), np.cos(r)], axis=-1).astype(np.float32)
    ec = np.concatenate([np.sin(c), np.cos(c)], axis=-1).astype(np.float32)
    erT = np.ascontiguousarray(er.T.reshape(4, 128, 32).transpose(1, 0, 2))
    ecT = np.ascontiguousarray(ec.T.reshape(4, 128, 64).transpose(1, 0, 2))
    return erT, ecT


def _avg_mats():
    wins = (2, 4, 8, 16)

    def amat(n, w):
        t = np.arange(n)
        lo = np.clip(t - w // 2, 0, n)
        hi = np.clip(t - w // 2 + w, 0, n)
        A = np.zeros((n, n), np.float32)
        for i in range(n):
            A[i, lo[i]:hi[i]] = 1.0 / float(hi[i] - lo[i])
        return A - np.eye(n, dtype=np.float32)

    avgL = np.zeros((128, 4, 128), np.float32)
    avgC = np.zeros((128, 4, 2, 2, 128), np.float32)
    for i, w in enumerate(wins):
        a64 = amat(64, w).T
        avgL[0:64, i, 0:64] = a64
        avgL[64:128, i, 64:128] = a64
        a256 = amat(256, w).T
        for st in range(2):
            for tt in range(2):
                avgC[:, i, st, tt, :] = a256[st * 128:(st + 1) * 128, tt * 128:(tt + 1) * 128]
    return avgL, avgC


def _consts():
    c = {}
    c["ident"] = np.eye(128, dtype=np.float32)
    s_i = np.arange(128) // 16
    mf = (s_i[None, :] >= s_i[:, None]).astype(np.float32)
    mb = (s_i[:, None] >= s_i[None, :]).astype(np.float32)
    c["masks"] = np.ascontiguousarray(np.stack([mf, mb], axis=1))
    c["expv"] = np.ascontiguousarray(np.broadcast_to(np.arange(-8, 9, dtype=np.float32), (128, 17)))
    sg = np.ones((128, 2), np.float32)
    sg[0:64, 0] = -1.0
    sg[64:128, 1] = -1.0
    c["sgn"] = sg
    idx = np.zeros((128, 2, NCH), np.float32)
    idx[:, 0, :] = np.arange(NCH, dtype=np.float32)
    idx[:, 1, 0:32] = 256.0 + np.arange(32, dtype=np.float32)
    idx[:, 1, 32:] = np.arange(256, dtype=np.float32)
    c["idx"] = idx
    dsel = np.zeros((128, 8, 240), np.float32)
    for a in range(8):
        for i in range(16):
            dsel[16 * a + i, a, 112 + i] = 1.0
    c["dsel"] = dsel
    c["erT"], c["ecT"] = _pe_tables()
    c["avgL"], c["avgC"] = _avg_mats()
    return c


def _chunkT(v):
    v = np.asarray(v, np.float32)
    lead = v.shape[:-1]
    n = v.shape[-1] // 128
    v = v.reshape(lead + (n, 128))
    v = np.moveaxis(v, -1, 0)
    return np.ascontiguousarray(v)


def _prep_shared(inp):
    f = lambda k: np.asarray(inp[k], np.float32)
    sh = dict(_consts())
    for k in ("w_mod", "w_in", "w_out", "glu_w", "ffn_w_gate", "ffn_w_up", "ffn_w_down"):
        sh[k] = np.ascontiguousarray(f(k))
    sh["bmodT"] = _chunkT(f("b_mod"))
    sh["gains"] = np.ascontiguousarray(np.stack([_chunkT(f(k)) for k in
                                                 ("norm_mix_pre", "norm_mix_post", "norm_ffn_pre", "norm_ffn_post")], axis=2))
    sh["glubT"] = _chunkT(f("glu_b"))
    sh["pscaleT"] = _chunkT(f("pool_scale"))
    sh["sguT"] = np.ascontiguousarray(f("sgu_w").transpose(3, 0, 1, 2))
    sh["sgub"] = np.ascontiguousarray(f("sgu_b").reshape(1, 2 * 4 * 128))
    pw = f("pool_w")
    poolw = np.zeros((128, 2, 2, 128), np.float32)
    for l in range(2):
        for m in range(2):
            for q in range(2):
                poolw[q * 64:(q + 1) * 64, l, m, q * 64:(q + 1) * 64] = pw[l, 2 * m + q]
    sh["poolw"] = poolw
    def pp(a):
        a = a.transpose(3, 0, 1, 2).reshape(64, 2, 64)
        return np.ascontiguousarray(np.concatenate([a, a], axis=0))
    sh["lamre"] = pp(f("ssm_lam_re"))
    sh["lamim"] = pp(f("ssm_lam_im"))
    sh["logdt"] = np.ascontiguousarray(np.broadcast_to(f("ssm_log_dt").reshape(1, 2, 64), (128, 2, 64)))
    bre = f("ssm_b_re").transpose(3, 0, 1, 2, 4).reshape(64, 2, 64, 16)
    bim = f("ssm_b_im").transpose(3, 0, 1, 2, 4).reshape(64, 2, 64, 16)
    sh["b_x"] = np.ascontiguousarray(np.concatenate([bre, bim], axis=0))
    sh["b_y"] = np.ascontiguousarray(np.concatenate([bim, bre], axis=0))
    cre = f("ssm_c_re").transpose(4, 0, 1, 2, 3).reshape(64, 2, 64, 16)
    cim = f("ssm_c_im").transpose(4, 0, 1, 2, 3).reshape(64, 2, 64, 16)
    sh["c_x"] = np.ascontiguousarray(np.concatenate([cre, cim], axis=0))
    sh["c_y"] = np.ascontiguousarray(np.concatenate([cim, cre], axis=0))
    d = f("ssm_d")
    sh["d_rep"] = np.ascontiguousarray(np.broadcast_to(d.transpose(2, 0, 1)[None], (8, 16, 2, 32)).reshape(128, 2, 32))
    return sh


class _Stop(Exception):
    pass


class Gen:
    def _chk(self, name):
        if self.stop == name:
            raise _Stop()

    def __init__(self, nb, dbg=(), nlayers=2, skip=(), stop=None):
        self.skip = set(skip)
        self.stop = stop
        self.nb = nb
        self.dbg = set(dbg)
        self.nlayers = nlayers
        self.nc = bass.Bass("TRN2", target_bir_lowering=False)
        self.dr = {}
        self.dbg_outs = {}

    def din(self, name, shape, dt=F32):
        self.dr[name] = self.nc.dram_tensor(name, list(shape), dt, kind="ExternalInput").ap()

    def dscr(self, name, shape, dt):
        self.dr[name] = self.nc.dram_tensor(name, list(shape), dt, kind="Internal").ap()

    def act(self, out, in_, func, reads, writes, bias=None, scale=None):
        kw = {}
        if bias is not None:
            kw["bias"] = bias
        if scale is not None:
            kw["scale"] = scale
        return self.fw.op("act", lambda h: h.activation(out, in_, func, **kw), reads, writes)

    def tt(self, eng, out, a, b, op, reads, writes):
        return self.fw.op(eng, lambda h: h.tensor_tensor(out, a, b, op), reads, writes)

    def tsc(self, out, in_, s1, s2, op0, op1, reads, writes, eng="dve"):
        if s2 is None:
            return self.fw.op(eng, lambda h: h.tensor_scalar(out, in_, s1, None, op0), reads, writes)
        return self.fw.op(eng, lambda h: h.tensor_scalar(out, in_, s1, s2, op0, op1), reads, writes)

    def stt(self, out, in0, scalar, in1, op0, op1, reads, writes, eng="dve"):
        return self.fw.op(eng, lambda h: h.scalar_tensor_tensor(out, in0, scalar, in1, op0, op1), reads, writes)

    def cp(self, eng, out, in_, reads, writes):
        if eng == "act":
            return self.fw.op("act", lambda h: h.activation(out, in_, AF.Copy), reads, writes)
        return self.fw.op(eng, lambda h: h.tensor_copy(out, in_), reads, writes)

    def dma(self, q, out, in_, reads, writes, sem):
        return self.fw.op(q, lambda h: h.dma_start(out=out, in_=in_), reads, writes, dma=sem)

    def mm(self, out, lhsT, rhs, start, stop, reads, writes, inc=None):
        if inc is None:
            inc = stop
        return self.fw.op("pe", lambda h: h.matmul(out, lhsT, rhs, start=start, stop=stop), reads, writes, inc=inc)

    def tr(self, out, in_, ident, reads, writes):
        return self.fw.op("pe", lambda h: h.transpose(out, in_, ident), reads, writes)

    def scan(self, out, d0, d1, init, reads, writes, eng="dve"):
        return self.fw.op(eng, lambda h: h.tensor_tensor_scan(out, d0, d1, init, ALU.mult, ALU.add), reads, writes)

    def rr(self, out, in_, kk, reads, writes, tk):
        self.tsc(kk, in_, 1.0 / TWO_PI, MAGIC, ALU.mult, ALU.add, reads, [tk])
        self.tsc(kk, kk, MAGIC, None, ALU.subtract, None, [tk], [tk])
        self.stt(out, kk, -CW1, in_, ALU.mult, ALU.add, [tk] + list(reads), writes)
        self.stt(out, kk, -CW2, out, ALU.mult, ALU.add, [tk] + list(writes), writes)

    def dump(self, name, ap, reads, dt=F32):
        if name not in self.dbg:
            return
        o = self.nc.dram_tensor("dbg_" + name, list(ap.shape), dt, kind="ExternalOutput").ap()
        self.dbg_outs[name] = o
        self.dma("sp", o, ap, reads, [], self.dsem_dbg)

    def build(self):
        nc, nb = self.nc, self.nb
        din = self.din
        for name, shape in (("ident", (128, 128)), ("masks", (128, 2, 128)), ("expv", (128, 17)), ("sgn", (128, 2)),
                            ("idx", (128, 2, NCH)), ("dsel", (128, 8, 240)), ("erT", (128, 4, 32)), ("ecT", (128, 4, 64)),
                            ("avgL", (128, 4, 128)), ("avgC", (128, 4, 2, 2, 128)),
                            ("w_mod", (2, D, 6 * D)), ("w_in", (2, D, DIN)), ("w_out", (2, D, D)), ("glu_w", (2, 512, 512)),
                            ("ffn_w_gate", (2, D, DFF)), ("ffn_w_up", (2, D, DFF)), ("ffn_w_down", (2, DFF, D)),
                            ("bmodT", (128, 2, 48)), ("gains", (128, 2, 4, 8)), ("glubT", (128, 2, 4)), ("pscaleT", (128, 2, 2)),
                            ("sguT", (128, 2, 4, 128)), ("sgub", (1, 1024)), ("poolw", (128, 2, 2, 128)),
                            ("lamre", (128, 2, 64)), ("lamim", (128, 2, 64)), ("logdt", (128, 2, 64)),
                            ("b_x", (128, 2, 64, 16)), ("b_y", (128, 2, 64, 16)), ("c_x", (128, 2, 64, 16)), ("c_y", (128, 2, 64, 16)),
                            ("d_rep", (128, 2, 32)),
                            ("x", (nb, NLAT, D)), ("ctx", (nb, NCTX, D)), ("cT", (128, KT, nb + 1))):
            din(name, shape)
        self.dr["out"] = nc.dram_tensor("out", [nb, NLAT, D], F32, kind="ExternalOutput").ap()
        self.dscr("wsc_in", (2, 10, 128, KT, 128), BF16)
        self.dscr("wsc_out", (2, 8, 128, KT, 128), BF16)
        self.dscr("wsc_glu", (2, 128, 4, 512), BF16)
        self.dscr("wsc_g", (2, JT, 128, KT, 128), BF16)
        self.dscr("wsc_u", (2, JT, 128, KT, 128), BF16)
        self.dscr("wsc_d", (2, 8, 128, JT, 128), BF16)
        self.dscr("ssmw", (2, NPAIR, 128, SSMW_COLS), BF16)
        self.dscr("tab", (2, 2, NPAIR, 128, TAB_COLS), F32)

        with ExitStack() as st:
            self.st = st
            fw = self.fw = FW(nc, st)
            self.dsem_dbg = DmaSem(fw, "dbg")
            self.alloc_persistent()
            self.preamble()
            fw.barrier([self.dsem_w, self.dsem_c, self.dsem_c2, self.dsem_s])
            self.dump("ssmw", self.dr["ssmw"], [], BF16)
            self.dump("tab", self.dr["tab"], [], F32)
            fw.emit()
            if "main" not in self.skip:
                self.main()
            fw.wait_dma("sp", [self.dsem_dbg, self.dsem_w, self.dsem_s, self.dsem_c, self.dsem_c2])
            fw.emit()
        return nc

    def SB(self, st, name, shape, dt=F32):
        return st.enter_context(self.nc.sbuf_tensor("sb_" + name, list(shape), dt))

    def alloc_persistent(self):
        st, nb = self.st, self.nb
        SB = lambda n, s, d=F32: self.SB(st, n, s, d)
        self.ident = SB("ident", (128, 128))
        self.ones = SB("ones", (128, 128), BF16)
        self.dselb = SB("dselb", (128, 8, 240), BF16)
        self.erT = SB("erT", (128, 4, 32))
        self.ecT = SB("ecT", (128, 4, 64))
        self.avgLb = SB("avgLb", (128, 4, 128), BF16)
        self.avgCb = SB("avgCb", (128, 4, 2, 2, 128), BF16)
        self.sguTb = SB("sguTb", (128, 2, 4, 128), BF16)
        self.sgubb = SB("sgubb", (1, 1024), BF16)
        self.poolwb = SB("poolwb", (128, 2, 2, 128), BF16)
        self.glub = SB("glub", (128, 2, 4))
        self.pscale = SB("pscale", (128, 2, 2))
        self.PAR = SB("PAR", (128, 2, 6, KT, nb + 1))
        self.RHO = SB("RHO", (128, 2, 2, NPAIR))
        self.tC = T()
        self.dsem_w = DmaSem(self.fw, "dw")
        self.dsem_c = DmaSem(self.fw, "dc")
        self.dsem_c2 = DmaSem(self.fw, "dc2")
        self.dsem_s = DmaSem(self.fw, "ds")

    def preamble(self):
        nc, fw, dr, nb = self.nc, self.fw, self.dr, self.nb
        tC = self.tC
        for dst, src in ((self.ident, "ident"), (self.erT, "erT"), (self.ecT, "ecT"), (self.glub, "glubT"), (self.pscale, "pscaleT")):
            self.dma("sp", dst[:], dr[src], [], [tC], self.dsem_c)
        for dst, src in ((self.dselb, "dsel"), (self.avgLb, "avgL"), (self.avgCb, "avgC"), (self.sguTb, "sguT"), (self.sgubb, "sgub"), (self.poolwb, "poolw")):
            self.dma("pool", dst[:], dr[src], [], [tC], self.dsem_c2)
        fw.op("dve", lambda h: h.memset(self.ones[:], 1.0), [], [tC])
        fw.wait_dma("dve", [self.dsem_c, self.dsem_c2])
        fw.wait_dma("act", [self.dsem_c, self.dsem_c2])
        fw.wait_dma("pe", [self.dsem_c, self.dsem_c2])
        fw.wait_dma("pool", [self.dsem_c, self.dsem_c2])

        with ExitStack() as ps_st:
            self.pps = ps_st.enter_context(nc.psum_tensor("pps", [128, 8, 512], F32))
            if "mod" not in self.skip:
                self.pre_mod()
            else:
                self.weight_casts()
            if "ssmgen" not in self.skip:
                self.pre_ssm()
            fw.barrier([self.dsem_w, self.dsem_c, self.dsem_c2, self.dsem_s])
            fw.emit()

    def weight_casts(self):
        dr = self.dr
        for l in range(0 if "wcast" not in self.skip else 2, 2):
            for s in range(10):
                self.dma("pool", dr["wsc_in"][l, s], dr["w_in"][l, :, s * 128:(s + 1) * 128].rearrange("(kt p) c -> p kt c", p=128), [], [], self.dsem_w)
            for m in range(8):
                self.dma("pool", dr["wsc_out"][l, m], dr["w_out"][l, :, m * 128:(m + 1) * 128].rearrange("(kt p) c -> p kt c", p=128), [], [], self.dsem_w)
                self.dma("pool", dr["wsc_d"][l, m], dr["ffn_w_down"][l, :, m * 128:(m + 1) * 128].rearrange("(j p) c -> p j c", p=128), [], [], self.dsem_w)
            self.dma("pool", dr["wsc_glu"][l], dr["glu_w"][l].rearrange("(kt p) c -> p kt c", p=128), [], [], self.dsem_w)
            for j in range(JT):
                self.dma("pool", dr["wsc_g"][l, j], dr["ffn_w_gate"][l, :, j * 128:(j + 1) * 128].rearrange("(kt p) c -> p kt c", p=128), [], [], self.dsem_w)
                self.dma("pool", dr["wsc_u"][l, j], dr["ffn_w_up"][l, :, j * 128:(j + 1) * 128].rearrange("(kt p) c -> p kt c", p=128), [], [], self.dsem_w)

    def pre_mod(self):
        nc, fw, dr, nb = self.nc, self.fw, self.dr, self.nb
        tC = self.tC
        with ExitStack() as st:
            SB = lambda n, s, d=F32: self.SB(st, n, s, d)
            cT = SB("cT_sb", (128, KT, nb + 1))
            cs = SB("cs_sb", (128, KT, nb + 1), BF16)
            bmod = SB("bmod_sb", (128, 2, 48))
            gains = SB("gains_sb", (128, 2, 4, KT))
            modT = SB("modT_sb", (128, 2, 48, nb + 1))
            tmp = SB("modtmp", (128, KT, nb + 1))
            ring = [SB(f"wmod{i}", (128, KT, 512), BF16) for i in range(2)]
            rT = [T(), T()]
            rS = [DmaSem(fw, f"wm{i}") for i in range(2)]
            t0, tm = T(), T()
            dl = DmaSem(fw, "modld")
            self.dma("sp", cT[:], dr["cT"], [], [t0], dl)
            self.dma("sp", bmod[:], dr["bmodT"], [], [t0], dl)
            self.dma("sp", gains[:], dr["gains"], [], [t0], dl)
            self.act(cs[:], cT[:], AF.Silu, [t0], [t0])
            tps = T(True)
            n = 0
            for l in range(2):
                for s in range(12):
                    r = n % 2
                    n += 1
                    self.dma("pool", ring[r][:], dr["w_mod"][l, :, s * 512:(s + 1) * 512].rearrange("(kt p) c -> p kt c", p=128), [], [rT[r]], rS[r])
                    for f in range(4):
                        t_idx = s * 4 + f
                        for kt in range(KT):
                            self.mm(self.pps[:, l, t_idx * 8:t_idx * 8 + nb + 1], ring[r][:, kt, f * 128:(f + 1) * 128], cs[:, kt, :],
                                    kt == 0, kt == KT - 1, [rT[r], t0], [tps])
                self.tt("dve", modT[:, l], fr(self.pps[:, l, 0:1], [[8, 48], [1, nb + 1]]), fr(bmod[:, l, 0:1], [[1, 48], [0, nb + 1]]),
                        ALU.add, [tps, t0], [tm])
            PAR = self.PAR
            for l in range(2):
                def gb(kind):
                    return fr(gains[:, l, kind, 0:1], [[1, KT], [0, nb + 1]])
                self.tsc(tmp[:], modT[:, l, 8:16, :], 1.0, None, ALU.add, None, [tm], [tm])
                self.tt("dve", PAR[:, l, 0], tmp[:], gb(0), ALU.mult, [tm, t0], [tC])
                self.cp("dve", PAR[:, l, 1], modT[:, l, 0:8, :], [tm], [tC])
                self.tt("dve", PAR[:, l, 2], modT[:, l, 16:24, :], gb(1), ALU.mult, [tm, t0], [tC])
                self.tsc(tmp[:], modT[:, l, 32:40, :], 1.0, None, ALU.add, None, [tm], [tm])
                self.tt("dve", PAR[:, l, 3], tmp[:], gb(2), ALU.mult, [tm, t0], [tC])
                self.cp("dve", PAR[:, l, 4], modT[:, l, 24:32, :], [tm], [tC])
                self.tt("dve", PAR[:, l, 5], modT[:, l, 40:48, :], gb(3), ALU.mult, [tm, t0], [tC])
            self.dump("par", PAR[:], [tC])
            self.weight_casts()
            fw.barrier()
            fw.emit()

    def pre_ssm(self):
        nc, fw, dr, nb = self.nc, self.fw, self.dr, self.nb
        tC = self.tC
        PI = math.pi
        with ExitStack() as st:
            SB = lambda n, s, d=F32: self.SB(st, n, s, d)
            BIG = [SB(f"big{i}", (128, 4608)) for i in range(5)]
            tB = [T() for _ in range(5)]
            MTMP = SB("mtmp", (128, 32, 128)); tMT = T()
            STG = SB("stg", (128, NPAIR, 4, 128), BF16); tSTG = T()
            MST = SB("mst", (128, NPAIR, 2, 128), BF16); tMST = T()
            lamre = SB("lamre_sb", (128, 2, 64)); lamim = SB("lamim_sb", (128, 2, 64)); logdt = SB("logdt_sb", (128, 2, 64))
            masks = SB("masks_sb", (128, 2, 128)); expv = SB("expv_sb", (128, 17)); sgn = SB("sgn_sb", (128, 2))
            idx = SB("idx_sb", (128, 2, NCH)); drep = SB("drep_sb", (128, 2, 32))
            tI = T()
            dl = DmaSem(fw, "ssmld")
            for dst, src in ((lamre, "lamre"), (lamim, "lamim"), (logdt, "logdt"), (masks, "masks"), (expv, "expv"),
                             (sgn, "sgn"), (idx, "idx"), (drep, "d_rep")):
                self.dma("sp", dst[:], dr[src], [], [tI], dl)
            fw.wait_dma("dve", [dl]); fw.wait_dma("act", [dl]); fw.wait_dma("pe", [dl])
            DT = SB("dt_sb", (128, 64)); A_ = SB("a_sb", (128, 64)); TH = SB("th_sb", (128, 64))
            AE = SB("ae_sb", (128, 64, 17)); TE = SB("te_sb", (128, 64, 17))
            CS = SB("cs2_sb", (128, 64, 17)); LY = SB("ly_sb", (128, 64, 17))
            KK = LY
            tL = T(); tK = tL
            sm = [SB(f"sm{i}", (128, 64)) for i in range(8)]
            tS = T()
            bx = SB("bx_sb", (128, 64, 16)); by = SB("by_sb", (128, 64, 16))
            cx = SB("cx_sb", (128, 64, 16)); cy = SB("cy_sb", (128, 64, 16))
            BX = SB("BX_sb", (128, 64, 16)); BY = SB("BY_sb", (128, 64, 16))
            tBC = T()
            PHQ = SB("phq_sb", (128, 2, NPAIR)); tPH = T()
            dbc = DmaSem(fw, "bcld")
            STG2, tSTG2 = STG, tSTG
            sgn_m = sgn[:, 0:1]
            sgn_pm = sgn[:, 1:2]
            pq = [0]

            def psq():
                i = pq[0] % 6
                pq[0] += 1
                if not hasattr(self, "_pqT"):
                    self._pqT = [T(True) for _ in range(6)]
                return self.pps[:, 2 + i, 0:128], self._pqT[i]

            try:
              for l in range(2):
                lr = lamre[:, l, :]; li = lamim[:, l, :]
                self.dma("sp", bx[:], dr["b_x"][:, l], [], [tBC], dbc)
                self.dma("sp", by[:], dr["b_y"][:, l], [], [tBC], dbc)
                self.dma("sp", cx[:], dr["c_x"][:, l], [], [tBC], dbc)
                self.dma("sp", cy[:], dr["c_y"][:, l], [], [tBC], dbc)
                self.act(DT[:], logdt[:, l, :], AF.Exp, [tI, tS], [tS])
                self.tt("dve", A_[:], lr, DT[:], ALU.mult, [tI, tS], [tS])
                self.tt("dve", TH[:], li, DT[:], ALU.mult, [tI, tS], [tS])
                a_bc = fr(A_[:, 0:1], [[1, 64], [0, 17]])
                th_bc = fr(TH[:, 0:1], [[1, 64], [0, 17]])
                e_bc = fr(expv[:, 0:1], [[0, 64], [1, 17]])
                self.tt("dve", AE[:], a_bc, e_bc, ALU.mult, [tS, tI], [tL])
                self.act(AE[:], AE[:], AF.Exp, [tL], [tL])
                self.tt("dve", TE[:], th_bc, e_bc, ALU.mult, [tS, tI], [tL])
                self.rr(TE[:], TE[:], KK[:], [tL], [tL], tK)
                self.tsc(CS[:], TE[:], PI / 2, None, ALU.add, None, [tL], [tL])
                self.rr(CS[:], CS[:], KK[:], [tL], [tL], tK)
                self.act(TE[:], TE[:], AF.Sin, [tL], [tL])
                self.act(CS[:], CS[:], AF.Sin, [tL], [tL])
                self.tt("dve", CS[:], CS[:], AE[:], ALU.mult, [tL], [tL])
                self.tt("dve", TE[:], TE[:], AE[:], ALU.mult, [tL], [tL])
                self.tsc(LY[:], TE[:], sgn_m, None, ALU.mult, None, [tL, tI], [tL])
                LX = CS
                LI = TE
                self.dump("LX", LX[:], [tL]); self.dump("LI", LI[:], [tL])
                self._chk("ssmA")
                lbr = fr(LX[:, 0:1, 9:10], [[17, 64]])
                lbi = fr(LI[:, 0:1, 9:10], [[17, 64]])
                xr, den, t1, t2, br_, bi_, bis, e8 = [s_[:] for s_ in sm]
                self.tsc(xr, lbr, -1.0, None, ALU.add, None, [tL], [tS])
                self.tt("dve", den, lr, lr, ALU.mult, [tI], [tS])
                self.tt("dve", t1, li, li, ALU.mult, [tI], [tS])
                self.tt("dve", den, den, t1, ALU.add, [tS], [tS])
                fw.op("dve", lambda h: h.reciprocal(den, den), [tS], [tS])
                self.tt("dve", t1, xr, lr, ALU.mult, [tS, tI], [tS])
                self.tt("dve", t2, lbi, li, ALU.mult, [tL, tI], [tS])
                self.tt("dve", t1, t1, t2, ALU.add, [tS], [tS])
                self.tt("dve", br_, t1, den, ALU.mult, [tS], [tS])
                self.tt("dve", t1, lbi, lr, ALU.mult, [tL, tI], [tS])
                self.tt("dve", t2, xr, li, ALU.mult, [tS, tI], [tS])
                self.tt("dve", t1, t1, t2, ALU.subtract, [tS], [tS])
                self.tt("dve", bi_, t1, den, ALU.mult, [tS], [tS])
                self.tsc(bis, bi_, sgn_m, None, ALU.mult, None, [tS, tI], [tS])
                br_bc = fr(br_[:, 0:1], [[1, 64], [0, 16]])
                bis_bc = fr(bis[:, 0:1], [[1, 64], [0, 16]])
                w1 = fr(BIG[0][:, 0:1], [[16, 64], [1, 16]])
                w2 = fr(BIG[1][:, 0:1], [[16, 64], [1, 16]])
                fw.wait_dma("dve", [dbc])
                self.tt("dve", w1, bx[:], br_bc, ALU.mult, [tBC, tS], [tB[0]])
                self.tt("dve", w2, by[:], bis_bc, ALU.mult, [tBC, tS], [tB[1]])
                self.tt("dve", BX[:], w1, w2, ALU.add, [tB[0], tB[1]], [tBC])
                self.tt("dve", w1, by[:], br_bc, ALU.mult, [tBC, tS], [tB[0]])
                self.tt("dve", w2, bx[:], bis_bc, ALU.mult, [tBC, tS], [tB[1]])
                self.tt("dve", BY[:], w1, w2, ALU.subtract, [tB[0], tB[1]], [tBC])
                self.tsc(cx[:], cx[:], sgn_pm, None, ALU.mult, None, [tBC, tI], [tBC])
                self.tsc(cy[:], cy[:], sgn_pm, None, ALU.mult, None, [tBC, tI], [tBC])
                self.act(e8, A_[:], AF.Exp, [tS], [tS], scale=8.0)
                for gq in range(2):
                    self.cp("dve", self.RHO[gq * 64:(gq + 1) * 64, l], fr(e8[gq * 64:(gq + 1) * 64, gq:gq + 1], [[32, 2], [2, NPAIR]]), [tS], [tC])
                self.tsc(t1, TH[:], 8.0, None, ALU.mult, None, [tS], [tS])
                self.rr(t1, t1, t2, [tS], [tS], tS)
                for gq in range(2):
                    self.cp("dve", PHQ[gq * 64:(gq + 1) * 64], fr(t1[gq * 64:(gq + 1) * 64, gq:gq + 1], [[32, 2], [2, NPAIR]]), [tS], [tPH])

                self.dump("BX", BX[:], [tBC]); self.dump("cx", cx[:], [tBC]); self.dump("RHO", self.RHO[:], [tC]); self.dump("PHQ", PHQ[:], [tPH])
                self._chk("ssmB")
                for k in range(2):
                    if k == 0:
                        iB, sB, iBp, sBp, iC, sC = 15, -1, 7, -1, 9, 1
                    else:
                        iB, sB, iBp, sBp, iC, sC = 8, 1, 0, 1, 16, -1

                    def build(dst, i0, stp, X, Y, tsrc):
                        lx = fr(LX[:, k * 32:k * 32 + 1, i0:i0 + 1], [[17, 32], [stp, 8], [0, 16]])
                        ly = fr(LY[:, k * 32:k * 32 + 1, i0:i0 + 1], [[17, 32], [stp, 8], [0, 16]])
                        xv = fr(X[:, k * 32:k * 32 + 1, 0:1], [[16, 32], [0, 8], [1, 16]])
                        yv = fr(Y[:, k * 32:k * 32 + 1, 0:1], [[16, 32], [0, 8], [1, 16]])
                        o4 = lambda b: fr(b[:, 0:1], [[128, 32], [16, 8], [1, 16]])
                        self.tt("dve", o4(BIG[0]), lx, xv, ALU.mult, [tL, tsrc], [tB[0]])
                        self.tt("dve", o4(BIG[1]), ly, yv, ALU.mult, [tL, tsrc], [tB[1]])
                        self.tt("dve", o4(BIG[dst]), o4(BIG[0]), o4(BIG[1]), ALU.add, [tB[0], tB[1]], [tB[dst]])

                    build(2, iB, sB, BX, BY, tBC)
                    build(3, iBp, sBp, BX, BY, tBC)
                    build(4, iC, sC, cx, cy, tBC)
                    Bt, Bpt, Ct = BIG[2], BIG[3], BIG[4]
                    self.dump("Bt", Bt[:], [tB[2]]); self.dump("Ct", Ct[:], [tB[4]])
                    self._chk("ssmC")
                    stg = STG
                    fw.op("dve", lambda h: h.memset(STG[:], 0.0), [], [tSTG])
                    for g in range(NG):
                        q, gq = g // 2, g % 2
                        pa, pt = psq()
                        self.tr(pa, Bt[:, g * 128:(g + 1) * 128], self.ident[:], [tB[2], tC], [pt])
                        self.cp("act", stg[:, q, gq, gq * 64:(gq + 1) * 64], pa[:, 0:64], [pt], [tSTG])
                        self.cp("act", stg[:, q, 2 + gq, gq * 64:(gq + 1) * 64], pa[:, 64:128], [pt], [tSTG])
                        pm, pmt = psq()
                        self.mm(pm, Bpt[:, g * 128:(g + 1) * 128], Ct[:, g * 128:(g + 1) * 128], True, True, [tB[3], tB[4]], [pmt])
                        if k == 0:
                            self.tt("dve", MTMP[:, g, :], pm, masks[:, 0, :], ALU.mult, [pmt, tI], [tMT])
                            self.stt(MTMP[:, g, :], self.ident[:], drep[:, l, g:g + 1], MTMP[:, g, :], ALU.mult, ALU.add, [tC, tI, tMT], [tMT])
                        else:
                            self.tt("dve", BIG[0][:, 0:128], pm, masks[:, 1, :], ALU.mult, [pmt, tI], [tB[0]])
                            self.tt("dve", MST[:, q, gq, :], BIG[0][:, 0:128], MTMP[:, g, :], ALU.add, [tB[0], tMT], [tMST])
                        if g == 0:
                            self._chk("ssmD1")
                    self._chk("ssmD2")
                    c0 = 256 + k * 512
                    dview = dr["ssmw"][l].rearrange("q p c -> p q c")
                    self.dma("sp", dview[:, :, c0:c0 + 512], stg[:].rearrange("p q v c -> p q (v c)"), [tSTG], [], self.dsem_s)
                    self._chk("ssmD")
                    fw.op("dve", lambda h: h.memset(STG[:], 0.0), [], [tSTG])
                    for v, (r0, gq) in enumerate(((0, 0), (0, 1), (64, 0), (64, 1))):
                        ro = 64 * gq
                        src = fr(Ct[r0:r0 + 64, gq * 128:gq * 128 + 1], [[256, NPAIR], [1, 128]])
                        self.cp("dve", STG2[ro:ro + 64, :, v, :], src, [tB[4]], [tSTG2])
                    c0 = 1280 + k * 512
                    self.dma("sp", dview[:, :, c0:c0 + 512], STG2[:].rearrange("p q v c -> p q (v c)"), [tSTG2], [], self.dsem_s)
                    self._chk("ssmE")
                    ARG, K2, SINT, COST, NSIN = [fr(b[:, 0:1], [[NCH, NPAIR], [1, NCH]]) for b in BIG]
                    ph_bc = fr(PHQ[:, k, 0:1], [[1, NPAIR], [0, NCH]])
                    ix_bc = fr(idx[:, k, 0:1], [[0, NPAIR], [1, NCH]])
                    self.tt("dve", ARG, ph_bc, ix_bc, ALU.mult, [tPH, tI], [tB[0]])
                    self.rr(ARG, ARG, K2, [tB[0]], [tB[0]], tB[1])
                    self.tsc(COST, ARG, PI / 2, None, ALU.add, None, [tB[0]], [tB[3]])
                    self.rr(COST, COST, K2, [tB[3]], [tB[3]], tB[1])
                    self.act(SINT, ARG, AF.Sin, [tB[0]], [tB[2]])
                    self.act(COST, COST, AF.Sin, [tB[3]], [tB[3]])
                    self.tsc(NSIN, SINT, -1.0, None, ALU.mult, None, [tB[2]], [tB[4]])
                    tview = dr["tab"][l, k].rearrange("q p c -> p q c")
                    s_re, s_im = (SINT, NSIN) if k == 0 else (NSIN, SINT)
                    t_re, t_im = (tB[2], tB[4]) if k == 0 else (tB[4], tB[2])
                    self.dma("sp", tview[:, :, 0:NCH], COST, [tB[3]], [], self.dsem_s)
                    self.dma("sp", tview[:, :, NCH:2 * NCH], COST, [tB[3]], [], self.dsem_s)
                    self.dma("sp", tview[:, :, 2 * NCH:3 * NCH], s_re, [t_re], [], self.dsem_s)
                    self.dma("sp", tview[:, :, 3 * NCH:4 * NCH], s_im, [t_im], [], self.dsem_s)
                dview = dr["ssmw"][l].rearrange("q p c -> p q c")
                self.dma("sp", dview[:, :, 0:256], MST[:].rearrange("p q g c -> p q (g c)"), [tMST], [], self.dsem_s)
            except _Stop:
                pass
            fw.barrier([self.dsem_s])
            fw.emit()

    def main(self):
        nc, fw, dr, nb, st = self.nc, self.fw, self.dr, self.nb, self.st
        SB = lambda n, s, d=F32: self.SB(st, n, s, d)
        self.XT = SB("XT", (128, KT, NT))
        self.XTb = Buf(self.XT, KT, NT, 128)
        self.HT = SB("HT", (128, KT, NT), BF16)
        self.HTb = Buf(self.HT, KT, NT, 128)
        self.HTflat = self.HT[:].rearrange("p a b -> p (a b)")
        self.PH = SB("PH", (128, 23040), BF16)
        self.SCf = SB("SCf", (128, 4096))
        self.SCb = SB("SCb", (128, 4096), BF16)
        self.WR = SB("WR", (128, 2, 2816), BF16)
        self.tWR = [T(), T()]
        self.dWR = [DmaSem(fw, "wr0"), DmaSem(fw, "wr1")]
        self.wn = 0
        self.ps = st.enter_context(nc.psum_tensor("ps", [128, 8, 512], F32))
        self.psT = [T(True) for _ in range(8)]
        self.psn = 0
        self.psqn = 0
        self.psxn = 0
        self.ps_banks = list(range(8))
        PH = self.PH
        self.UA = PH[:, 0:4608].rearrange("p (m n) -> p m n", m=2)
        self.UAb = Buf(self.UA, 2, NT, 128)
        self.V = PH[:, 4608:9216].rearrange("p (c f) -> p c f", c=18)
        self.tV = [T() for _ in range(18)]
        self.US = PH[:, 9216:18432].rearrange("p (m s c) -> p m s c", m=4, s=8)
        self.GT = PH[:, 9216:18432].rearrange("p (m n) -> p m n", m=4)
        self.tUS = [T() for _ in range(4)]
        self.GTb = Buf(self.GT, 4, NT, 128)
        self.PP = PH[:, 18432:23040].rearrange("p (m n) -> p m n", m=2)
        self.PPb = Buf(self.PP, 2, NT, 128)
        self.dsem_x = [DmaSem(fw, f"x{i}") for i in range(4)]
        self.dsem_o = [DmaSem(fw, f"o{i}") for i in range(4)]
        self.dsem_t = [DmaSem(fw, "t0"), DmaSem(fw, "t1")]
        self.dsem_wo = DmaSem(fw, "wo")
        self.tIO = [T() for _ in range(4)]
        fw.wait_dma("sp", [self.dsem_w, self.dsem_s, self.dsem_c, self.dsem_c2])
        for bi in range(nb):
            self.load_x(bi)
            if self.stop == "load":
                break
            for l in range(self.nlayers):
                self.layer(l, bi)
                if self.stop is not None:
                    break
            if self.stop is not None:
                break
            self.store_out(bi)
            fw.barrier()
        fw.wait_dma("sp", self.dsem_o + [self.dsem_dbg])

    def bank(self):
        b = self.ps_banks[self.psn % len(self.ps_banks)]
        self.psn += 1
        return self.ps[:, b, :], self.psT[b]

    def bank67(self):
        b = 6 + self.psqn % 2
        self.psqn += 1
        return self.ps[:, b, :], self.psT[b]

    def wslot(self):
        s = self.wn % 2
        self.wn += 1
        return s

    def load_x(self, bi):
        fw, dr = self.fw, self.dr
        self.ps_banks = list(range(8))
        XIN = [self.SCf[:, k * 1024:(k + 1) * 1024] for k in range(4)]
        tX = self.tIO
        for i in range(18):
            s = i % 4
            src = dr["ctx"][bi, i * 128:(i + 1) * 128, :] if i < 2 else dr["x"][bi, (i - 2) * 128:(i - 1) * 128, :]
            self.dma("sp", XIN[s], src, [], [tX[s]], self.dsem_x[s])
            c0 = i * 128
            for half in range(2):
                bk, bt = self.bank()
                for jj in range(4):
                    j = half * 4 + jj
                    fw.op("pe", (lambda o, a: (lambda h: h.transpose(o, a, self.ident[:])))(bk[:, jj * 128:(jj + 1) * 128], XIN[s][:, j * 128:(j + 1) * 128]),
                          [tX[s], self.tC], [bt], inc=(jj == 3))
                wr = self.XTb.ts(half * 4, half * 4 + 4, c0, c0 + 128)
                if i < 2:
                    self.cp("dve", self.XT[:, half * 4:half * 4 + 4, c0:c0 + 128], bk.rearrange("p (a b) -> p a b", a=4), [bt], wr)
                else:
                    r0 = 2 * (i - 2)
                    o4 = fr(self.XT[:, half * 4:half * 4 + 1, c0:c0 + 1], [[NT, 4], [64, 2], [1, 64]])
                    i4 = fr(bk[:, 0:1], [[128, 4], [64, 2], [1, 64]])
                    if half == 0:
                        pe = fr(self.erT[:, 0:1, r0:r0 + 1], [[32, 4], [1, 2], [0, 64]])
                    else:
                        pe = fr(self.ecT[:, 0:1, 0:1], [[64, 4], [0, 2], [1, 64]])
                    self.tt("dve", o4, i4, pe, ALU.add, [bt, self.tC], wr)
        self.dump(f"xt0_{bi}", self.XT[:], self.XTb.ts(0, KT, 0, NT))

    def store_out(self, bi):
        fw, dr = self.fw, self.dr
        self.ps_banks = list(range(8))
        OS = [self.SCf[:, k * 1024:(k + 1) * 1024] for k in range(4)]
        tO = self.tIO
        for i in range(16):
            s = i % 4
            c0 = NCTX + i * 128
            for half in range(2):
                bk, bt = self.bank()
                for jj in range(4):
                    j = half * 4 + jj
                    fw.op("pe", (lambda o, a: (lambda h: h.transpose(o, a, self.ident[:])))(bk[:, jj * 128:(jj + 1) * 128], self.XT[:, j, c0:c0 + 128]),
                          self.XTb.ts(j, j + 1, c0, c0 + 128) + [self.tC], [bt], inc=(jj == 3))
                self.cp("act" if half == 0 else "dve", OS[s][:, half * 512:(half + 1) * 512], bk, [bt], [tO[s]])
            self.dma("sp", dr["out"][bi, i * 128:(i + 1) * 128, :], OS[s], [tO[s]], [], self.dsem_o[s])

    def rstd_from_bank(self, bk, bt, w, RS, tRS):
        self.act(RS[:, 0:w], bk[:, 0:w], AF.Sqrt, [bt], [tRS], bias=EPS, scale=1.0 / D)
        self.fw.op("dve", lambda h: h.reciprocal(RS[:, 0:w], RS[:, 0:w]), [tRS], [tRS])

    def prenorm(self, l, a, b, qa, qb, bi):
        w = b - a
        pb = self.nb if a < NCTX else bi
        SQ = self.SCb[:, 0:4096].rearrange("p (j n) -> p j n", j=KT)
        RS = self.SCf[:, 0:512]
        TMP = [self.SCf[:, 512:1024], self.SCf[:, 1024:1536]]
        st = self._pn
        self.act(SQ[:, :, 0:w], self.XT[:, :, a:b], AF.Square, self.XTb.ts(0, KT, a, b), [st["sq"]])
        bk, bt = self.bank()
        for j in range(KT):
            self.mm(bk[:, 0:w], self.ones[:], SQ[:, j, 0:w], j == 0, j == KT - 1, [st["sq"], self.tC], [bt])
        self.rstd_from_bank(bk, bt, w, RS, st["rs"])
        for j in range(KT):
            r = j % 2
            self.stt(TMP[r][:, 0:w], self.XT[:, j, a:b], self.PAR[:, l, qa, j, pb:pb + 1], RS[:, 0:w], ALU.mult, ALU.mult,
                     self.XTb.ts(j, j + 1, a, b) + [st["rs"], self.tC], [st["tmp"][r]])
            self.act(self.HT[:, j, a:b], TMP[r][:, 0:w], AF.Identity, [st["tmp"][r], self.tC], self.HTb.ts(j, j + 1, a, b),
                     bias=self.PAR[:, l, qb, j, pb:pb + 1])

    def postnorm_update(self, l, a, b, q, bi, MT, tMT, slot=None):
        w = b - a
        pb = self.nb if a < NCTX else bi
        if slot is None:
            SQ = self.SCb[:, 0:4096].rearrange("p (j n) -> p j n", j=KT)
            RS = self.SCf[:, 0:512]
            TMP = [self.SCf[:, 512:1024], self.SCf[:, 1024:1536]]
            st = self._pn
        else:
            SQ = self.SCb[:, slot * 2048:(slot + 1) * 2048].rearrange("p (j n) -> p j n", j=KT)
            RS = self.SCf[:, slot * 256:(slot + 1) * 256]
            TMP = [self.SCf[:, 512 + (2 * slot + i) * 256:512 + (2 * slot + i + 1) * 256] for i in range(2)]
            st = self._pn2[slot]
        for m in range(KT):
            self.tt("pool", SQ[:, m, 0:w], MT[:, m, 0:w], MT[:, m, 0:w], ALU.mult, [tMT[m]], [st["sqm"][m]])
        bk, bt = self.bank()
        for m in range(KT):
            self.mm(bk[:, 0:w], self.ones[:], SQ[:, m, 0:w], m == 0, m == KT - 1, [st["sqm"][m], self.tC], [bt])
        self.rstd_from_bank(bk, bt, w, RS, st["rs"])
        for m in range(KT):
            r = m % 2
            self.tt("dve", TMP[r][:, 0:w], MT[:, m, 0:w], RS[:, 0:w], ALU.mult, [tMT[m], st["rs"]], [st["tmp"][r]])
            xt = self.XTb.ts(m, m + 1, a, b)
            self.stt(self.XT[:, m, a:b], TMP[r][:, 0:w], self.PAR[:, l, q, m, pb:pb + 1], self.XT[:, m, a:b], ALU.mult, ALU.add,
                     [st["tmp"][r], self.tC] + xt, xt)

    def layer(self, l, bi):
        fw = self.fw
        last = (l == 1)
        self._pn = {"sq": T(), "rs": T(), "tmp": [T(), T()], "sqm": [T() for _ in range(KT)], "vf": [T(), T()], "ln": T()}
        fw.barrier()
        self.ps_banks = list(range(8))
        lat_tiles = [(NCTX + 512 * i, NCTX + 512 * (i + 1)) for i in range(4)]
        tiles = [(0, NCTX)] + lat_tiles
        self.prenorm(l, tiles[0][0], tiles[0][1], 0, 1, bi)
        self._win_seq = [(ti, pair) for ti, (a, b) in enumerate(tiles) for pair in range(5)
                         if not (last and a < NCTX and pair not in (2, 3))]
        self._win_pos = 0
        self._win_slots = {}
        self.w_in_prefetch(l)
        for ti, (a, b) in enumerate(tiles):
            if ti + 1 < len(tiles):
                self.prenorm(l, tiles[ti + 1][0], tiles[ti + 1][1], 0, 1, bi)
            self.w_in(l, [(a, b)], last, ti)
        self.dump(f"ua_{l}_{bi}", self.UA, self.UAb.ts(0, 2, 0, NT), BF16)
        self.dump(f"v_{l}_{bi}", self.V, self.tV, BF16)
        self.dump(f"us_{l}_{bi}", self.US, self.tUS, BF16)
        self.dump(f"pp_{l}_{bi}", self.PP, self.PPb.ts(0, 2, 0, NT), BF16)
        fw.barrier()
        if self.stop == "A":
            return
        self.ps_banks = [0, 1, 2, 3]
        self.sgu(l, last)
        self.ssm(l, last)
        self.pool(l, last)
        self.glu(l, last)
        self.dump(f"a_{l}_{bi}", self.UA, self.UAb.ts(0, 2, 0, NT), BF16)
        self.dump(f"s_{l}_{bi}", self.GT, self.GTb.ts(0, 4, 0, NT) + self.tUS, BF16)
        self.dump(f"p_{l}_{bi}", self.PP, self.PPb.ts(0, 2, 0, NT), BF16)
        fw.barrier()
        if self.stop == "B":
            return
        self.ps_banks = list(range(8))
        self.w_out(l, bi, last)
        self.dump(f"xmid_{l}_{bi}", self.XT[:], self.XTb.ts(0, KT, 0, NT))
        fw.barrier()
        if self.stop == "C":
            return
        self.ffn(l, bi, last)
        self.dump(f"xend_{l}_{bi}", self.XT[:], self.XTb.ts(0, KT, 0, NT))

    def w_in_prefetch(self, l):
        if self._win_pos >= len(self._win_seq):
            return
        key = self._win_seq[self._win_pos]
        self._win_pos += 1
        s = self.wslot()
        dst = self.WR[:, s, 0:2048].rearrange("p (s k c) -> p s k c", s=2, k=KT)
        self.dma("sp", dst, self.dr["wsc_in"][l, 2 * key[1]:2 * key[1] + 2].rearrange("s p k c -> p s k c"), [], [self.tWR[s]], self.dWR[s])
        self._win_slots[key] = s

    def w_in(self, l, tiles, last, ti):
        fw, dr = self.fw, self.dr
        WRv = lambda s: self.WR[:, s, 0:2048].rearrange("p (s k c) -> p s k c", s=2, k=KT)
        VF = [self.SCf[:, 1536:1792], self.SCf[:, 1792:2048]]
        VC = self.SCf[:, 2048:2304]
        VS = self.SCf[:, 2304:2560]
        S1 = self.SCf[:, 2560:2564]
        S2 = self.SCf[:, 2564:2568]
        tVF = self._pn["vf"]
        tLN = self._pn["ln"]
        nvf = 0
        for pair in range(5):
            if (ti, pair) not in self._win_slots:
                continue
            s = self._win_slots.pop((ti, pair))
            self.w_in_prefetch(l)
            W = WRv(s)
            tW = self.tWR[s]
            for (a, b) in tiles:
                w = b - a
                isctx = a < NCTX
                hts = self.HTb.ts(0, KT, a, b)
                if pair == 1:
                    for t0 in range(a, b, 128):
                        ck = t0 // 128
                        bk, bt = self.bank()
                        for kt in range(KT):
                            rhs = fr(W[:, 0:1, kt:kt + 1, 0:1], [[KT * 128, 2], [1, 128]])
                            self.mm(bk[:, 0:256], self.HT[:, kt, t0:t0 + 128], rhs, kt == 0, kt == KT - 1, hts + [tW], [bt])
                        r = nvf % 2
                        nvf += 1
                        vf = VF[r]
                        self.act(vf, bk[:, 0:256], AF.Gelu, [bt], [tVF[r]])
                        v3 = lambda ap: ap.rearrange("p (h d) -> p h d", h=4)
                        bc = lambda ap: fr(ap[:, 0:1], [[1, 4], [0, 64]])
                        fw.op("dve", (lambda o, i: (lambda h: h.tensor_reduce(o, i, AX.X, ALU.add)))(S1, v3(vf)), [tVF[r]], [tLN])
                        self.tsc(S1, S1, -1.0 / 64, None, ALU.mult, None, [tLN], [tLN])
                        self.tt("dve", v3(VC), v3(vf), bc(S1), ALU.add, [tVF[r], tLN], [tLN])
                        self.tt("dve", VS, VC, VC, ALU.mult, [tLN], [tLN])
                        fw.op("dve", (lambda o, i: (lambda h: h.tensor_reduce(o, i, AX.X, ALU.add)))(S2, v3(VS)), [tLN], [tLN])
                        self.act(S2, S2, AF.Sqrt, [tLN], [tLN], bias=EPS, scale=1.0 / 64)
                        fw.op("dve", (lambda o: (lambda h: h.reciprocal(o, o)))(S2), [tLN], [tLN])
                        self.tt("dve", v3(self.V[:, ck, :]), v3(VC), bc(S2), ALU.mult, [tLN], [self.tV[ck]])
                    continue
                for mm_ in range(2):
                    bk, bt = self.bank()
                    for kt in range(KT):
                        self.mm(bk[:, 0:w], W[:, mm_, kt, :], self.HT[:, kt, a:b], kt == 0, kt == KT - 1, hts + [tW], [bt])
                    if pair == 0:
                        self.act(self.UA[:, mm_, a:b], bk[:, 0:w], AF.Gelu, [bt], self.UAb.ts(mm_, mm_ + 1, a, b))
                    elif pair in (2, 3):
                        m = (pair - 2) * 2 + mm_
                        c0, nc_ = a // 8, w // 8
                        o3 = fr(self.US[:, m:m + 1, 0:1, c0:c0 + 1], [[NCH, 8], [1, nc_]])
                        i3 = fr(bk[:, 0:1], [[1, 8], [8, nc_]])
                        self.cp("act", o3, i3, [bt], [self.tUS[m]] + self.GTb.ts(m, m + 1, 0, NT))
                    else:
                        self.cp("dve", self.PP[:, mm_, a:b], bk[:, 0:w], [bt], self.PPb.ts(mm_, mm_ + 1, a, b))

    def sgu(self, l, last):
        for ck in range(2 if last else 0, 18):
            c0 = ck * 128
            for hp in range(2):
                qs = []
                bk, qt = self.bank67()
                for hh in range(2):
                    hd = 2 * hp + hh
                    qa = bk[:, hh * 128:(hh + 1) * 128]
                    self.mm(qa, self.V[:, ck, hp * 128:(hp + 1) * 128], self.sguTb[:, l, hd, :], True, False, [self.tV[ck], self.tC], [qt], inc=False)
                    o = (l * 4 + hd) * 128
                    self.mm(qa, self.ones[0:1, 0:128], self.sgubb[0:1, o:o + 128], False, True, [self.tC], [qt], inc=(hh == 1))
                    qs.append((qa, qt))
                for hh in range(2):
                    r0 = 64 * hh
                    ts_ = self.UAb.ts(hp, hp + 1, c0, c0 + 128)
                    self.tt("dve", self.UA[r0:r0 + 64, hp, c0:c0 + 128], qs[hh][0][r0:r0 + 64, :], self.UA[r0:r0 + 64, hp, c0:c0 + 128], ALU.mult,
                            [qs[hh][1]] + ts_, ts_)

    def pool(self, l, last):
        QT = [self.SCb[:, i * 128:(i + 1) * 128] for i in range(8)]
        tQ = [T() for _ in range(8)]

        def first(ti, m, slot):
            c0 = ti * 128
            bk, qt = self.bank67()
            qa = bk[:, 0:128]
            self.mm(qa, self.PP[:, m, c0:c0 + 128], self.poolwb[:, l, m, :], True, True, self.PPb.ts(m, m + 1, c0, c0 + 128) + [self.tC], [qt])
            self.cp("act", QT[slot], qa, [qt], [tQ[slot]])

        def finish(ti, m, rhs_list):
            c0 = ti * 128
            bk, qt = self.bank67()
            for gq in range(2):
                qa = bk[:, gq * 128:(gq + 1) * 128]
                n = len(rhs_list[gq])
                for i, (slot, rhs) in enumerate(rhs_list[gq]):
                    self.mm(qa, QT[slot], rhs, i == 0, i == n - 1, [tQ[slot], self.tC], [qt], inc=(gq == 1 and i == n - 1))
            for gq in range(2):
                qa = bk[:, gq * 128:(gq + 1) * 128]
                r0 = 64 * gq
                self.tsc(self.PP[r0:r0 + 64, m, c0:c0 + 128], qa[r0:r0 + 64, :], self.pscale[r0:r0 + 64, l, m:m + 1], None, ALU.mult, None,
                         [qt, self.tC], self.PPb.ts(m, m + 1, c0, c0 + 128))

        if not last:
            for m in range(2):
                for ti in range(2):
                    first(ti, m, 4 + 2 * m + ti)
            for m in range(2):
                for tt_ in range(2):
                    finish(tt_, m, [[(4 + 2 * m + st_, self.avgCb[:, 2 * m + gq, st_, tt_, :]) for st_ in range(2)] for gq in range(2)])
        n = 0
        for ti in range(2, 18):
            for m in range(2):
                slot = n % 4
                n += 1
                first(ti, m, slot)
                finish(ti, m, [[(slot, self.avgLb[:, 2 * m + gq, :])] for gq in range(2)])

    def glu(self, l, last):
        dr = self.dr
        s = self.wslot()
        GW = self.WR[:, s, 0:2048].rearrange("p (k c) -> p k c", k=4)
        self.dma("sp", GW, dr["wsc_glu"][l], [], [self.tWR[s]], self.dWR[s])
        SG = [self.SCb[:, 1024:1536], self.SCb[:, 1536:2048], self.SCb[:, 2048:2560], self.SCb[:, 2560:3072]]
        tSG = [T() for _ in range(4)]
        tiles = [(NCTX + 512 * i, NCTX + 512 * (i + 1)) for i in range(4)]
        if not last:
            tiles = [(0, NCTX)] + tiles
        for (a, b) in tiles:
            w = b - a
            gts = self.GTb.ts(0, 4, a, b)
            bks = []
            for m in range(4):
                bk, bt = self.bank()
                for kt in range(4):
                    self.mm(bk[:, 0:w], GW[:, kt, m * 128:(m + 1) * 128], self.GT[:, kt, a:b], kt == 0, kt == 3, gts + [self.tWR[s]], [bt])
                bks.append((bk, bt))
            for m in range(4):
                bk, bt = bks[m]
                self.act(SG[m][:, 0:w], bk[:, 0:w], AF.Sigmoid, [bt, self.tC], [tSG[m]], bias=self.glub[:, l, m:m + 1])
            for m in range(4):
                ts_ = self.GTb.ts(m, m + 1, a, b)
                self.tt("pool", self.GT[:, m, a:b], self.GT[:, m, a:b], SG[m][:, 0:w], ALU.mult, [tSG[m]] + ts_, ts_)

    def ssm(self, l, last):
        fw, dr = self.fw, self.dr
        HF = self.HTflat
        UTs = [HF[:, r * 2304:(r + 1) * 2304].rearrange("p (g c) -> p g c", g=8) for r in range(2)]
        GYs = [HF[:, 4608 + r * 2304:4608 + (r + 1) * 2304].rearrange("p (g c) -> p g c", g=8) for r in range(2)]
        Hbs = [HF[:, 9216 + i * 576:9216 + (i + 1) * 576].rearrange("p (r c) -> p r c", r=2) for i in range(4)]
        tUTs = [[T() for _ in range(8)] for _ in range(2)]
        tGYs = [[T() for _ in range(8)] for _ in range(2)]
        tHs = [T() for _ in range(4)]
        TB = [self.SCf[:, i * 1152:(i + 1) * 1152].rearrange("p (t r c) -> p t r c", t=2, r=2) for i in range(2)]
        tTB = [T(), T()]
        F2 = HF[:, 11520:14976].bitcast(F32)
        sets = [[self.SCf[:, 2304 + i * 576:2304 + (i + 1) * 576].rearrange("p (r c) -> p r c", r=2) for i in range(3)],
                [F2[:, i * 576:(i + 1) * 576].rearrange("p (r c) -> p r c", r=2) for i in range(3)]]
        tsets = [[T(), T(), T()], [T(), T(), T()]]
        nunit = 0

        def swapped(ap3, stride):
            return fr(ap3[:, 1:2, 0:1], [[-stride, 2], [1, NCH]])

        def sel_in(j):
            UT, tUT = UTs[j % 2], tUTs[j % 2]
            for gp in range(8):
                bk, bt = self.bank()
                for s in range(8):
                    self.mm(bk[:, 0:NCH], self.dselb[:, gp, 112 - 16 * s:240 - 16 * s], self.US[:, j, s, :], s == 0, s == 7,
                            [self.tUS[j], self.tC], [bt])
                self.cp("act", UT[:, gp, :], bk[:, 0:NCH], [bt], [tUT[gp]])

        def front(j, qq):
            nonlocal nunit
            UT, tUT = UTs[j % 2], tUTs[j % 2]
            q = j * 4 + qq
            s = self.wslot()
            SW = self.WR[:, s, 0:SSMW_COLS]
            tSW = self.tWR[s]
            self.dma("sp", SW, dr["ssmw"][l, q], [], [tSW], self.dWR[s])
            g0, g1 = 2 * qq, 2 * qq + 1
            Hk = []
            for k in range(2):
                u = nunit
                nunit += 1
                r = u % 2
                self.dma("sp", TB[r], dr["tab"][l, k, q].rearrange("p (t r c) -> p t r c", t=2, r=2), [], [tTB[r]], self.dsem_t[r])
                COS2, SIN2 = TB[r][:, 0], TB[r][:, 1]
                XP, XQ, G = sets[r]
                tXP, tXQ, tG = tsets[r]
                Hb, tH = Hbs[u % 4], tHs[u % 4]
                Hk.append((Hb, tH))
                xb = 4 + 2 * r
                X = self.ps[:, xb:xb + 2, 0:NCH]
                tX, tX2 = self.psT[xb], self.psT[xb + 1]
                zb = lambda v: SW[:, 256 + k * 512 + v * 128:256 + k * 512 + (v + 1) * 128]
                for ri in range(2):
                    self.mm(self.ps[:, xb + ri, 0:NCH], zb(2 * ri), UT[:, g0, :], True, False, [tSW, tUT[g0]], [tX, tX2], inc=False)
                    self.mm(self.ps[:, xb + ri, 0:NCH], zb(2 * ri + 1), UT[:, g1, :], False, True, [tSW, tUT[g1]], [tX, tX2], inc=(ri == 1))
                self.tt("dve", XP, X, COS2, ALU.mult, [tX, tX2, tTB[r]], [tXP])
                self.tt("dve", XQ, swapped(X, 512), SIN2, ALU.mult, [tX, tX2, tTB[r]], [tXQ])
                self.tt("dve", XP, XP, XQ, ALU.add, [tXP, tXQ], [tXP])
                rho = fr(self.RHO[:, l, k, q:q + 1], [[0, NCH]])
                for ri in range(2):
                    if k == 0:
                        self.scan(G[:, ri, :], rho, XP[:, ri, :], 0.0, [tXP, self.tC], [tG])
                    else:
                        rho32 = fr(self.RHO[:, l, k, q:q + 1], [[0, 32]])
                        rho256 = fr(self.RHO[:, l, k, q:q + 1], [[0, 256]])
                        self.scan(G[:, ri, 31::-1], rho32, XP[:, ri, 31::-1], 0.0, [tXP, self.tC], [tG])
                        self.scan(G[:, ri, 287:31:-1], rho256, XP[:, ri, 287:31:-1], G[:, ri, 0:1], [tXP, self.tC, tG], [tG])
                self.tt("pool", XP, G, COS2, ALU.mult, [tG, tTB[r]], [tXP])
                self.tt("pool", XQ, swapped(G, NCH), SIN2, ALU.mult, [tG, tTB[r]], [tXQ])
                self.tt("pool", Hb, XP, XQ, ALU.subtract, [tXP, tXQ], [tH])
            return (SW, tSW, Hk)

        def back(j, qq, ctxt):
            SW, tSW, Hk = ctxt
            UT, tUT = UTs[j % 2], tUTs[j % 2]
            GY, tGY = GYs[j % 2], tGYs[j % 2]
            for gq in range(2):
                g = 2 * qq + gq
                bk, bt = self.bank()
                cpv = lambda k, v: SW[:, 1280 + k * 512 + v * 128:1280 + k * 512 + (v + 1) * 128]
                self.mm(bk[:, 0:NCH], SW[:, gq * 128:(gq + 1) * 128], UT[:, g, :], True, False, [tSW, tUT[g]], [bt], inc=False)
                for ri in range(2):
                    self.mm(bk[:, 1:NCH], cpv(0, 2 * ri + gq), Hk[0][0][:, ri, 0:NCH - 1], False, False, [tSW, Hk[0][1]], [bt], inc=False)
                segs = ((32, 287, 33), (287, 288, 0), (0, 31, 1))
                for si, (o0, o1, i0_) in enumerate(segs):
                    for ri in range(2):
                        lastmm = (si == 2 and ri == 1)
                        self.mm(bk[:, o0:o1], cpv(1, 2 * ri + gq), Hk[1][0][:, ri, i0_:i0_ + (o1 - o0)], False, lastmm, [tSW, Hk[1][1]], [bt], inc=lastmm)
                self.act(GY[:, g, :], bk[:, 0:NCH], AF.Gelu, [bt], [tGY[g]])

        def sel_out(j):
            GY, tGY = GYs[j % 2], tGYs[j % 2]
            for t in range(8):
                bk, bt = self.bank()
                for gp in range(8):
                    self.mm(bk[:, 0:NCH], self.dselb[:, t, 112 - 16 * gp:240 - 16 * gp], GY[:, gp, :], gp == 0, gp == 7, [tGY[gp], self.tC], [bt])
                self.cp("dve", self.GT[:, j, t:NT:8], bk[:, 0:NCH], [bt], [self.tUS[j]] + self.GTb.ts(j, j + 1, 0, NT))

        jobs = [(j, qq) for j in range(4) for qq in range(4)]
        sel_in(0)
        sel_in(1)
        ctx_next = front(*jobs[0])
        for i, (j, qq) in enumerate(jobs):
            ctx_cur = ctx_next
            if i + 1 < len(jobs):
                ctx_next = front(*jobs[i + 1])
            back(j, qq, ctx_cur)
            if qq == 3:
                sel_out(j)
                if j + 2 < 4:
                    sel_in(j + 2)

    def w_out(self, l, bi, last):
        dr = self.dr
        HF = self.HTflat
        MTs = [HF[:, r * 4096:(r + 1) * 4096].bitcast(F32).rearrange("p (m n) -> p m n", m=KT) for r in range(2)]
        WO = HF[:, 8192:16384].rearrange("p (m k c) -> p m k c", m=8, k=KT)
        tWO = T()
        self.dma("sp", WO, dr["wsc_out"][l].rearrange("m p k c -> p m k c"), [], [tWO], self.dsem_wo)
        tMTs = [[T() for _ in range(KT)] for _ in range(2)]
        self._pn2 = [{"sq": T(), "rs": T(), "tmp": [T(), T()], "sqm": [T() for _ in range(KT)]} for _ in range(2)]
        tiles = [(NCTX + 256 * i, NCTX + 256 * (i + 1)) for i in range(8)]
        if not last:
            tiles = [(0, NCTX)] + tiles
        for ti, (a, b) in enumerate(tiles):
            w = b - a
            MT, tMT = MTs[ti % 2], tMTs[ti % 2]
            for m in range(KT):
                bk, bt = self.bank()
                for kt in range(KT):
                    if kt < 2:
                        rhs, rt = self.UA[:, kt, a:b], self.UAb.ts(kt, kt + 1, a, b)
                    elif kt < 6:
                        rhs, rt = self.GT[:, kt - 2, a:b], self.GTb.ts(kt - 2, kt - 1, a, b)
                    else:
                        rhs, rt = self.PP[:, kt - 6, a:b], self.PPb.ts(kt - 6, kt - 5, a, b)
                    self.mm(bk[:, 0:w], WO[:, m, kt, :], rhs, kt == 0, kt == KT - 1, rt + [tWO], [bt])
                self.cp("act", MT[:, m, 0:w], bk[:, 0:w], [bt], [tMT[m]])
            self.postnorm_update(l, a, b, 2, bi, MT, tMT, slot=ti % 2)

    def ffn(self, l, bi, last):
        fw, dr = self.fw, self.dr
        if last:
            groups = [[(256, 768), (768, 1280)], [(1280, 1792), (1792, 2304)]]
        else:
            groups = [[(0, 256), (256, 768)], [(768, 1280), (1280, 1536)], [(1536, 2048), (2048, 2304)]]
        HF = self.HTflat
        MT = HF[:, 0:8192].bitcast(F32).rearrange("p (m n) -> p m n", m=KT)
        tMT = [T() for _ in range(KT)]
        SG = [self.SCb[:, 0:512], self.SCb[:, 512:1024], self.SCb[:, 1024:1536], self.SCb[:, 1536:2048]]
        tSG = [T() for _ in range(4)]
        nsg = 0
        tA = [[T() for _ in range(2)] for _ in range(JT)]
        self._pn = {"sq": T(), "rs": T(), "tmp": [T(), T()], "sqm": [T() for _ in range(KT)], "vf": [T(), T()], "ln": T()}
        for grp in groups:
            g0 = grp[0][0]
            glen = grp[-1][1] - g0
            A = self.PH[:, 0:JT * glen].rearrange("p (j n) -> p j n", j=JT)
            for (a, b) in grp:
                self.prenorm(l, a, b, 3, 4, bi)
            for j in range(JT):
                s = self.wslot()
                W = self.WR[:, s, 0:2048].rearrange("p (s k c) -> p s k c", s=2, k=KT)
                self.dma("sp", W[:, 0], dr["wsc_g"][l, j], [], [self.tWR[s]], self.dWR[s])
                self.dma("sp", W[:, 1], dr["wsc_u"][l, j], [], [self.tWR[s]], self.dWR[s])
                for ci, (a, b) in enumerate(grp):
                    w = b - a
                    hts = self.HTb.ts(0, KT, a, b)
                    bg, tg = self.bank()
                    for kt in range(KT):
                        self.mm(bg[:, 0:w], W[:, 0, kt, :], self.HT[:, kt, a:b], kt == 0, kt == KT - 1, hts + [self.tWR[s]], [tg])
                    bu, tu = self.bank()
                    for kt in range(KT):
                        self.mm(bu[:, 0:w], W[:, 1, kt, :], self.HT[:, kt, a:b], kt == 0, kt == KT - 1, hts + [self.tWR[s]], [tu])
                    r = nsg % 4
                    nsg += 1
                    self.act(SG[r][:, 0:w], bg[:, 0:w], AF.Silu, [tg], [tSG[r]])
                    self.tt("dve", A[:, j, a - g0:b - g0], bu[:, 0:w], SG[r][:, 0:w], ALU.mult, [tu, tSG[r]], [tA[j][ci]])
            for ci, (a, b) in enumerate(grp):
                w = b - a
                for m in range(KT):
                    s = self.wslot()
                    WD = self.WR[:, s, 0:JT * 128].rearrange("p (j c) -> p j c", j=JT)
                    self.dma("sp", WD, dr["wsc_d"][l, m], [], [self.tWR[s]], self.dWR[s])
                    bk, bt = self.bank()
                    for j in range(JT):
                        self.mm(bk[:, 0:w], WD[:, j, :], A[:, j, a - g0:b - g0], j == 0, j == JT - 1, [tA[j][ci], self.tWR[s]], [bt])
                    self.cp("act", MT[:, m, 0:w], bk[:, 0:w], [bt], [tMT[m]])
                self.postnorm_update(l, a, b, 5, bi, MT, tMT)
        fw.barrier()


_CACHE = {}


def _get_nc(nb, dbg=(), nlayers=2, skip=(), stop=None):
    key = (nb, tuple(sorted(dbg)), nlayers, tuple(sorted(skip)), stop)
    if key not in _CACHE:
        g = Gen(nb, dbg, nlayers, skip, stop)
        g.build()
        _CACHE[key] = g
    return _CACHE[key]


def run_cores(inp, batch_lists, dbg=(), nlayers=2, trace=False, skip=(), stop=None):
    nb = len(batch_lists[0])
    g = _get_nc(nb, dbg, nlayers, skip, stop)
    sh = _prep_shared(inp)
    x = np.asarray(inp["x"], np.float32)
    ctx = np.asarray(inp["ctx"], np.float32)
    c = np.asarray(inp["c"], np.float32)
    c_ctx = np.asarray(inp["c_ctx"], np.float32)
    in_maps = []
    for bl in batch_lists:
        m = dict(sh)
        m["x"] = np.ascontiguousarray(x[bl])
        m["ctx"] = np.ascontiguousarray(ctx[bl])
        cc = np.concatenate([c[bl], c_ctx[None]], axis=0)
        m["cT"] = np.ascontiguousarray(cc.reshape(nb + 1, KT, 128).transpose(2, 1, 0))
        in_maps.append(m)
    res = run_bass_kernel_spmd(g.nc, in_maps, core_ids=list(range(len(batch_lists))), trace=trace)
    return res


def kernel(**inputs):
    nb = np.asarray(inputs["x"]).shape[0] // NCORES
    batch_lists = [list(range(c * nb, (c + 1) * nb)) for c in range(NCORES)]
    res = run_cores(inputs, batch_lists)
    out = np.concatenate([np.asarray(r["out"], np.float32) for r in res.results], axis=0)
    return out
```

```python
import math
from contextlib import ExitStack

import numpy as np
import ml_dtypes

import concourse.bass as bass
import concourse.mybir as mybir
from concourse.bass_utils import run_bass_kernel_spmd

F32 = mybir.dt.float32
BF16 = mybir.dt.bfloat16
AF = mybir.ActivationFunctionType
ALU = mybir.AluOpType
AX = mybir.AxisListType

D = 1024
KT = 8
DIN = 1280
DFF = 2816
JT = 22
NCTX = 256
NLAT = 2048
NT = 2304
NCH = 288
NG = 32
NPAIR = 16
EPS = 1e-6
NCORES = 8
TWO_PI = 6.283185307179586
CW1 = 6.28125
CW2 = TWO_PI - 6.28125
MAGIC = 12582912.0
SSMW_COLS = 2304
TAB_COLS = 1152

SEM_CAP = 20000


class T:
    __slots__ = ("w", "r", "x")

    def __init__(self, x=False):
        self.w = None
        self.r = []
        self.x = x


class DmaSem:
    def __init__(self, fw, name):
        self.sem = fw.new_sem(name)
        self.count = 0


class FW:
    ENG = ("pe", "act", "dve", "pool", "sp")

    def __init__(self, nc, stack):
        self.nc = nc
        self.stack = stack
        self.lists = {e: [] for e in self.ENG}
        self.cnt = {e: 0 for e in self.ENG}
        self.sems = {e: [] for e in self.ENG}
        self.known = {e: {} for e in self.ENG}
        self.epoch_known = {e: {} for e in self.ENG}
        self.nops = 0

    def new_sem(self, name):
        return self.stack.enter_context(self.nc.semaphore(name))

    def _eng_sem(self, e, n):
        idx = (n - 1) // SEM_CAP
        while len(self.sems[e]) <= idx:
            self.sems[e].append(self.new_sem(f"s_{e}{len(self.sems[e])}"))
        return ("c", e, idx), self.sems[e][idx], (n - 1) % SEM_CAP + 1

    def _need(self, eng, dep, waits):
        if dep is None:
            return
        if dep[0] == "c":
            _, e, n = dep
            if e == eng and n > self.cnt[e]:
                return
            key, sem, val = self._eng_sem(e, n)
            if self.epoch_known[eng].get(e, -1) > key[2]:
                return
        else:
            _, ds, val = dep
            key, sem = ("d", id(ds)), ds.sem
        if self.known[eng].get(key, 0) >= val:
            return
        cur = waits.get(key)
        if cur is None or cur[1] < val:
            waits[key] = (sem, val)

    def _commit(self, eng, waits):
        for key, (sem, val) in waits.items():
            self.known[eng][key] = val
            if key[0] == "c":
                if self.epoch_known[eng].get(key[1], -1) < key[2]:
                    self.epoch_known[eng][key[1]] = key[2]

    def op(self, eng, fn, reads=(), writes=(), inc=True, dma=None):
        ex = [t for t in reads if t.x]
        if ex:
            reads = [t for t in reads if not t.x]
            writes = list(writes) + ex
        waits = {}
        for t in reads:
            self._need(eng, t.w, waits)
        for t in writes:
            self._need(eng, t.w, waits)
            for r in t.r:
                self._need(eng, r, waits)
        self._commit(eng, waits)
        if dma is not None:
            dma.count += 16
            ticket = ("d", dma, dma.count)
            incinfo = (dma.sem, 16)
        elif inc:
            self.cnt[eng] += 1
            n = self.cnt[eng]
            _, sem, _ = self._eng_sem(eng, n)
            ticket = ("c", eng, n)
            incinfo = (sem, 1)
        else:
            ticket = ("c", eng, self.cnt[eng] + 1)
            incinfo = None
        self.lists[eng].append((list(waits.values()), fn, incinfo))
        self.nops += 1
        for t in reads:
            t.r.append(ticket)
        for t in writes:
            t.w = ticket
            t.r = []
        return ticket

    def barrier(self, dsems=()):
        for e in self.ENG:
            waits = {}
            for e2 in self.ENG:
                if e2 != e and self.cnt[e2] > 0:
                    self._need(e, ("c", e2, self.cnt[e2]), waits)
            for ds in dsems:
                if ds.count:
                    self._need(e, ("d", ds, ds.count), waits)
            self._commit(e, waits)
            if waits:
                self.lists[e].append((list(waits.values()), None, None))

    def wait_dma(self, eng, dsems):
        waits = {}
        for ds in dsems:
            if ds.count:
                self._need(eng, ("d", ds, ds.count), waits)
        self._commit(eng, waits)
        if waits:
            self.lists[eng].append((list(waits.values()), None, None))

    def emit(self):
        nc = self.nc
        lists = self.lists
        self.lists = {e: [] for e in self.ENG}

        def run(h, lst):
            for waits, fn, incinfo in lst:
                for sem, val in waits:
                    h.wait_ge(sem, val)
                if fn is None:
                    continue
                ins = fn(h)
                if incinfo is not None:
                    ins.then_inc(incinfo[0], incinfo[1])

        with nc.Block() as block:
            @block.tensor
            def _(h):
                run(h, lists["pe"])

            @block.scalar
            def _(h):
                run(h, lists["act"])

            @block.vector
            def _(h):
                run(h, lists["dve"])

            @block.gpsimd
            def _(h):
                run(h, lists["pool"])

            @block.sync
            def _(h):
                run(h, lists["sp"])


def fr(ap, dims):
    return bass.AP(ap.tensor, ap.offset, [list(ap.ap[0])] + [list(d) for d in dims])


class Buf:
    def __init__(self, t, J, N, blk):
        self.t, self.J, self.N, self.blk = t, J, N, blk
        self.T = [[T() for _ in range((N + blk - 1) // blk)] for _ in range(J)]

    def ts(self, j0, j1, a, b):
        return [self.T[j][k] for j in range(j0, j1) for k in range(a // self.blk, (b - 1) // self.blk + 1)]


def _pe_tables():
    quarter = D // 4
    omega = (1.0 / (10000.0 ** (np.arange(quarter, dtype=np.float32) / np.float32(quarter)))).astype(np.float32)
    r = np.arange(NLAT // 64, dtype=np.float32)[:, None] * omega
    c = np.arange(64, dtype=np.float32)[:, None] * omega
    er = np.concatenate([np.sin(r), np.cos(r)], axis=-1).astype(np.float32)
    ec = np.concatenate([np.sin(c), np.cos(c)], axis=-1).astype(np.float32)
    erT = np.ascontiguousarray(er.T.reshape(4, 128, 32).transpose(1, 0, 2))
    ecT = np.ascontiguousarray(ec.T.reshape(4, 128, 64).transpose(1, 0, 2))
    return erT, ecT


def _avg_mats():
    wins = (2, 4, 8, 16)

    def amat(n, w):
        t = np.arange(n)
        lo = np.clip(t - w // 2, 0, n)
        hi = np.clip(t - w // 2 + w, 0, n)
        A = np.zeros((n, n), np.float32)
        for i in range(n):
            A[i, lo[i]:hi[i]] = 1.0 / float(hi[i] - lo[i])
        return A - np.eye(n, dtype=np.float32)

    avgL = np.zeros((128, 4, 128), np.float32)
    avgC = np.zeros((128, 4, 2, 2, 128), np.float32)
    for i, w in enumerate(wins):
        a64 = amat(64, w).T
        avgL[0:64, i, 0:64] = a64
        avgL[64:128, i, 64:128] = a64
        a256 = amat(256, w).T
        for st in range(2):
            for tt in range(2):
                avgC[:, i, st, tt, :] = a256[st * 128:(st + 1) * 128, tt * 128:(tt + 1) * 128]
    return avgL, avgC


def _consts():
    c = {}
    c["ident"] = np.eye(128, dtype=np.float32)
    s_i = np.arange(128) // 16
    mf = (s_i[None, :] >= s_i[:, None]).astype(np.float32)
    mb = (s_i[:, None] >= s_i[None, :]).astype(np.float32)
    c["masks"] = np.ascontiguousarray(np.stack([mf, mb], axis=1))
    c["expv"] = np.ascontiguousarray(np.broadcast_to(np.arange(-8, 9, dtype=np.float32), (128, 17)))
    sg = np.ones((128, 2), np.float32)
    sg[0:64, 0] = -1.0
    sg[64:128, 1] = -1.0
    c["sgn"] = sg
    idx = np.zeros((128, 2, NCH), np.float32)
    idx[:, 0, :] = np.arange(NCH, dtype=np.float32)
    idx[:, 1, 0:32] = 256.0 + np.arange(32, dtype=np.float32)
    idx[:, 1, 32:] = np.arange(256, dtype=np.float32)
    c["idx"] = idx
    dsel = np.zeros((128, 8, 240), np.float32)
    for a in range(8):
        for i in range(16):
            dsel[16 * a + i, a, 112 + i] = 1.0
    c["dsel"] = dsel
    c["erT"], c["ecT"] = _pe_tables()
    c["avgL"], c["avgC"] = _avg_mats()
    return c


def _chunkT(v):
    v = np.asarray(v, np.float32)
    lead = v.shape[:-1]
    n = v.shape[-1] // 128
    v = v.reshape(lead + (n, 128))
    v = np.moveaxis(v, -1, 0)
    return np.ascontiguousarray(v)


def _prep_shared(inp):
    f = lambda k: np.asarray(inp[k], np.float32)
    sh = dict(_consts())
    for k in ("w_mod", "w_in", "w_out", "glu_w", "ffn_w_gate", "ffn_w_up", "ffn_w_down"):
        sh[k] = np.ascontiguousarray(f(k))
    sh["bmodT"] = _chunkT(f("b_mod"))
    sh["gains"] = np.ascontiguousarray(np.stack([_chunkT(f(k)) for k in
                                                 ("norm_mix_pre", "norm_mix_post", "norm_ffn_pre", "norm_ffn_post")], axis=2))
    sh["glubT"] = _chunkT(f("glu_b"))
    sh["pscaleT"] = _chunkT(f("pool_scale"))
    sh["sguT"] = np.ascontiguousarray(f("sgu_w").transpose(3, 0, 1, 2))
    sh["sgub"] = np.ascontiguousarray(f("sgu_b").reshape(1, 2 * 4 * 128))
    pw = f("pool_w")
    poolw = np.zeros((128, 2, 2, 128), np.float32)
    for l in range(2):
        for m in range(2):
            for q in range(2):
                poolw[q * 64:(q + 1) * 64, l, m, q * 64:(q + 1) * 64] = pw[l, 2 * m + q]
    sh["poolw"] = poolw
    def pp(a):
        a = a.transpose(3, 0, 1, 2).reshape(64, 2, 64)
        return np.ascontiguousarray(np.concatenate([a, a], axis=0))
    sh["lamre"] = pp(f("ssm_lam_re"))
    sh["lamim"] = pp(f("ssm_lam_im"))
    sh["logdt"] = np.ascontiguousarray(np.broadcast_to(f("ssm_log_dt").reshape(1, 2, 64), (128, 2, 64)))
    bre = f("ssm_b_re").transpose(3, 0, 1, 2, 4).reshape(64, 2, 64, 16)
    bim = f("ssm_b_im").transpose(3, 0, 1, 2, 4).reshape(64, 2, 64, 16)
    sh["b_x"] = np.ascontiguousarray(np.concatenate([bre, bim], axis=0))
    sh["b_y"] = np.ascontiguousarray(np.concatenate([bim, bre], axis=0))
    cre = f("ssm_c_re").transpose(4, 0, 1, 2, 3).reshape(64, 2, 64, 16)
    cim = f("ssm_c_im").transpose(4, 0, 1, 2, 3).reshape(64, 2, 64, 16)
    sh["c_x"] = np.ascontiguousarray(np.concatenate([cre, cim], axis=0))
    sh["c_y"] = np.ascontiguousarray(np.concatenate([cim, cre], axis=0))
    d = f("ssm_d")
    sh["d_rep"] = np.ascontiguousarray(np.broadcast_to(d.transpose(2, 0, 1)[None], (8, 16, 2, 32)).reshape(128, 2, 32))
    return sh


class _Stop(Exception):
    pass


class Gen:
    def _chk(self, name):
        if self.stop == name:
            raise _Stop()

    def __init__(self, nb, dbg=(), nlayers=2, skip=(), stop=None):
        self.skip = set(skip)
        self.stop = stop
        self.nb = nb
        self.dbg = set(dbg)
        self.nlayers = nlayers
        self.nc = bass.Bass("TRN2", target_bir_lowering=False)
        self.dr = {}
        self.dbg_outs = {}

    def din(self, name, shape, dt=F32):
        self.dr[name] = self.nc.dram_tensor(name, list(shape), dt, kind="ExternalInput").ap()

    def dscr(self, name, shape, dt):
        self.dr[name] = self.nc.dram_tensor(name, list(shape), dt, kind="Internal").ap()

    def act(self, out, in_, func, reads, writes, bias=None, scale=None):
        kw = {}
        if bias is not None:
            kw["bias"] = bias
        if scale is not None:
            kw["scale"] = scale
        return self.fw.op("act", lambda h: h.activation(out, in_, func, **kw), reads, writes)

    def tt(self, eng, out, a, b, op, reads, writes):
        return self.fw.op(eng, lambda h: h.tensor_tensor(out, a, b, op), reads, writes)

    def tsc(self, out, in_, s1, s2, op0, op1, reads, writes, eng="dve"):
        if s2 is None:
            return self.fw.op(eng, lambda h: h.tensor_scalar(out, in_, s1, None, op0), reads, writes)
        return self.fw.op(eng, lambda h: h.tensor_scalar(out, in_, s1, s2, op0, op1), reads, writes)

    def stt(self, out, in0, scalar, in1, op0, op1, reads, writes, eng="dve"):
        return self.fw.op(eng, lambda h: h.scalar_tensor_tensor(out, in0, scalar, in1, op0, op1), reads, writes)

    def cp(self, eng, out, in_, reads, writes):
        if eng == "act":
            return self.fw.op("act", lambda h: h.activation(out, in_, AF.Copy), reads, writes)
        return self.fw.op(eng, lambda h: h.tensor_copy(out, in_), reads, writes)

    def dma(self, q, out, in_, reads, writes, sem):
        return self.fw.op(q, lambda h: h.dma_start(out=out, in_=in_), reads, writes, dma=sem)

    def mm(self, out, lhsT, rhs, start, stop, reads, writes, inc=None):
        if inc is None:
            inc = stop
        return self.fw.op("pe", lambda h: h.matmul(out, lhsT, rhs, start=start, stop=stop), reads, writes, inc=inc)

    def tr(self, out, in_, ident, reads, writes):
        return self.fw.op("pe", lambda h: h.transpose(out, in_, ident), reads, writes)

    def scan(self, out, d0, d1, init, reads, writes, eng="dve"):
        return self.fw.op(eng, lambda h: h.tensor_tensor_scan(out, d0, d1, init, ALU.mult, ALU.add), reads, writes)

    def rr(self, out, in_, kk, reads, writes, tk):
        self.tsc(kk, in_, 1.0 / TWO_PI, MAGIC, ALU.mult, ALU.add, reads, [tk])
        self.tsc(kk, kk, MAGIC, None, ALU.subtract, None, [tk], [tk])
        self.stt(out, kk, -CW1, in_, ALU.mult, ALU.add, [tk] + list(reads), writes)
        self.stt(out, kk, -CW2, out, ALU.mult, ALU.add, [tk] + list(writes), writes)
        self.tsc(out, out, -3.1415925, 3.1415925, ALU.max, ALU.min, list(writes), writes)

    def dump(self, name, ap, reads, dt=F32):
        if name not in self.dbg:
            return
        o = self.nc.dram_tensor("dbg_" + name, list(ap.shape), dt, kind="ExternalOutput").ap()
        self.dbg_outs[name] = o
        self.dma("sp", o, ap, reads, [], self.dsem_dbg)

    def build(self):
        nc, nb = self.nc, self.nb
        din = self.din
        for name, shape in (("ident", (128, 128)), ("masks", (128, 2, 128)), ("expv", (128, 17)), ("sgn", (128, 2)),
                            ("idx", (128, 2, NCH)), ("dsel", (128, 8, 240)), ("erT", (128, 4, 32)), ("ecT", (128, 4, 64)),
                            ("avgL", (128, 4, 128)), ("avgC", (128, 4, 2, 2, 128)),
                            ("w_mod", (2, D, 6 * D)), ("w_in", (2, D, DIN)), ("w_out", (2, D, D)), ("glu_w", (2, 512, 512)),
                            ("ffn_w_gate", (2, D, DFF)), ("ffn_w_up", (2, D, DFF)), ("ffn_w_down", (2, DFF, D)),
                            ("bmodT", (128, 2, 48)), ("gains", (128, 2, 4, 8)), ("glubT", (128, 2, 4)), ("pscaleT", (128, 2, 2)),
                            ("sguT", (128, 2, 4, 128)), ("sgub", (1, 1024)), ("poolw", (128, 2, 2, 128)),
                            ("lamre", (128, 2, 64)), ("lamim", (128, 2, 64)), ("logdt", (128, 2, 64)),
                            ("b_x", (128, 2, 64, 16)), ("b_y", (128, 2, 64, 16)), ("c_x", (128, 2, 64, 16)), ("c_y", (128, 2, 64, 16)),
                            ("d_rep", (128, 2, 32)),
                            ("x", (nb, NLAT, D)), ("ctx", (nb, NCTX, D)), ("cT", (128, KT, nb + 1))):
            din(name, shape)
        self.dr["out"] = nc.dram_tensor("out", [nb, NLAT, D], F32, kind="ExternalOutput").ap()
        self.dscr("wsc_in", (2, 10, 128, KT, 128), BF16)
        self.dscr("wsc_out", (2, 8, 128, KT, 128), BF16)
        self.dscr("wsc_glu", (2, 128, 4, 512), BF16)
        self.dscr("wsc_g", (2, JT, 128, KT, 128), BF16)
        self.dscr("wsc_u", (2, JT, 128, KT, 128), BF16)
        self.dscr("wsc_d", (2, 8, 128, JT, 128), BF16)
        self.dscr("ssmw", (2, NPAIR, 128, SSMW_COLS), BF16)
        self.dscr("tab", (2, 2, NPAIR, 128, TAB_COLS), F32)

        with ExitStack() as st:
            self.st = st
            fw = self.fw = FW(nc, st)
            self.dsem_dbg = DmaSem(fw, "dbg")
            self.alloc_persistent()
            self.preamble()
            fw.barrier([self.dsem_w, self.dsem_c, self.dsem_c2, self.dsem_s])
            self.dump("ssmw", self.dr["ssmw"], [], BF16)
            self.dump("tab", self.dr["tab"], [], F32)
            fw.emit()
            if "main" not in self.skip:
                self.main()
            fw.wait_dma("sp", [self.dsem_dbg, self.dsem_w, self.dsem_s, self.dsem_c, self.dsem_c2])
            fw.emit()
        return nc

    def SB(self, st, name, shape, dt=F32):
        return st.enter_context(self.nc.sbuf_tensor("sb_" + name, list(shape), dt))

    def alloc_persistent(self):
        st, nb = self.st, self.nb
        SB = lambda n, s, d=F32: self.SB(st, n, s, d)
        self.ident = SB("ident", (128, 128))
        self.ones = SB("ones", (128, 128), BF16)
        self.dselb = SB("dselb", (128, 8, 240), BF16)
        self.erT = SB("erT", (128, 4, 32))
        self.ecT = SB("ecT", (128, 4, 64))
        self.avgLb = SB("avgLb", (128, 4, 128), BF16)
        self.avgCb = SB("avgCb", (128, 4, 2, 2, 128), BF16)
        self.sguTb = SB("sguTb", (128, 2, 4, 128), BF16)
        self.sgubb = SB("sgubb", (1, 1024), BF16)
        self.poolwb = SB("poolwb", (128, 2, 2, 128), BF16)
        self.glub = SB("glub", (128, 2, 4))
        self.pscale = SB("pscale", (128, 2, 2))
        self.PAR = SB("PAR", (128, 2, 6, KT, nb + 1))
        self.RHO = SB("RHO", (128, 2, 2, NPAIR))
        self.tC = T()
        self.dsem_w = DmaSem(self.fw, "dw")
        self.dsem_c = DmaSem(self.fw, "dc")
        self.dsem_c2 = DmaSem(self.fw, "dc2")
        self.dsem_s = DmaSem(self.fw, "ds")

    def preamble(self):
        nc, fw, dr, nb = self.nc, self.fw, self.dr, self.nb
        tC = self.tC
        for dst, src in ((self.ident, "ident"), (self.erT, "erT"), (self.ecT, "ecT"), (self.glub, "glubT"), (self.pscale, "pscaleT")):
            self.dma("sp", dst[:], dr[src], [], [tC], self.dsem_c)
        for dst, src in ((self.dselb, "dsel"), (self.avgLb, "avgL"), (self.avgCb, "avgC"), (self.sguTb, "sguT"), (self.sgubb, "sgub"), (self.poolwb, "poolw")):
            self.dma("pool", dst[:], dr[src], [], [tC], self.dsem_c2)
        fw.op("dve", lambda h: h.memset(self.ones[:], 1.0), [], [tC])
        fw.wait_dma("dve", [self.dsem_c, self.dsem_c2])
        fw.wait_dma("act", [self.dsem_c, self.dsem_c2])
        fw.wait_dma("pe", [self.dsem_c, self.dsem_c2])
        fw.wait_dma("pool", [self.dsem_c, self.dsem_c2])

        with ExitStack() as ps_st:
            self.pps = ps_st.enter_context(nc.psum_tensor("pps", [128, 8, 512], F32))
            if "mod" not in self.skip:
                self.pre_mod()
            else:
                self.weight_casts()
            if "ssmgen" not in self.skip:
                self.pre_ssm()
            fw.barrier([self.dsem_w, self.dsem_c, self.dsem_c2, self.dsem_s])
            fw.emit()

    def weight_casts(self):
        dr = self.dr
        for l in range(0 if "wcast" not in self.skip else 2, 2):
            for s in range(10):
                self.dma("pool", dr["wsc_in"][l, s], dr["w_in"][l, :, s * 128:(s + 1) * 128].rearrange("(kt p) c -> p kt c", p=128), [], [], self.dsem_w)
            for m in range(8):
                self.dma("pool", dr["wsc_out"][l, m], dr["w_out"][l, :, m * 128:(m + 1) * 128].rearrange("(kt p) c -> p kt c", p=128), [], [], self.dsem_w)
                self.dma("pool", dr["wsc_d"][l, m], dr["ffn_w_down"][l, :, m * 128:(m + 1) * 128].rearrange("(j p) c -> p j c", p=128), [], [], self.dsem_w)
            self.dma("pool", dr["wsc_glu"][l], dr["glu_w"][l].rearrange("(kt p) c -> p kt c", p=128), [], [], self.dsem_w)
            for j in range(JT):
                self.dma("pool", dr["wsc_g"][l, j], dr["ffn_w_gate"][l, :, j * 128:(j + 1) * 128].rearrange("(kt p) c -> p kt c", p=128), [], [], self.dsem_w)
                self.dma("pool", dr["wsc_u"][l, j], dr["ffn_w_up"][l, :, j * 128:(j + 1) * 128].rearrange("(kt p) c -> p kt c", p=128), [], [], self.dsem_w)

    def pre_mod(self):
        nc, fw, dr, nb = self.nc, self.fw, self.dr, self.nb
        tC = self.tC
        with ExitStack() as st:
            SB = lambda n, s, d=F32: self.SB(st, n, s, d)
            cT = SB("cT_sb", (128, KT, nb + 1))
            cs = SB("cs_sb", (128, KT, nb + 1), BF16)
            bmod = SB("bmod_sb", (128, 2, 48))
            gains = SB("gains_sb", (128, 2, 4, KT))
            modT = SB("modT_sb", (128, 2, 48, nb + 1))
            tmp = SB("modtmp", (128, KT, nb + 1))
            ring = [SB(f"wmod{i}", (128, KT, 512), BF16) for i in range(2)]
            rT = [T(), T()]
            rS = [DmaSem(fw, f"wm{i}") for i in range(2)]
            t0, tm = T(), T()
            dl = DmaSem(fw, "modld")
            self.dma("sp", cT[:], dr["cT"], [], [t0], dl)
            self.dma("sp", bmod[:], dr["bmodT"], [], [t0], dl)
            self.dma("sp", gains[:], dr["gains"], [], [t0], dl)
            self.act(cs[:], cT[:], AF.Silu, [t0], [t0])
            tps = T(True)
            n = 0
            for l in range(2):
                for s in range(12):
                    r = n % 2
                    n += 1
                    self.dma("pool", ring[r][:], dr["w_mod"][l, :, s * 512:(s + 1) * 512].rearrange("(kt p) c -> p kt c", p=128), [], [rT[r]], rS[r])
                    for f in range(4):
                        t_idx = s * 4 + f
                        for kt in range(KT):
                            self.mm(self.pps[:, l, t_idx * 8:t_idx * 8 + nb + 1], ring[r][:, kt, f * 128:(f + 1) * 128], cs[:, kt, :],
                                    kt == 0, kt == KT - 1, [rT[r], t0], [tps])
                self.tt("dve", modT[:, l], fr(self.pps[:, l, 0:1], [[8, 48], [1, nb + 1]]), fr(bmod[:, l, 0:1], [[1, 48], [0, nb + 1]]),
                        ALU.add, [tps, t0], [tm])
            PAR = self.PAR
            for l in range(2):
                def gb(kind):
                    return fr(gains[:, l, kind, 0:1], [[1, KT], [0, nb + 1]])
                self.tsc(tmp[:], modT[:, l, 8:16, :], 1.0, None, ALU.add, None, [tm], [tm])
                self.tt("dve", PAR[:, l, 0], tmp[:], gb(0), ALU.mult, [tm, t0], [tC])
                self.cp("dve", PAR[:, l, 1], modT[:, l, 0:8, :], [tm], [tC])
                self.tt("dve", PAR[:, l, 2], modT[:, l, 16:24, :], gb(1), ALU.mult, [tm, t0], [tC])
                self.tsc(tmp[:], modT[:, l, 32:40, :], 1.0, None, ALU.add, None, [tm], [tm])
                self.tt("dve", PAR[:, l, 3], tmp[:], gb(2), ALU.mult, [tm, t0], [tC])
                self.cp("dve", PAR[:, l, 4], modT[:, l, 24:32, :], [tm], [tC])
                self.tt("dve", PAR[:, l, 5], modT[:, l, 40:48, :], gb(3), ALU.mult, [tm, t0], [tC])
            self.dump("par", PAR[:], [tC])
            self.weight_casts()
            fw.barrier()
            fw.emit()

    def pre_ssm(self):
        nc, fw, dr, nb = self.nc, self.fw, self.dr, self.nb
        tC = self.tC
        PI = math.pi
        with ExitStack() as st:
            SB = lambda n, s, d=F32: self.SB(st, n, s, d)
            BIG = [SB(f"big{i}", (128, 4608)) for i in range(5)]
            tB = [T() for _ in range(5)]
            MTMP = SB("mtmp", (128, 32, 128)); tMT = T()
            STG = SB("stg", (128, NPAIR, 4, 128), BF16); tSTG = T()
            MST = SB("mst", (128, NPAIR, 2, 128), BF16); tMST = T()
            lamre = SB("lamre_sb", (128, 2, 64)); lamim = SB("lamim_sb", (128, 2, 64)); logdt = SB("logdt_sb", (128, 2, 64))
            masks = SB("masks_sb", (128, 2, 128)); expv = SB("expv_sb", (128, 17)); sgn = SB("sgn_sb", (128, 2))
            idx = SB("idx_sb", (128, 2, NCH)); drep = SB("drep_sb", (128, 2, 32))
            tI = T()
            dl = DmaSem(fw, "ssmld")
            for dst, src in ((lamre, "lamre"), (lamim, "lamim"), (logdt, "logdt"), (masks, "masks"), (expv, "expv"),
                             (sgn, "sgn"), (idx, "idx"), (drep, "d_rep")):
                self.dma("sp", dst[:], dr[src], [], [tI], dl)
            fw.wait_dma("dve", [dl]); fw.wait_dma("act", [dl]); fw.wait_dma("pe", [dl])
            DT = SB("dt_sb", (128, 64)); A_ = SB("a_sb", (128, 64)); TH = SB("th_sb", (128, 64))
            AE = SB("ae_sb", (128, 64, 17)); TE = SB("te_sb", (128, 64, 17))
            CS = SB("cs2_sb", (128, 64, 17)); LY = SB("ly_sb", (128, 64, 17))
            KK = LY
            tL = T(); tK = tL
            sm = [SB(f"sm{i}", (128, 64)) for i in range(8)]
            tS = T()
            bx = SB("bx_sb", (128, 64, 16)); by = SB("by_sb", (128, 64, 16))
            cx = SB("cx_sb", (128, 64, 16)); cy = SB("cy_sb", (128, 64, 16))
            BX = SB("BX_sb", (128, 64, 16)); BY = SB("BY_sb", (128, 64, 16))
            tBC = T()
            PHQ = SB("phq_sb", (128, 2, NPAIR)); tPH = T()
            dbc = DmaSem(fw, "bcld")
            STG2, tSTG2 = STG, tSTG
            sgn_m = sgn[:, 0:1]
            sgn_pm = sgn[:, 1:2]
            pq = [0]

            def psq():
                i = pq[0] % 6
                pq[0] += 1
                if not hasattr(self, "_pqT"):
                    self._pqT = [T(True) for _ in range(6)]
                return self.pps[:, 2 + i, 0:128], self._pqT[i]

            try:
              for l in range(2):
                lr = lamre[:, l, :]; li = lamim[:, l, :]
                self.dma("sp", bx[:], dr["b_x"][:, l], [], [tBC], dbc)
                self.dma("sp", by[:], dr["b_y"][:, l], [], [tBC], dbc)
                self.dma("sp", cx[:], dr["c_x"][:, l], [], [tBC], dbc)
                self.dma("sp", cy[:], dr["c_y"][:, l], [], [tBC], dbc)
                self.act(DT[:], logdt[:, l, :], AF.Exp, [tI, tS], [tS])
                self.tt("dve", A_[:], lr, DT[:], ALU.mult, [tI, tS], [tS])
                self.tt("dve", TH[:], li, DT[:], ALU.mult, [tI, tS], [tS])
                a_bc = fr(A_[:, 0:1], [[1, 64], [0, 17]])
                th_bc = fr(TH[:, 0:1], [[1, 64], [0, 17]])
                e_bc = fr(expv[:, 0:1], [[0, 64], [1, 17]])
                self.tt("dve", AE[:], a_bc, e_bc, ALU.mult, [tS, tI], [tL])
                self.act(AE[:], AE[:], AF.Exp, [tL], [tL])
                self.tt("dve", TE[:], th_bc, e_bc, ALU.mult, [tS, tI], [tL])
                self.rr(TE[:], TE[:], KK[:], [tL], [tL], tK)
                self.tsc(CS[:], TE[:], PI / 2, None, ALU.add, None, [tL], [tL])
                self.rr(CS[:], CS[:], KK[:], [tL], [tL], tK)
                self.act(TE[:], TE[:], AF.Sin, [tL], [tL])
                self.act(CS[:], CS[:], AF.Sin, [tL], [tL])
                self.tt("dve", CS[:], CS[:], AE[:], ALU.mult, [tL], [tL])
                self.tt("dve", TE[:], TE[:], AE[:], ALU.mult, [tL], [tL])
                self.tsc(LY[:], TE[:], sgn_m, None, ALU.mult, None, [tL, tI], [tL])
                LX = CS
                LI = TE
                self.dump("LX", LX[:], [tL]); self.dump("LI", LI[:], [tL])
                self._chk("ssmA")
                lbr = fr(LX[:, 0:1, 9:10], [[17, 64]])
                lbi = fr(LI[:, 0:1, 9:10], [[17, 64]])
                xr, den, t1, t2, br_, bi_, bis, e8 = [s_[:] for s_ in sm]
                self.tsc(xr, lbr, -1.0, None, ALU.add, None, [tL], [tS])
                self.tt("dve", den, lr, lr, ALU.mult, [tI], [tS])
                self.tt("dve", t1, li, li, ALU.mult, [tI], [tS])
                self.tt("dve", den, den, t1, ALU.add, [tS], [tS])
                fw.op("dve", lambda h: h.reciprocal(den, den), [tS], [tS])
                self.tt("dve", t1, xr, lr, ALU.mult, [tS, tI], [tS])
                self.tt("dve", t2, lbi, li, ALU.mult, [tL, tI], [tS])
                self.tt("dve", t1, t1, t2, ALU.add, [tS], [tS])
                self.tt("dve", br_, t1, den, ALU.mult, [tS], [tS])
                self.tt("dve", t1, lbi, lr, ALU.mult, [tL, tI], [tS])
                self.tt("dve", t2, xr, li, ALU.mult, [tS, tI], [tS])
                self.tt("dve", t1, t1, t2, ALU.subtract, [tS], [tS])
                self.tt("dve", bi_, t1, den, ALU.mult, [tS], [tS])
                self.tsc(bis, bi_, sgn_m, None, ALU.mult, None, [tS, tI], [tS])
                br_bc = fr(br_[:, 0:1], [[1, 64], [0, 16]])
                bis_bc = fr(bis[:, 0:1], [[1, 64], [0, 16]])
                w1 = fr(BIG[0][:, 0:1], [[16, 64], [1, 16]])
                w2 = fr(BIG[1][:, 0:1], [[16, 64], [1, 16]])
                fw.wait_dma("dve", [dbc])
                self.tt("dve", w1, bx[:], br_bc, ALU.mult, [tBC, tS], [tB[0]])
                self.tt("dve", w2, by[:], bis_bc, ALU.mult, [tBC, tS], [tB[1]])
                self.tt("dve", BX[:], w1, w2, ALU.add, [tB[0], tB[1]], [tBC])
                self.tt("dve", w1, by[:], br_bc, ALU.mult, [tBC, tS], [tB[0]])
                self.tt("dve", w2, bx[:], bis_bc, ALU.mult, [tBC, tS], [tB[1]])
                self.tt("dve", BY[:], w1, w2, ALU.subtract, [tB[0], tB[1]], [tBC])
                self.tsc(cx[:], cx[:], sgn_pm, None, ALU.mult, None, [tBC, tI], [tBC])
                self.tsc(cy[:], cy[:], sgn_pm, None, ALU.mult, None, [tBC, tI], [tBC])
                self.act(e8, A_[:], AF.Exp, [tS], [tS], scale=8.0)
                for gq in range(2):
                    self.cp("dve", self.RHO[gq * 64:(gq + 1) * 64, l], fr(e8[gq * 64:(gq + 1) * 64, gq:gq + 1], [[32, 2], [2, NPAIR]]), [tS], [tC])
                self.tsc(t1, TH[:], 8.0, None, ALU.mult, None, [tS], [tS])
                self.rr(t1, t1, t2, [tS], [tS], tS)
                for gq in range(2):
                    self.cp("dve", PHQ[gq * 64:(gq + 1) * 64], fr(t1[gq * 64:(gq + 1) * 64, gq:gq + 1], [[32, 2], [2, NPAIR]]), [tS], [tPH])

                self.dump("BX", BX[:], [tBC]); self.dump("cx", cx[:], [tBC]); self.dump("RHO", self.RHO[:], [tC]); self.dump("PHQ", PHQ[:], [tPH])
                self._chk("ssmB")
                for k in range(2):
                    if k == 0:
                        iB, sB, iBp, sBp, iC, sC = 15, -1, 7, -1, 9, 1
                    else:
                        iB, sB, iBp, sBp, iC, sC = 8, 1, 0, 1, 16, -1

                    def build(dst, i0, stp, X, Y, tsrc):
                        lx = fr(LX[:, k * 32:k * 32 + 1, i0:i0 + 1], [[17, 32], [stp, 8], [0, 16]])
                        ly = fr(LY[:, k * 32:k * 32 + 1, i0:i0 + 1], [[17, 32], [stp, 8], [0, 16]])
                        xv = fr(X[:, k * 32:k * 32 + 1, 0:1], [[16, 32], [0, 8], [1, 16]])
                        yv = fr(Y[:, k * 32:k * 32 + 1, 0:1], [[16, 32], [0, 8], [1, 16]])
                        o4 = lambda b: fr(b[:, 0:1], [[128, 32], [16, 8], [1, 16]])
                        self.tt("dve", o4(BIG[0]), lx, xv, ALU.mult, [tL, tsrc], [tB[0]])
                        self.tt("dve", o4(BIG[1]), ly, yv, ALU.mult, [tL, tsrc], [tB[1]])
                        self.tt("dve", o4(BIG[dst]), o4(BIG[0]), o4(BIG[1]), ALU.add, [tB[0], tB[1]], [tB[dst]])

                    build(2, iB, sB, BX, BY, tBC)
                    build(3, iBp, sBp, BX, BY, tBC)
                    build(4, iC, sC, cx, cy, tBC)
                    Bt, Bpt, Ct = BIG[2], BIG[3], BIG[4]
                    self.dump("Bt", Bt[:], [tB[2]]); self.dump("Ct", Ct[:], [tB[4]])
                    self._chk("ssmC")
                    stg = STG
                    fw.op("dve", lambda h: h.memset(STG[:], 0.0), [], [tSTG])
                    for g in range(NG):
                        q, gq = g // 2, g % 2
                        pa, pt = psq()
                        self.tr(pa, Bt[:, g * 128:(g + 1) * 128], self.ident[:], [tB[2], tC], [pt])
                        self.cp("act", stg[:, q, gq, gq * 64:(gq + 1) * 64], pa[:, 0:64], [pt], [tSTG])
                        self.cp("act", stg[:, q, 2 + gq, gq * 64:(gq + 1) * 64], pa[:, 64:128], [pt], [tSTG])
                        pm, pmt = psq()
                        self.mm(pm, Bpt[:, g * 128:(g + 1) * 128], Ct[:, g * 128:(g + 1) * 128], True, True, [tB[3], tB[4]], [pmt])
                        if k == 0:
                            self.tt("dve", MTMP[:, g, :], pm, masks[:, 0, :], ALU.mult, [pmt, tI], [tMT])
                            self.stt(MTMP[:, g, :], self.ident[:], drep[:, l, g:g + 1], MTMP[:, g, :], ALU.mult, ALU.add, [tC, tI, tMT], [tMT])
                        else:
                            self.tt("dve", BIG[0][:, 0:128], pm, masks[:, 1, :], ALU.mult, [pmt, tI], [tB[0]])
                            self.tt("dve", MST[:, q, gq, :], BIG[0][:, 0:128], MTMP[:, g, :], ALU.add, [tB[0], tMT], [tMST])
                        if g == 0:
                            self._chk("ssmD1")
                    self._chk("ssmD2")
                    c0 = 256 + k * 512
                    dview = dr["ssmw"][l].rearrange("q p c -> p q c")
                    self.dma("sp", dview[:, :, c0:c0 + 512], stg[:].rearrange("p q v c -> p q (v c)"), [tSTG], [], self.dsem_s)
                    self._chk("ssmD")
                    fw.op("dve", lambda h: h.memset(STG[:], 0.0), [], [tSTG])
                    for v, (r0, gq) in enumerate(((0, 0), (0, 1), (64, 0), (64, 1))):
                        ro = 64 * gq
                        src = fr(Ct[r0:r0 + 64, gq * 128:gq * 128 + 1], [[256, NPAIR], [1, 128]])
                        self.cp("dve", STG2[ro:ro + 64, :, v, :], src, [tB[4]], [tSTG2])
                    c0 = 1280 + k * 512
                    self.dma("sp", dview[:, :, c0:c0 + 512], STG2[:].rearrange("p q v c -> p q (v c)"), [tSTG2], [], self.dsem_s)
                    self._chk("ssmE")
                    ARG, K2, SINT, COST, NSIN = [fr(b[:, 0:1], [[NCH, NPAIR], [1, NCH]]) for b in BIG]
                    ph_bc = fr(PHQ[:, k, 0:1], [[1, NPAIR], [0, NCH]])
                    ix_bc = fr(idx[:, k, 0:1], [[0, NPAIR], [1, NCH]])
                    self.tt("dve", ARG, ph_bc, ix_bc, ALU.mult, [tPH, tI], [tB[0]])
                    self.rr(ARG, ARG, K2, [tB[0]], [tB[0]], tB[1])
                    self.tsc(COST, ARG, PI / 2, None, ALU.add, None, [tB[0]], [tB[3]])
                    self.rr(COST, COST, K2, [tB[3]], [tB[3]], tB[1])
                    self.act(SINT, ARG, AF.Sin, [tB[0]], [tB[2]])
                    self.act(COST, COST, AF.Sin, [tB[3]], [tB[3]])
                    self.tsc(NSIN, SINT, -1.0, None, ALU.mult, None, [tB[2]], [tB[4]])
                    tview = dr["tab"][l, k].rearrange("q p c -> p q c")
                    s_re, s_im = (SINT, NSIN) if k == 0 else (NSIN, SINT)
                    t_re, t_im = (tB[2], tB[4]) if k == 0 else (tB[4], tB[2])
                    self.dma("sp", tview[:, :, 0:NCH], COST, [tB[3]], [], self.dsem_s)
                    self.dma("sp", tview[:, :, NCH:2 * NCH], COST, [tB[3]], [], self.dsem_s)
                    self.dma("sp", tview[:, :, 2 * NCH:3 * NCH], s_re, [t_re], [], self.dsem_s)
                    self.dma("sp", tview[:, :, 3 * NCH:4 * NCH], s_im, [t_im], [], self.dsem_s)
                dview = dr["ssmw"][l].rearrange("q p c -> p q c")
                self.dma("sp", dview[:, :, 0:256], MST[:].rearrange("p q g c -> p q (g c)"), [tMST], [], self.dsem_s)
            except _Stop:
                pass
            fw.barrier([self.dsem_s])
            fw.emit()

    def main(self):
        nc, fw, dr, nb, st = self.nc, self.fw, self.dr, self.nb, self.st
        SB = lambda n, s, d=F32: self.SB(st, n, s, d)
        self.XT = SB("XT", (128, KT, NT))
        self.XTb = Buf(self.XT, KT, NT, 128)
        self.HT = SB("HT", (128, KT, NT), BF16)
        self.HTb = Buf(self.HT, KT, NT, 128)
        self.HTflat = self.HT[:].rearrange("p a b -> p (a b)")
        self.PH = SB("PH", (128, 23040), BF16)
        self.SCf = SB("SCf", (128, 4096))
        self.SCb = SB("SCb", (128, 4096), BF16)
        self.WR = SB("WR", (128, 2, 2816), BF16)
        self.tWR = [T(), T()]
        self.dWR = [DmaSem(fw, "wr0"), DmaSem(fw, "wr1")]
        self.wn = 0
        self.ps = st.enter_context(nc.psum_tensor("ps", [128, 8, 512], F32))
        self.psT = [T(True) for _ in range(8)]
        self.psn = 0
        self.psqn = 0
        self.psxn = 0
        self.ps_banks = list(range(8))
        PH = self.PH
        self.UA = PH[:, 0:4608].rearrange("p (m n) -> p m n", m=2)
        self.UAb = Buf(self.UA, 2, NT, 128)
        self.V = PH[:, 4608:9216].rearrange("p (c f) -> p c f", c=18)
        self.tV = [T() for _ in range(18)]
        self.US = PH[:, 9216:18432].rearrange("p (m s c) -> p m s c", m=4, s=8)
        self.GT = PH[:, 9216:18432].rearrange("p (m n) -> p m n", m=4)
        self.tUS = [T() for _ in range(4)]
        self.GTb = Buf(self.GT, 4, NT, 128)
        self.PP = PH[:, 18432:23040].rearrange("p (m n) -> p m n", m=2)
        self.PPb = Buf(self.PP, 2, NT, 128)
        self.dsem_x = [DmaSem(fw, f"x{i}") for i in range(4)]
        self.dsem_o = [DmaSem(fw, f"o{i}") for i in range(4)]
        self.dsem_t = [DmaSem(fw, "t0"), DmaSem(fw, "t1")]
        self.dsem_wo = DmaSem(fw, "wo")
        self.tIO = [T() for _ in range(4)]
        fw.wait_dma("sp", [self.dsem_w, self.dsem_s, self.dsem_c, self.dsem_c2])
        for bi in range(nb):
            self.load_x(bi)
            if self.stop == "load":
                break
            for l in range(self.nlayers):
                self.layer(l, bi)
                if self.stop is not None:
                    break
            if self.stop is not None:
                break
            self.store_out(bi)
            fw.barrier()
        fw.wait_dma("sp", self.dsem_o + [self.dsem_dbg])

    def bank(self):
        b = self.ps_banks[self.psn % len(self.ps_banks)]
        self.psn += 1
        return self.ps[:, b, :], self.psT[b]

    def bank67(self):
        b = 6 + self.psqn % 2
        self.psqn += 1
        return self.ps[:, b, :], self.psT[b]

    def wslot(self):
        s = self.wn % 2
        self.wn += 1
        return s

    def load_x(self, bi):
        fw, dr = self.fw, self.dr
        self.ps_banks = list(range(8))
        XIN = [self.SCf[:, k * 1024:(k + 1) * 1024] for k in range(4)]
        tX = self.tIO
        for i in range(18):
            s = i % 4
            src = dr["ctx"][bi, i * 128:(i + 1) * 128, :] if i < 2 else dr["x"][bi, (i - 2) * 128:(i - 1) * 128, :]
            self.dma("sp", XIN[s], src, [], [tX[s]], self.dsem_x[s])
            c0 = i * 128
            for half in range(2):
                bk, bt = self.bank()
                for jj in range(4):
                    j = half * 4 + jj
                    fw.op("pe", (lambda o, a: (lambda h: h.transpose(o, a, self.ident[:])))(bk[:, jj * 128:(jj + 1) * 128], XIN[s][:, j * 128:(j + 1) * 128]),
                          [tX[s], self.tC], [bt], inc=(jj == 3))
                wr = self.XTb.ts(half * 4, half * 4 + 4, c0, c0 + 128)
                if i < 2:
                    self.cp("dve", self.XT[:, half * 4:half * 4 + 4, c0:c0 + 128], bk.rearrange("p (a b) -> p a b", a=4), [bt], wr)
                else:
                    r0 = 2 * (i - 2)
                    o4 = fr(self.XT[:, half * 4:half * 4 + 1, c0:c0 + 1], [[NT, 4], [64, 2], [1, 64]])
                    i4 = fr(bk[:, 0:1], [[128, 4], [64, 2], [1, 64]])
                    if half == 0:
                        pe = fr(self.erT[:, 0:1, r0:r0 + 1], [[32, 4], [1, 2], [0, 64]])
                    else:
                        pe = fr(self.ecT[:, 0:1, 0:1], [[64, 4], [0, 2], [1, 64]])
                    self.tt("dve", o4, i4, pe, ALU.add, [bt, self.tC], wr)
        self.dump(f"xt0_{bi}", self.XT[:], self.XTb.ts(0, KT, 0, NT))

    def store_out(self, bi):
        fw, dr = self.fw, self.dr
        self.ps_banks = list(range(8))
        OS = [self.SCf[:, k * 1024:(k + 1) * 1024] for k in range(4)]
        tO = self.tIO
        for i in range(16):
            s = i % 4
            c0 = NCTX + i * 128
            for half in range(2):
                bk, bt = self.bank()
                for jj in range(4):
                    j = half * 4 + jj
                    fw.op("pe", (lambda o, a: (lambda h: h.transpose(o, a, self.ident[:])))(bk[:, jj * 128:(jj + 1) * 128], self.XT[:, j, c0:c0 + 128]),
                          self.XTb.ts(j, j + 1, c0, c0 + 128) + [self.tC], [bt], inc=(jj == 3))
                self.cp("act" if half == 0 else "dve", OS[s][:, half * 512:(half + 1) * 512], bk, [bt], [tO[s]])
            self.dma("sp", dr["out"][bi, i * 128:(i + 1) * 128, :], OS[s], [tO[s]], [], self.dsem_o[s])

    def rstd_from_bank(self, bk, bt, w, RS, tRS):
        self.act(RS[:, 0:w], bk[:, 0:w], AF.Sqrt, [bt], [tRS], bias=EPS, scale=1.0 / D)
        self.fw.op("dve", lambda h: h.reciprocal(RS[:, 0:w], RS[:, 0:w]), [tRS], [tRS])

    def prenorm(self, l, a, b, qa, qb, bi):
        w = b - a
        pb = self.nb if a < NCTX else bi
        SQ = self.SCb[:, 0:4096].rearrange("p (j n) -> p j n", j=KT)
        RS = self.SCf[:, 0:512]
        TMP = [self.SCf[:, 512:1024], self.SCf[:, 1024:1536]]
        st = self._pn
        self.act(SQ[:, :, 0:w], self.XT[:, :, a:b], AF.Square, self.XTb.ts(0, KT, a, b), [st["sq"]])
        bk, bt = self.bank()
        for j in range(KT):
            self.mm(bk[:, 0:w], self.ones[:], SQ[:, j, 0:w], j == 0, j == KT - 1, [st["sq"], self.tC], [bt])
        self.rstd_from_bank(bk, bt, w, RS, st["rs"])
        for j in range(KT):
            r = j % 2
            self.stt(TMP[r][:, 0:w], self.XT[:, j, a:b], self.PAR[:, l, qa, j, pb:pb + 1], RS[:, 0:w], ALU.mult, ALU.mult,
                     self.XTb.ts(j, j + 1, a, b) + [st["rs"], self.tC], [st["tmp"][r]])
            self.act(self.HT[:, j, a:b], TMP[r][:, 0:w], AF.Identity, [st["tmp"][r], self.tC], self.HTb.ts(j, j + 1, a, b),
                     bias=self.PAR[:, l, qb, j, pb:pb + 1])

    def postnorm_update(self, l, a, b, q, bi, MT, tMT, slot=None):
        w = b - a
        pb = self.nb if a < NCTX else bi
        if slot is None:
            SQ = self.SCb[:, 0:4096].rearrange("p (j n) -> p j n", j=KT)
            RS = self.SCf[:, 0:512]
            TMP = [self.SCf[:, 512:1024], self.SCf[:, 1024:1536]]
            st = self._pn
        else:
            SQ = self.SCb[:, slot * 2048:(slot + 1) * 2048].rearrange("p (j n) -> p j n", j=KT)
            RS = self.SCf[:, slot * 256:(slot + 1) * 256]
            TMP = [self.SCf[:, 512 + (2 * slot + i) * 256:512 + (2 * slot + i + 1) * 256] for i in range(2)]
            st = self._pn2[slot]
        for m in range(KT):
            self.tt("pool", SQ[:, m, 0:w], MT[:, m, 0:w], MT[:, m, 0:w], ALU.mult, [tMT[m]], [st["sqm"][m]])
        bk, bt = self.bank()
        for m in range(KT):
            self.mm(bk[:, 0:w], self.ones[:], SQ[:, m, 0:w], m == 0, m == KT - 1, [st["sqm"][m], self.tC], [bt])
        self.rstd_from_bank(bk, bt, w, RS, st["rs"])
        for m in range(KT):
            r = m % 2
            self.tt("dve", TMP[r][:, 0:w], MT[:, m, 0:w], RS[:, 0:w], ALU.mult, [tMT[m], st["rs"]], [st["tmp"][r]])
            xt = self.XTb.ts(m, m + 1, a, b)
            self.stt(self.XT[:, m, a:b], TMP[r][:, 0:w], self.PAR[:, l, q, m, pb:pb + 1], self.XT[:, m, a:b], ALU.mult, ALU.add,
                     [st["tmp"][r], self.tC] + xt, xt)

    def layer(self, l, bi):
        fw = self.fw
        last = (l == 1)
        self._pn = {"sq": T(), "rs": T(), "tmp": [T(), T()], "sqm": [T() for _ in range(KT)], "vf": [T(), T()], "ln": T()}
        fw.barrier()
        self.ps_banks = list(range(8))
        lat_tiles = [(NCTX + 512 * i, NCTX + 512 * (i + 1)) for i in range(4)]
        tiles = [(0, NCTX)] + lat_tiles
        self.prenorm(l, tiles[0][0], tiles[0][1], 0, 1, bi)
        self._win_seq = [(ti, pair) for ti, (a, b) in enumerate(tiles) for pair in range(5)
                         if not (last and a < NCTX and pair not in (2, 3))]
        self._win_pos = 0
        self._win_slots = {}
        self.w_in_prefetch(l)
        for ti, (a, b) in enumerate(tiles):
            if ti + 1 < len(tiles):
                self.prenorm(l, tiles[ti + 1][0], tiles[ti + 1][1], 0, 1, bi)
            self.w_in(l, [(a, b)], last, ti)
        self.dump(f"ua_{l}_{bi}", self.UA, self.UAb.ts(0, 2, 0, NT), BF16)
        self.dump(f"v_{l}_{bi}", self.V, self.tV, BF16)
        self.dump(f"us_{l}_{bi}", self.US, self.tUS, BF16)
        self.dump(f"pp_{l}_{bi}", self.PP, self.PPb.ts(0, 2, 0, NT), BF16)
        fw.barrier()
        if self.stop == "A":
            return
        self.ps_banks = [0, 1, 2, 3]
        self.sgu(l, last)
        self.ssm(l, last)
        self.pool(l, last)
        self.glu(l, last)
        self.dump(f"a_{l}_{bi}", self.UA, self.UAb.ts(0, 2, 0, NT), BF16)
        self.dump(f"s_{l}_{bi}", self.GT, self.GTb.ts(0, 4, 0, NT) + self.tUS, BF16)
        self.dump(f"p_{l}_{bi}", self.PP, self.PPb.ts(0, 2, 0, NT), BF16)
        fw.barrier()
        if self.stop == "B":
            return
        self.ps_banks = list(range(8))
        self.w_out(l, bi, last)
        self.dump(f"xmid_{l}_{bi}", self.XT[:], self.XTb.ts(0, KT, 0, NT))
        fw.barrier()
        if self.stop == "C":
            return
        self.ffn(l, bi, last)
        self.dump(f"xend_{l}_{bi}", self.XT[:], self.XTb.ts(0, KT, 0, NT))

    def w_in_prefetch(self, l):
        if self._win_pos >= len(self._win_seq):
            return
        key = self._win_seq[self._win_pos]
        self._win_pos += 1
        s = self.wslot()
        dst = self.WR[:, s, 0:2048].rearrange("p (s k c) -> p s k c", s=2, k=KT)
        self.dma("sp", dst, self.dr["wsc_in"][l, 2 * key[1]:2 * key[1] + 2].rearrange("s p k c -> p s k c"), [], [self.tWR[s]], self.dWR[s])
        self._win_slots[key] = s

    def w_in(self, l, tiles, last, ti):
        fw, dr = self.fw, self.dr
        WRv = lambda s: self.WR[:, s, 0:2048].rearrange("p (s k c) -> p s k c", s=2, k=KT)
        VF = [self.SCf[:, 1536:1792], self.SCf[:, 1792:2048]]
        VC = self.SCf[:, 2048:2304]
        VS = self.SCf[:, 2304:2560]
        S1 = self.SCf[:, 2560:2564]
        S2 = self.SCf[:, 2564:2568]
        tVF = self._pn["vf"]
        tLN = self._pn["ln"]
        nvf = 0
        for pair in range(5):
            if (ti, pair) not in self._win_slots:
                continue
            s = self._win_slots.pop((ti, pair))
            self.w_in_prefetch(l)
            W = WRv(s)
            tW = self.tWR[s]
            for (a, b) in tiles:
                w = b - a
                isctx = a < NCTX
                hts = self.HTb.ts(0, KT, a, b)
                if pair == 1:
                    for t0 in range(a, b, 128):
                        ck = t0 // 128
                        bk, bt = self.bank()
                        for kt in range(KT):
                            rhs = fr(W[:, 0:1, kt:kt + 1, 0:1], [[KT * 128, 2], [1, 128]])
                            self.mm(bk[:, 0:256], self.HT[:, kt, t0:t0 + 128], rhs, kt == 0, kt == KT - 1, hts + [tW], [bt])
                        r = nvf % 2
                        nvf += 1
                        vf = VF[r]
                        self.act(vf, bk[:, 0:256], AF.Gelu, [bt], [tVF[r]])
                        v3 = lambda ap: ap.rearrange("p (h d) -> p h d", h=4)
                        bc = lambda ap: fr(ap[:, 0:1], [[1, 4], [0, 64]])
                        fw.op("dve", (lambda o, i: (lambda h: h.tensor_reduce(o, i, AX.X, ALU.add)))(S1, v3(vf)), [tVF[r]], [tLN])
                        self.tsc(S1, S1, -1.0 / 64, None, ALU.mult, None, [tLN], [tLN])
                        self.tt("dve", v3(VC), v3(vf), bc(S1), ALU.add, [tVF[r], tLN], [tLN])
                        self.tt("dve", VS, VC, VC, ALU.mult, [tLN], [tLN])
                        fw.op("dve", (lambda o, i: (lambda h: h.tensor_reduce(o, i, AX.X, ALU.add)))(S2, v3(VS)), [tLN], [tLN])
                        self.act(S2, S2, AF.Sqrt, [tLN], [tLN], bias=EPS, scale=1.0 / 64)
                        fw.op("dve", (lambda o: (lambda h: h.reciprocal(o, o)))(S2), [tLN], [tLN])
                        self.tt("dve", v3(self.V[:, ck, :]), v3(VC), bc(S2), ALU.mult, [tLN], [self.tV[ck]])
                    continue
                for mm_ in range(2):
                    bk, bt = self.bank()
                    for kt in range(KT):
                        self.mm(bk[:, 0:w], W[:, mm_, kt, :], self.HT[:, kt, a:b], kt == 0, kt == KT - 1, hts + [tW], [bt])
                    if pair == 0:
                        self.act(self.UA[:, mm_, a:b], bk[:, 0:w], AF.Gelu, [bt], self.UAb.ts(mm_, mm_ + 1, a, b))
                    elif pair in (2, 3):
                        m = (pair - 2) * 2 + mm_
                        c0, nc_ = a // 8, w // 8
                        o3 = fr(self.US[:, m:m + 1, 0:1, c0:c0 + 1], [[NCH, 8], [1, nc_]])
                        i3 = fr(bk[:, 0:1], [[1, 8], [8, nc_]])
                        self.cp("act", o3, i3, [bt], [self.tUS[m]] + self.GTb.ts(m, m + 1, 0, NT))
                    else:
                        self.cp("dve", self.PP[:, mm_, a:b], bk[:, 0:w], [bt], self.PPb.ts(mm_, mm_ + 1, a, b))

    def sgu(self, l, last):
        for ck in range(2 if last else 0, 18):
            c0 = ck * 128
            for hp in range(2):
                qs = []
                bk, qt = self.bank67()
                for hh in range(2):
                    hd = 2 * hp + hh
                    qa = bk[:, hh * 128:(hh + 1) * 128]
                    self.mm(qa, self.V[:, ck, hp * 128:(hp + 1) * 128], self.sguTb[:, l, hd, :], True, False, [self.tV[ck], self.tC], [qt], inc=False)
                    o = (l * 4 + hd) * 128
                    self.mm(qa, self.ones[0:1, 0:128], self.sgubb[0:1, o:o + 128], False, True, [self.tC], [qt], inc=(hh == 1))
                    qs.append((qa, qt))
                for hh in range(2):
                    r0 = 64 * hh
                    ts_ = self.UAb.ts(hp, hp + 1, c0, c0 + 128)
                    self.tt("dve", self.UA[r0:r0 + 64, hp, c0:c0 + 128], qs[hh][0][r0:r0 + 64, :], self.UA[r0:r0 + 64, hp, c0:c0 + 128], ALU.mult,
                            [qs[hh][1]] + ts_, ts_)

    def pool(self, l, last):
        QT = [self.SCb[:, i * 128:(i + 1) * 128] for i in range(8)]
        tQ = [T() for _ in range(8)]

        def first(ti, m, slot):
            c0 = ti * 128
            bk, qt = self.bank67()
            qa = bk[:, 0:128]
            self.mm(qa, self.PP[:, m, c0:c0 + 128], self.poolwb[:, l, m, :], True, True, self.PPb.ts(m, m + 1, c0, c0 + 128) + [self.tC], [qt])
            self.cp("act", QT[slot], qa, [qt], [tQ[slot]])

        def finish(ti, m, rhs_list):
            c0 = ti * 128
            bk, qt = self.bank67()
            for gq in range(2):
                qa = bk[:, gq * 128:(gq + 1) * 128]
                n = len(rhs_list[gq])
                for i, (slot, rhs) in enumerate(rhs_list[gq]):
                    self.mm(qa, QT[slot], rhs, i == 0, i == n - 1, [tQ[slot], self.tC], [qt], inc=(gq == 1 and i == n - 1))
            for gq in range(2):
                qa = bk[:, gq * 128:(gq + 1) * 128]
                r0 = 64 * gq
                self.tsc(self.PP[r0:r0 + 64, m, c0:c0 + 128], qa[r0:r0 + 64, :], self.pscale[r0:r0 + 64, l, m:m + 1], None, ALU.mult, None,
                         [qt, self.tC], self.PPb.ts(m, m + 1, c0, c0 + 128))

        if not last:
            for m in range(2):
                for ti in range(2):
                    first(ti, m, 4 + 2 * m + ti)
            for m in range(2):
                for tt_ in range(2):
                    finish(tt_, m, [[(4 + 2 * m + st_, self.avgCb[:, 2 * m + gq, st_, tt_, :]) for st_ in range(2)] for gq in range(2)])
        n = 0
        for ti in range(2, 18):
            for m in range(2):
                slot = n % 4
                n += 1
                first(ti, m, slot)
                finish(ti, m, [[(slot, self.avgLb[:, 2 * m + gq, :])] for gq in range(2)])

    def glu(self, l, last):
        dr = self.dr
        s = self.wslot()
        GW = self.WR[:, s, 0:2048].rearrange("p (k c) -> p k c", k=4)
        self.dma("sp", GW, dr["wsc_glu"][l], [], [self.tWR[s]], self.dWR[s])
        SG = [self.SCb[:, 1024:1536], self.SCb[:, 1536:2048], self.SCb[:, 2048:2560], self.SCb[:, 2560:3072]]
        tSG = [T() for _ in range(4)]
        tiles = [(NCTX + 512 * i, NCTX + 512 * (i + 1)) for i in range(4)]
        if not last:
            tiles = [(0, NCTX)] + tiles
        for (a, b) in tiles:
            w = b - a
            gts = self.GTb.ts(0, 4, a, b)
            bks = []
            for m in range(4):
                bk, bt = self.bank()
                for kt in range(4):
                    self.mm(bk[:, 0:w], GW[:, kt, m * 128:(m + 1) * 128], self.GT[:, kt, a:b], kt == 0, kt == 3, gts + [self.tWR[s]], [bt])
                bks.append((bk, bt))
            for m in range(4):
                bk, bt = bks[m]
                self.act(SG[m][:, 0:w], bk[:, 0:w], AF.Sigmoid, [bt, self.tC], [tSG[m]], bias=self.glub[:, l, m:m + 1])
            for m in range(4):
                ts_ = self.GTb.ts(m, m + 1, a, b)
                self.tt("pool", self.GT[:, m, a:b], self.GT[:, m, a:b], SG[m][:, 0:w], ALU.mult, [tSG[m]] + ts_, ts_)

    def ssm(self, l, last):
        fw, dr = self.fw, self.dr
        HF = self.HTflat
        UTs = [HF[:, r * 2304:(r + 1) * 2304].rearrange("p (g c) -> p g c", g=8) for r in range(2)]
        GYs = [HF[:, 4608 + r * 2304:4608 + (r + 1) * 2304].rearrange("p (g c) -> p g c", g=8) for r in range(2)]
        Hbs = [HF[:, 9216 + i * 576:9216 + (i + 1) * 576].rearrange("p (r c) -> p r c", r=2) for i in range(4)]
        tUTs = [[T() for _ in range(8)] for _ in range(2)]
        tGYs = [[T() for _ in range(8)] for _ in range(2)]
        tHs = [T() for _ in range(4)]
        TB = [self.SCf[:, i * 1152:(i + 1) * 1152].rearrange("p (t r c) -> p t r c", t=2, r=2) for i in range(2)]
        tTB = [T(), T()]
        F2 = HF[:, 11520:14976].bitcast(F32)
        sets = [[self.SCf[:, 2304 + i * 576:2304 + (i + 1) * 576].rearrange("p (r c) -> p r c", r=2) for i in range(3)],
                [F2[:, i * 576:(i + 1) * 576].rearrange("p (r c) -> p r c", r=2) for i in range(3)]]
        tsets = [[T(), T(), T()], [T(), T(), T()]]
        nunit = 0

        def swapped(ap3, stride):
            return fr(ap3[:, 1:2, 0:1], [[-stride, 2], [1, NCH]])

        def sel_in(j):
            UT, tUT = UTs[j % 2], tUTs[j % 2]
            for gp in range(8):
                bk, bt = self.bank()
                for s in range(8):
                    self.mm(bk[:, 0:NCH], self.dselb[:, gp, 112 - 16 * s:240 - 16 * s], self.US[:, j, s, :], s == 0, s == 7,
                            [self.tUS[j], self.tC], [bt])
                self.cp("act", UT[:, gp, :], bk[:, 0:NCH], [bt], [tUT[gp]])

        def front(j, qq):
            nonlocal nunit
            UT, tUT = UTs[j % 2], tUTs[j % 2]
            q = j * 4 + qq
            s = self.wslot()
            SW = self.WR[:, s, 0:SSMW_COLS]
            tSW = self.tWR[s]
            self.dma("sp", SW, dr["ssmw"][l, q], [], [tSW], self.dWR[s])
            g0, g1 = 2 * qq, 2 * qq + 1
            Hk = []
            for k in range(2):
                u = nunit
                nunit += 1
                r = u % 2
                self.dma("sp", TB[r], dr["tab"][l, k, q].rearrange("p (t r c) -> p t r c", t=2, r=2), [], [tTB[r]], self.dsem_t[r])
                COS2, SIN2 = TB[r][:, 0], TB[r][:, 1]
                XP, XQ, G = sets[r]
                tXP, tXQ, tG = tsets[r]
                Hb, tH = Hbs[u % 4], tHs[u % 4]
                Hk.append((Hb, tH))
                xb = 4 + 2 * r
                X = self.ps[:, xb:xb + 2, 0:NCH]
                tX, tX2 = self.psT[xb], self.psT[xb + 1]
                zb = lambda v: SW[:, 256 + k * 512 + v * 128:256 + k * 512 + (v + 1) * 128]
                for ri in range(2):
                    self.mm(self.ps[:, xb + ri, 0:NCH], zb(2 * ri), UT[:, g0, :], True, False, [tSW, tUT[g0]], [tX, tX2], inc=False)
                    self.mm(self.ps[:, xb + ri, 0:NCH], zb(2 * ri + 1), UT[:, g1, :], False, True, [tSW, tUT[g1]], [tX, tX2], inc=(ri == 1))
                self.tt("dve", XP, X, COS2, ALU.mult, [tX, tX2, tTB[r]], [tXP])
                self.tt("dve", XQ, swapped(X, 512), SIN2, ALU.mult, [tX, tX2, tTB[r]], [tXQ])
                self.tt("dve", XP, XP, XQ, ALU.add, [tXP, tXQ], [tXP])
                rho = fr(self.RHO[:, l, k, q:q + 1], [[0, NCH]])
                for ri in range(2):
                    if k == 0:
                        self.scan(G[:, ri, :], rho, XP[:, ri, :], 0.0, [tXP, self.tC], [tG])
                    else:
                        rho32 = fr(self.RHO[:, l, k, q:q + 1], [[0, 32]])
                        rho256 = fr(self.RHO[:, l, k, q:q + 1], [[0, 256]])
                        self.scan(G[:, ri, 31::-1], rho32, XP[:, ri, 31::-1], 0.0, [tXP, self.tC], [tG])
                        self.scan(G[:, ri, 287:31:-1], rho256, XP[:, ri, 287:31:-1], G[:, ri, 0:1], [tXP, self.tC, tG], [tG])
                self.tt("pool", XP, G, COS2, ALU.mult, [tG, tTB[r]], [tXP])
                self.tt("pool", XQ, swapped(G, NCH), SIN2, ALU.mult, [tG, tTB[r]], [tXQ])
                self.tt("pool", Hb, XP, XQ, ALU.subtract, [tXP, tXQ], [tH])
            return (SW, tSW, Hk)

        def back(j, qq, ctxt):
            SW, tSW, Hk = ctxt
            UT, tUT = UTs[j % 2], tUTs[j % 2]
            GY, tGY = GYs[j % 2], tGYs[j % 2]
            for gq in range(2):
                g = 2 * qq + gq
                bk, bt = self.bank()
                cpv = lambda k, v: SW[:, 1280 + k * 512 + v * 128:1280 + k * 512 + (v + 1) * 128]
                self.mm(bk[:, 0:NCH], SW[:, gq * 128:(gq + 1) * 128], UT[:, g, :], True, False, [tSW, tUT[g]], [bt], inc=False)
                for ri in range(2):
                    self.mm(bk[:, 1:NCH], cpv(0, 2 * ri + gq), Hk[0][0][:, ri, 0:NCH - 1], False, False, [tSW, Hk[0][1]], [bt], inc=False)
                segs = ((32, 287, 33), (287, 288, 0), (0, 31, 1))
                for si, (o0, o1, i0_) in enumerate(segs):
                    for ri in range(2):
                        lastmm = (si == 2 and ri == 1)
                        self.mm(bk[:, o0:o1], cpv(1, 2 * ri + gq), Hk[1][0][:, ri, i0_:i0_ + (o1 - o0)], False, lastmm, [tSW, Hk[1][1]], [bt], inc=lastmm)
                self.act(GY[:, g, :], bk[:, 0:NCH], AF.Gelu, [bt], [tGY[g]])

        def sel_out(j):
            GY, tGY = GYs[j % 2], tGYs[j % 2]
            for t in range(8):
                bk, bt = self.bank()
                for gp in range(8):
                    self.mm(bk[:, 0:NCH], self.dselb[:, t, 112 - 16 * gp:240 - 16 * gp], GY[:, gp, :], gp == 0, gp == 7, [tGY[gp], self.tC], [bt])
                self.cp("dve", self.GT[:, j, t:NT:8], bk[:, 0:NCH], [bt], [self.tUS[j]] + self.GTb.ts(j, j + 1, 0, NT))

        jobs = [(j, qq) for j in range(4) for qq in range(4)]
        sel_in(0)
        sel_in(1)
        ctx_next = front(*jobs[0])
        for i, (j, qq) in enumerate(jobs):
            ctx_cur = ctx_next
            if i + 1 < len(jobs):
                ctx_next = front(*jobs[i + 1])
            back(j, qq, ctx_cur)
            if qq == 3:
                sel_out(j)
                if j + 2 < 4:
                    sel_in(j + 2)

    def w_out(self, l, bi, last):
        dr = self.dr
        HF = self.HTflat
        MTs = [HF[:, r * 4096:(r + 1) * 4096].bitcast(F32).rearrange("p (m n) -> p m n", m=KT) for r in range(2)]
        WO = HF[:, 8192:16384].rearrange("p (m k c) -> p m k c", m=8, k=KT)
        tWO = T()
        self.dma("sp", WO, dr["wsc_out"][l].rearrange("m p k c -> p m k c"), [], [tWO], self.dsem_wo)
        tMTs = [[T() for _ in range(KT)] for _ in range(2)]
        self._pn2 = [{"sq": T(), "rs": T(), "tmp": [T(), T()], "sqm": [T() for _ in range(KT)]} for _ in range(2)]
        tiles = [(NCTX + 256 * i, NCTX + 256 * (i + 1)) for i in range(8)]
        if not last:
            tiles = [(0, NCTX)] + tiles
        for ti, (a, b) in enumerate(tiles):
            w = b - a
            MT, tMT = MTs[ti % 2], tMTs[ti % 2]
            for m in range(KT):
                bk, bt = self.bank()
                for kt in range(KT):
                    if kt < 2:
                        rhs, rt = self.UA[:, kt, a:b], self.UAb.ts(kt, kt + 1, a, b)
                    elif kt < 6:
                        rhs, rt = self.GT[:, kt - 2, a:b], self.GTb.ts(kt - 2, kt - 1, a, b)
                    else:
                        rhs, rt = self.PP[:, kt - 6, a:b], self.PPb.ts(kt - 6, kt - 5, a, b)
                    self.mm(bk[:, 0:w], WO[:, m, kt, :], rhs, kt == 0, kt == KT - 1, rt + [tWO], [bt])
                self.cp("act", MT[:, m, 0:w], bk[:, 0:w], [bt], [tMT[m]])
            self.postnorm_update(l, a, b, 2, bi, MT, tMT, slot=ti % 2)

    def ffn(self, l, bi, last):
        fw, dr = self.fw, self.dr
        if last:
            groups = [[(256, 768), (768, 1280)], [(1280, 1792), (1792, 2304)]]
        else:
            groups = [[(0, 256), (256, 768)], [(768, 1280), (1280, 1536)], [(1536, 2048), (2048, 2304)]]
        HF = self.HTflat
        MT = HF[:, 0:8192].bitcast(F32).rearrange("p (m n) -> p m n", m=KT)
        tMT = [T() for _ in range(KT)]
        SG = [self.SCb[:, 0:512], self.SCb[:, 512:1024], self.SCb[:, 1024:1536], self.SCb[:, 1536:2048]]
        tSG = [T() for _ in range(4)]
        nsg = 0
        tA = [[T() for _ in range(2)] for _ in range(JT)]
        self._pn = {"sq": T(), "rs": T(), "tmp": [T(), T()], "sqm": [T() for _ in range(KT)], "vf": [T(), T()], "ln": T()}
        for grp in groups:
            g0 = grp[0][0]
            glen = grp[-1][1] - g0
            A = self.PH[:, 0:JT * glen].rearrange("p (j n) -> p j n", j=JT)
            for (a, b) in grp:
                self.prenorm(l, a, b, 3, 4, bi)
            for j in range(JT):
                s = self.wslot()
                W = self.WR[:, s, 0:2048].rearrange("p (s k c) -> p s k c", s=2, k=KT)
                self.dma("sp", W[:, 0], dr["wsc_g"][l, j], [], [self.tWR[s]], self.dWR[s])
                self.dma("sp", W[:, 1], dr["wsc_u"][l, j], [], [self.tWR[s]], self.dWR[s])
                for ci, (a, b) in enumerate(grp):
                    w = b - a
                    hts = self.HTb.ts(0, KT, a, b)
                    bg, tg = self.bank()
                    for kt in range(KT):
                        self.mm(bg[:, 0:w], W[:, 0, kt, :], self.HT[:, kt, a:b], kt == 0, kt == KT - 1, hts + [self.tWR[s]], [tg])
                    bu, tu = self.bank()
                    for kt in range(KT):
                        self.mm(bu[:, 0:w], W[:, 1, kt, :], self.HT[:, kt, a:b], kt == 0, kt == KT - 1, hts + [self.tWR[s]], [tu])
                    r = nsg % 4
                    nsg += 1
                    self.act(SG[r][:, 0:w], bg[:, 0:w], AF.Silu, [tg], [tSG[r]])
                    self.tt("dve", A[:, j, a - g0:b - g0], bu[:, 0:w], SG[r][:, 0:w], ALU.mult, [tu, tSG[r]], [tA[j][ci]])
            for ci, (a, b) in enumerate(grp):
                w = b - a
                for m in range(KT):
                    s = self.wslot()
                    WD = self.WR[:, s, 0:JT * 128].rearrange("p (j c) -> p j c", j=JT)
                    self.dma("sp", WD, dr["wsc_d"][l, m], [], [self.tWR[s]], self.dWR[s])
                    bk, bt = self.bank()
                    for j in range(JT):
                        self.mm(bk[:, 0:w], WD[:, j, :], A[:, j, a - g0:b - g0], j == 0, j == JT - 1, [tA[j][ci], self.tWR[s]], [bt])
                    self.cp("act", MT[:, m, 0:w], bk[:, 0:w], [bt], [tMT[m]])
                self.postnorm_update(l, a, b, 5, bi, MT, tMT)
        fw.barrier()


_CACHE = {}


def _get_nc(nb, dbg=(), nlayers=2, skip=(), stop=None):
    key = (nb, tuple(sorted(dbg)), nlayers, tuple(sorted(skip)), stop)
    if key not in _CACHE:
        g = Gen(nb, dbg, nlayers, skip, stop)
        g.build()
        _CACHE[key] = g
    return _CACHE[key]


def run_cores(inp, batch_lists, dbg=(), nlayers=2, trace=False, skip=(), stop=None):
    nb = len(batch_lists[0])
    g = _get_nc(nb, dbg, nlayers, skip, stop)
    sh = _prep_shared(inp)
    x = np.asarray(inp["x"], np.float32)
    ctx = np.asarray(inp["ctx"], np.float32)
    c = np.asarray(inp["c"], np.float32)
    c_ctx = np.asarray(inp["c_ctx"], np.float32)
    in_maps = []
    for bl in batch_lists:
        m = dict(sh)
        m["x"] = np.ascontiguousarray(x[bl])
        m["ctx"] = np.ascontiguousarray(ctx[bl])
        cc = np.concatenate([c[bl], c_ctx[None]], axis=0)
        m["cT"] = np.ascontiguousarray(cc.reshape(nb + 1, KT, 128).transpose(2, 1, 0))
        in_maps.append(m)
    res = run_bass_kernel_spmd(g.nc, in_maps, core_ids=list(range(len(batch_lists))), trace=trace)
    return res


def kernel(**inputs):
    nb = np.asarray(inputs["x"]).shape[0] // NCORES
    batch_lists = [list(range(c * nb, (c + 1) * nb)) for c in range(NCORES)]
    res = run_cores(inputs, batch_lists)
    out = np.concatenate([np.asarray(r["out"], np.float32) for r in res.results], axis=0)
    return out
```

```python
import math
from contextlib import ExitStack

import numpy as np
import ml_dtypes

import concourse.bass as bass
import concourse.mybir as mybir
from concourse.bass_utils import run_bass_kernel_spmd

F32 = mybir.dt.float32
BF16 = mybir.dt.bfloat16
AF = mybir.ActivationFunctionType
ALU = mybir.AluOpType
AX = mybir.AxisListType

D = 1024
KT = 8
DIN = 1280
DFF = 2816
JT = 22
NCTX = 256
NLAT = 2048
NT = 2304
NCH = 288
NG = 32
NPAIR = 16
EPS = 1e-6
NCORES = 8
TWO_PI = 6.283185307179586
CW1 = 6.28125
CW2 = TWO_PI - 6.28125
MAGIC = 12582912.0
SSMW_COLS = 2304
TAB_COLS = 1152

SEM_CAP = 20000


class T:
    __slots__ = ("w", "r", "x")

    def __init__(self, x=False):
        self.w = None
        self.r = []
        self.x = x


class DmaSem:
    def __init__(self, fw, name):
        self.sem = fw.new_sem(name)
        self.count = 0


class FW:
    ENG = ("pe", "act", "dve", "pool", "sp")

    def __init__(self, nc, stack):
        self.nc = nc
        self.stack = stack
        self.lists = {e: [] for e in self.ENG}
        self.cnt = {e: 0 for e in self.ENG}
        self.sems = {e: [] for e in self.ENG}
        self.known = {e: {} for e in self.ENG}
        self.epoch_known = {e: {} for e in self.ENG}
        self.nops = 0

    def new_sem(self, name):
        return self.stack.enter_context(self.nc.semaphore(name))

    def _eng_sem(self, e, n):
        idx = (n - 1) // SEM_CAP
        while len(self.sems[e]) <= idx:
            self.sems[e].append(self.new_sem(f"s_{e}{len(self.sems[e])}"))
        return ("c", e, idx), self.sems[e][idx], (n - 1) % SEM_CAP + 1

    def _need(self, eng, dep, waits):
        if dep is None:
            return
        if dep[0] == "c":
            _, e, n = dep
            if e == eng and n > self.cnt[e]:
                return
            key, sem, val = self._eng_sem(e, n)
            if self.epoch_known[eng].get(e, -1) > key[2]:
                return
        else:
            _, ds, val = dep
            key, sem = ("d", id(ds)), ds.sem
        if self.known[eng].get(key, 0) >= val:
            return
        cur = waits.get(key)
        if cur is None or cur[1] < val:
            waits[key] = (sem, val)

    def _commit(self, eng, waits):
        for key, (sem, val) in waits.items():
            self.known[eng][key] = val
            if key[0] == "c":
                if self.epoch_known[eng].get(key[1], -1) < key[2]:
                    self.epoch_known[eng][key[1]] = key[2]

    def op(self, eng, fn, reads=(), writes=(), inc=True, dma=None):
        ex = [t for t in reads if t.x]
        if ex:
            reads = [t for t in reads if not t.x]
            writes = list(writes) + ex
        waits = {}
        for t in reads:
            self._need(eng, t.w, waits)
        for t in writes:
            self._need(eng, t.w, waits)
            for r in t.r:
                self._need(eng, r, waits)
        self._commit(eng, waits)
        if dma is not None:
            dma.count += 16
            ticket = ("d", dma, dma.count)
            incinfo = (dma.sem, 16)
        elif inc:
            self.cnt[eng] += 1
            n = self.cnt[eng]
            _, sem, _ = self._eng_sem(eng, n)
            ticket = ("c", eng, n)
            incinfo = (sem, 1)
        else:
            ticket = ("c", eng, self.cnt[eng] + 1)
            incinfo = None
        self.lists[eng].append((list(waits.values()), fn, incinfo))
        self.nops += 1
        for t in reads:
            t.r.append(ticket)
        for t in writes:
            t.w = ticket
            t.r = []
        return ticket

    def barrier(self, dsems=()):
        for e in self.ENG:
            waits = {}
            for e2 in self.ENG:
                if e2 != e and self.cnt[e2] > 0:
                    self._need(e, ("c", e2, self.cnt[e2]), waits)
            for ds in dsems:
                if ds.count:
                    self._need(e, ("d", ds, ds.count), waits)
            self._commit(e, waits)
            if waits:
                self.lists[e].append((list(waits.values()), None, None))

    def wait_dma(self, eng, dsems):
        waits = {}
        for ds in dsems:
            if ds.count:
                self._need(eng, ("d", ds, ds.count), waits)
        self._commit(eng, waits)
        if waits:
            self.lists[eng].append((list(waits.values()), None, None))

    def emit(self):
        nc = self.nc
        lists = self.lists
        self.lists = {e: [] for e in self.ENG}

        def run(h, lst):
            for waits, fn, incinfo in lst:
                for sem, val in waits:
                    h.wait_ge(sem, val)
                if fn is None:
                    continue
                ins = fn(h)
                if incinfo is not None:
                    ins.then_inc(incinfo[0], incinfo[1])

        with nc.Block() as block:
            @block.tensor
            def _(h):
                run(h, lists["pe"])

            @block.scalar
            def _(h):
                run(h, lists["act"])

            @block.vector
            def _(h):
                run(h, lists["dve"])

            @block.gpsimd
            def _(h):
                run(h, lists["pool"])

            @block.sync
            def _(h):
                run(h, lists["sp"])


def fr(ap, dims):
    return bass.AP(ap.tensor, ap.offset, [list(ap.ap[0])] + [list(d) for d in dims])


class Buf:
    def __init__(self, t, J, N, blk):
        self.t, self.J, self.N, self.blk = t, J, N, blk
        self.T = [[T() for _ in range((N + blk - 1) // blk)] for _ in range(J)]

    def ts(self, j0, j1, a, b):
        return [self.T[j][k] for j in range(j0, j1) for k in range(a // self.blk, (b - 1) // self.blk + 1)]


def _pe_tables():
    quarter = D // 4
    omega = (1.0 / (10000.0 ** (np.arange(quarter, dtype=np.float32) / np.float32(quarter)))).astype(np.float32)
    r = np.arange(NLAT // 64, dtype=np.float32)[:, None] * omega
    c = np.arange(64, dtype=np.float32)[:, None] * omega
    er = np.concatenate([np.sin(r), np.cos(r)], axis=-1).astype(np.float32)
    ec = np.concatenate([np.sin(c), np.cos(c)], axis=-1).astype(np.float32)
    erT = np.ascontiguousarray(er.T.reshape(4, 128, 32).transpose(1, 0, 2))
    ecT = np.ascontiguousarray(ec.T.reshape(4, 128, 64).transpose(1, 0, 2))
    return erT, ecT


def _avg_mats():
    wins = (2, 4, 8, 16)

    def amat(n, w):
        t = np.arange(n)
        lo = np.clip(t - w // 2, 0, n)
        hi = np.clip(t - w // 2 + w, 0, n)
        A = np.zeros((n, n), np.float32)
        for i in range(n):
            A[i, lo[i]:hi[i]] = 1.0 / float(hi[i] - lo[i])
        return A - np.eye(n, dtype=np.float32)

    avgL = np.zeros((128, 4, 128), np.float32)
    avgC = np.zeros((128, 4, 2, 2, 128), np.float32)
    for i, w in enumerate(wins):
        a64 = amat(64, w).T
        avgL[0:64, i, 0:64] = a64
        avgL[64:128, i, 64:128] = a64
        a256 = amat(256, w).T
        for st in range(2):
            for tt in range(2):
                avgC[:, i, st, tt, :] = a256[st * 128:(st + 1) * 128, tt * 128:(tt + 1) * 128]
    return avgL, avgC


def _consts():
    c = {}
    c["ident"] = np.eye(128, dtype=np.float32)
    s_i = np.arange(128) // 16
    mf = (s_i[None, :] >= s_i[:, None]).astype(np.float32)
    mb = (s_i[:, None] >= s_i[None, :]).astype(np.float32)
    c["masks"] = np.ascontiguousarray(np.stack([mf, mb], axis=1))
    c["expv"] = np.ascontiguousarray(np.broadcast_to(np.arange(-8, 9, dtype=np.float32), (128, 17)))
    sg = np.ones((128, 2), np.float32)
    sg[0:64, 0] = -1.0
    sg[64:128, 1] = -1.0
    c["sgn"] = sg
    idx = np.zeros((128, 2, NCH), np.float32)
    idx[:, 0, :] = np.arange(NCH, dtype=np.float32)
    idx[:, 1, 0:32] = 256.0 + np.arange(32, dtype=np.float32)
    idx[:, 1, 32:] = np.arange(256, dtype=np.float32)
    c["idx"] = idx
    dsel = np.zeros((128, 8, 240), np.float32)
    for a in range(8):
        for i in range(16):
            dsel[16 * a + i, a, 112 + i] = 1.0
    c["dsel"] = dsel
    c["erT"], c["ecT"] = _pe_tables()
    c["avgL"], c["avgC"] = _avg_mats()
    return c


def _chunkT(v):
    v = np.asarray(v, np.float32)
    lead = v.shape[:-1]
    n = v.shape[-1] // 128
    v = v.reshape(lead + (n, 128))
    v = np.moveaxis(v, -1, 0)
    return np.ascontiguousarray(v)


def _prep_shared(inp):
    f = lambda k: np.asarray(inp[k], np.float32)
    sh = dict(_consts())
    for k in ("w_mod", "w_in", "w_out", "glu_w", "ffn_w_gate", "ffn_w_up", "ffn_w_down"):
        sh[k] = np.ascontiguousarray(f(k))
    sh["bmodT"] = _chunkT(f("b_mod"))
    sh["gains"] = np.ascontiguousarray(np.stack([_chunkT(f(k)) for k in
                                                 ("norm_mix_pre", "norm_mix_post", "norm_ffn_pre", "norm_ffn_post")], axis=2))
    sh["glubT"] = _chunkT(f("glu_b"))
    sh["pscaleT"] = _chunkT(f("pool_scale"))
    sh["sguT"] = np.ascontiguousarray(f("sgu_w").transpose(3, 0, 1, 2))
    sh["sgub"] = np.ascontiguousarray(f("sgu_b").reshape(1, 2 * 4 * 128))
    pw = f("pool_w")
    poolw = np.zeros((128, 2, 2, 128), np.float32)
    for l in range(2):
        for m in range(2):
            for q in range(2):
                poolw[q * 64:(q + 1) * 64, l, m, q * 64:(q + 1) * 64] = pw[l, 2 * m + q]
    sh["poolw"] = poolw
    def pp(a):
        a = a.transpose(3, 0, 1, 2).reshape(64, 2, 64)
        return np.ascontiguousarray(np.concatenate([a, a], axis=0))
    sh["lamre"] = pp(f("ssm_lam_re"))
    sh["lamim"] = pp(f("ssm_lam_im"))
    sh["logdt"] = np.ascontiguousarray(np.broadcast_to(f("ssm_log_dt").reshape(1, 2, 64), (128, 2, 64)))
    bre = f("ssm_b_re").transpose(3, 0, 1, 2, 4).reshape(64, 2, 64, 16)
    bim = f("ssm_b_im").transpose(3, 0, 1, 2, 4).reshape(64, 2, 64, 16)
    sh["b_x"] = np.ascontiguousarray(np.concatenate([bre, bim], axis=0))
    sh["b_y"] = np.ascontiguousarray(np.concatenate([bim, bre], axis=0))
    cre = f("ssm_c_re").transpose(4, 0, 1, 2, 3).reshape(64, 2, 64, 16)
    cim = f("ssm_c_im").transpose(4, 0, 1, 2, 3).reshape(64, 2, 64, 16)
    sh["c_x"] = np.ascontiguousarray(np.concatenate([cre, cim], axis=0))
    sh["c_y"] = np.ascontiguousarray(np.concatenate([cim, cre], axis=0))
    d = f("ssm_d")
    sh["d_rep"] = np.ascontiguousarray(np.broadcast_to(d.transpose(2, 0, 1)[None], (8, 16, 2, 32)).reshape(128, 2, 32))
    return sh


class _Stop(Exception):
    pass


class Gen:
    def _chk(self, name):
        if self.stop == name:
            raise _Stop()

    def __init__(self, nb, dbg=(), nlayers=2, skip=(), stop=None):
        self.skip = set(skip)
        self.stop = stop
        self.nb = nb
        self.dbg = set(dbg)
        self.nlayers = nlayers
        self.nc = bass.Bass("TRN2", target_bir_lowering=False)
        self.dr = {}
        self.dbg_outs = {}

    def din(self, name, shape, dt=F32):
        self.dr[name] = self.nc.dram_tensor(name, list(shape), dt, kind="ExternalInput").ap()

    def dscr(self, name, shape, dt):
        self.dr[name] = self.nc.dram_tensor(name, list(shape), dt, kind="Internal").ap()

    def act(self, out, in_, func, reads, writes, bias=None, scale=None):
        kw = {}
        if bias is not None:
            kw["bias"] = bias
        if scale is not None:
            kw["scale"] = scale
        return self.fw.op("act", lambda h: h.activation(out, in_, func, **kw), reads, writes)

    def tt(self, eng, out, a, b, op, reads, writes):
        return self.fw.op(eng, lambda h: h.tensor_tensor(out, a, b, op), reads, writes)

    def tsc(self, out, in_, s1, s2, op0, op1, reads, writes, eng="dve"):
        if s2 is None:
            return self.fw.op(eng, lambda h: h.tensor_scalar(out, in_, s1, None, op0), reads, writes)
        return self.fw.op(eng, lambda h: h.tensor_scalar(out, in_, s1, s2, op0, op1), reads, writes)

    def stt(self, out, in0, scalar, in1, op0, op1, reads, writes, eng="dve"):
        return self.fw.op(eng, lambda h: h.scalar_tensor_tensor(out, in0, scalar, in1, op0, op1), reads, writes)

    def cp(self, eng, out, in_, reads, writes):
        if eng == "act":
            return self.fw.op("act", lambda h: h.activation(out, in_, AF.Copy), reads, writes)
        return self.fw.op(eng, lambda h: h.tensor_copy(out, in_), reads, writes)

    def dma(self, q, out, in_, reads, writes, sem):
        return self.fw.op(q, lambda h: h.dma_start(out=out, in_=in_), reads, writes, dma=sem)

    def mm(self, out, lhsT, rhs, start, stop, reads, writes, inc=None):
        if inc is None:
            inc = stop
        return self.fw.op("pe", lambda h: h.matmul(out, lhsT, rhs, start=start, stop=stop), reads, writes, inc=inc)

    def tr(self, out, in_, ident, reads, writes):
        return self.fw.op("pe", lambda h: h.transpose(out, in_, ident), reads, writes)

    def scan(self, out, d0, d1, init, reads, writes, eng="dve"):
        return self.fw.op(eng, lambda h: h.tensor_tensor_scan(out, d0, d1, init, ALU.mult, ALU.add), reads, writes)

    def rr(self, out, in_, kk, reads, writes, tk):
        self.tsc(kk, in_, 1.0 / TWO_PI, MAGIC, ALU.mult, ALU.add, reads, [tk])
        self.tsc(kk, kk, MAGIC, None, ALU.subtract, None, [tk], [tk])
        self.stt(out, kk, -CW1, in_, ALU.mult, ALU.add, [tk] + list(reads), writes)
        self.stt(out, kk, -CW2, out, ALU.mult, ALU.add, [tk] + list(writes), writes)
        self.tsc(out, out, -3.1415925, 3.1415925, ALU.max, ALU.min, list(writes), writes)

    def dump(self, name, ap, reads, dt=F32):
        if name not in self.dbg:
            return
        o = self.nc.dram_tensor("dbg_" + name, list(ap.shape), dt, kind="ExternalOutput").ap()
        self.dbg_outs[name] = o
        self.dma("sp", o, ap, reads, [], self.dsem_dbg)

    def build(self):
        nc, nb = self.nc, self.nb
        din = self.din
        for name, shape in (("ident", (128, 128)), ("masks", (128, 2, 128)), ("expv", (128, 17)), ("sgn", (128, 2)),
                            ("idx", (128, 2, NCH)), ("dsel", (128, 8, 240)), ("erT", (128, 4, 32)), ("ecT", (128, 4, 64)),
                            ("avgL", (128, 4, 128)), ("avgC", (128, 4, 2, 2, 128)),
                            ("w_mod", (2, D, 6 * D)), ("w_in", (2, D, DIN)), ("w_out", (2, D, D)), ("glu_w", (2, 512, 512)),
                            ("ffn_w_gate", (2, D, DFF)), ("ffn_w_up", (2, D, DFF)), ("ffn_w_down", (2, DFF, D)),
                            ("bmodT", (128, 2, 48)), ("gains", (128, 2, 4, 8)), ("glubT", (128, 2, 4)), ("pscaleT", (128, 2, 2)),
                            ("sguT", (128, 2, 4, 128)), ("sgub", (1, 1024)), ("poolw", (128, 2, 2, 128)),
                            ("lamre", (128, 2, 64)), ("lamim", (128, 2, 64)), ("logdt", (128, 2, 64)),
                            ("b_x", (128, 2, 64, 16)), ("b_y", (128, 2, 64, 16)), ("c_x", (128, 2, 64, 16)), ("c_y", (128, 2, 64, 16)),
                            ("d_rep", (128, 2, 32)),
                            ("x", (nb, NLAT, D)), ("ctx", (nb, NCTX, D)), ("cT", (128, KT, nb + 1))):
            din(name, shape)
        self.dr["out"] = nc.dram_tensor("out", [nb, NLAT, D], F32, kind="ExternalOutput").ap()
        self.dscr("wsc_in", (2, 10, 128, KT, 128), BF16)
        self.dscr("wsc_out", (2, 8, 128, KT, 128), BF16)
        self.dscr("wsc_glu", (2, 128, 4, 512), BF16)
        self.dscr("wsc_g", (2, JT, 128, KT, 128), BF16)
        self.dscr("wsc_u", (2, JT, 128, KT, 128), BF16)
        self.dscr("wsc_d", (2, 8, 128, JT, 128), BF16)
        self.dscr("ssmw", (2, NPAIR, 128, SSMW_COLS), BF16)
        self.dscr("tab", (2, 2, NPAIR, 128, TAB_COLS), F32)

        with ExitStack() as st:
            self.st = st
            fw = self.fw = FW(nc, st)
            self.dsem_dbg = DmaSem(fw, "dbg")
            self.alloc_persistent()
            self.preamble()
            fw.barrier([self.dsem_w, self.dsem_c, self.dsem_c2, self.dsem_s])
            self.dump("ssmw", self.dr["ssmw"], [], BF16)
            self.dump("tab", self.dr["tab"], [], F32)
            fw.emit()
            if "main" not in self.skip:
                self.main()
            fw.wait_dma("sp", [self.dsem_dbg, self.dsem_w, self.dsem_s, self.dsem_c, self.dsem_c2])
            fw.emit()
        return nc

    def SB(self, st, name, shape, dt=F32):
        return st.enter_context(self.nc.sbuf_tensor("sb_" + name, list(shape), dt))

    def alloc_persistent(self):
        st, nb = self.st, self.nb
        SB = lambda n, s, d=F32: self.SB(st, n, s, d)
        self.ident = SB("ident", (128, 128))
        self.ones = SB("ones", (128, 128), BF16)
        self.dselb = SB("dselb", (128, 8, 240), BF16)
        self.erT = SB("erT", (128, 4, 32))
        self.ecT = SB("ecT", (128, 4, 64))
        self.avgLb = SB("avgLb", (128, 4, 128), BF16)
        self.avgCb = SB("avgCb", (128, 4, 2, 2, 128), BF16)
        self.sguTb = SB("sguTb", (128, 2, 4, 128), BF16)
        self.sgubb = SB("sgubb", (1, 1024), BF16)
        self.poolwb = SB("poolwb", (128, 2, 2, 128), BF16)
        self.glub = SB("glub", (128, 2, 4))
        self.pscale = SB("pscale", (128, 2, 2))
        self.PAR = SB("PAR", (128, 2, 6, KT, nb + 1))
        self.RHO = SB("RHO", (128, 2, 2, NPAIR))
        self.tC = T()
        self.dsem_w = DmaSem(self.fw, "dw")
        self.dsem_c = DmaSem(self.fw, "dc")
        self.dsem_c2 = DmaSem(self.fw, "dc2")
        self.dsem_s = DmaSem(self.fw, "ds")

    def preamble(self):
        nc, fw, dr, nb = self.nc, self.fw, self.dr, self.nb
        tC = self.tC
        for dst, src in ((self.ident, "ident"), (self.erT, "erT"), (self.ecT, "ecT"), (self.glub, "glubT"), (self.pscale, "pscaleT")):
            self.dma("sp", dst[:], dr[src], [], [tC], self.dsem_c)
        for dst, src in ((self.dselb, "dsel"), (self.avgLb, "avgL"), (self.avgCb, "avgC"), (self.sguTb, "sguT"), (self.sgubb, "sgub"), (self.poolwb, "poolw")):
            self.dma("pool", dst[:], dr[src], [], [tC], self.dsem_c2)
        fw.op("dve", lambda h: h.memset(self.ones[:], 1.0), [], [tC])
        fw.wait_dma("dve", [self.dsem_c, self.dsem_c2])
        fw.wait_dma("act", [self.dsem_c, self.dsem_c2])
        fw.wait_dma("pe", [self.dsem_c, self.dsem_c2])
        fw.wait_dma("pool", [self.dsem_c, self.dsem_c2])

        with ExitStack() as ps_st:
            self.pps = ps_st.enter_context(nc.psum_tensor("pps", [128, 8, 512], F32))
            if "mod" not in self.skip:
                self.pre_mod()
            else:
                self.weight_casts()
            if "ssmgen" not in self.skip:
                self.pre_ssm()
            fw.barrier([self.dsem_w, self.dsem_c, self.dsem_c2, self.dsem_s])
            fw.emit()

    def weight_casts(self):
        dr = self.dr
        for l in range(0 if "wcast" not in self.skip else 2, 2):
            for s in range(10):
                self.dma("pool", dr["wsc_in"][l, s], dr["w_in"][l, :, s * 128:(s + 1) * 128].rearrange("(kt p) c -> p kt c", p=128), [], [], self.dsem_w)
            for m in range(8):
                self.dma("pool", dr["wsc_out"][l, m], dr["w_out"][l, :, m * 128:(m + 1) * 128].rearrange("(kt p) c -> p kt c", p=128), [], [], self.dsem_w)
                self.dma("pool", dr["wsc_d"][l, m], dr["ffn_w_down"][l, :, m * 128:(m + 1) * 128].rearrange("(j p) c -> p j c", p=128), [], [], self.dsem_w)
            self.dma("pool", dr["wsc_glu"][l], dr["glu_w"][l].rearrange("(kt p) c -> p kt c", p=128), [], [], self.dsem_w)
            for j in range(JT):
                self.dma("pool", dr["wsc_g"][l, j], dr["ffn_w_gate"][l, :, j * 128:(j + 1) * 128].rearrange("(kt p) c -> p kt c", p=128), [], [], self.dsem_w)
                self.dma("pool", dr["wsc_u"][l, j], dr["ffn_w_up"][l, :, j * 128:(j + 1) * 128].rearrange("(kt p) c -> p kt c", p=128), [], [], self.dsem_w)

    def pre_mod(self):
        nc, fw, dr, nb = self.nc, self.fw, self.dr, self.nb
        tC = self.tC
        with ExitStack() as st:
            SB = lambda n, s, d=F32: self.SB(st, n, s, d)
            cT = SB("cT_sb", (128, KT, nb + 1))
            cs = SB("cs_sb", (128, KT, nb + 1), BF16)
            bmod = SB("bmod_sb", (128, 2, 48))
            gains = SB("gains_sb", (128, 2, 4, KT))
            modT = SB("modT_sb", (128, 2, 48, nb + 1))
            tmp = SB("modtmp", (128, KT, nb + 1))
            ring = [SB(f"wmod{i}", (128, KT, 512), BF16) for i in range(2)]
            rT = [T(), T()]
            rS = [DmaSem(fw, f"wm{i}") for i in range(2)]
            t0, tm = T(), T()
            dl = DmaSem(fw, "modld")
            self.dma("sp", cT[:], dr["cT"], [], [t0], dl)
            self.dma("sp", bmod[:], dr["bmodT"], [], [t0], dl)
            self.dma("sp", gains[:], dr["gains"], [], [t0], dl)
            self.act(cs[:], cT[:], AF.Silu, [t0], [t0])
            tps = T(True)
            n = 0
            for l in range(2):
                for s in range(12):
                    r = n % 2
                    n += 1
                    self.dma("pool", ring[r][:], dr["w_mod"][l, :, s * 512:(s + 1) * 512].rearrange("(kt p) c -> p kt c", p=128), [], [rT[r]], rS[r])
                    for f in range(4):
                        t_idx = s * 4 + f
                        for kt in range(KT):
                            self.mm(self.pps[:, l, t_idx * 8:t_idx * 8 + nb + 1], ring[r][:, kt, f * 128:(f + 1) * 128], cs[:, kt, :],
                                    kt == 0, kt == KT - 1, [rT[r], t0], [tps])
                self.tt("dve", modT[:, l], fr(self.pps[:, l, 0:1], [[8, 48], [1, nb + 1]]), fr(bmod[:, l, 0:1], [[1, 48], [0, nb + 1]]),
                        ALU.add, [tps, t0], [tm])
            PAR = self.PAR
            for l in range(2):
                def gb(kind):
                    return fr(gains[:, l, kind, 0:1], [[1, KT], [0, nb + 1]])
                self.tsc(tmp[:], modT[:, l, 8:16, :], 1.0, None, ALU.add, None, [tm], [tm])
                self.tt("dve", PAR[:, l, 0], tmp[:], gb(0), ALU.mult, [tm, t0], [tC])
                self.cp("dve", PAR[:, l, 1], modT[:, l, 0:8, :], [tm], [tC])
                self.tt("dve", PAR[:, l, 2], modT[:, l, 16:24, :], gb(1), ALU.mult, [tm, t0], [tC])
                self.tsc(tmp[:], modT[:, l, 32:40, :], 1.0, None, ALU.add, None, [tm], [tm])
                self.tt("dve", PAR[:, l, 3], tmp[:], gb(2), ALU.mult, [tm, t0], [tC])
                self.cp("dve", PAR[:, l, 4], modT[:, l, 24:32, :], [tm], [tC])
                self.tt("dve", PAR[:, l, 5], modT[:, l, 40:48, :], gb(3), ALU.mult, [tm, t0], [tC])
            self.dump("par", PAR[:], [tC])
            self.weight_casts()
            fw.barrier()
            fw.emit()

    def pre_ssm(self):
        nc, fw, dr, nb = self.nc, self.fw, self.dr, self.nb
        tC = self.tC
        PI = math.pi
        with ExitStack() as st:
            SB = lambda n, s, d=F32: self.SB(st, n, s, d)
            BIG = [SB(f"big{i}", (128, 4608)) for i in range(5)]
            tB = [T() for _ in range(5)]
            MTMP = SB("mtmp", (128, 32, 128)); tMT = T()
            STG = SB("stg", (128, NPAIR, 4, 128), BF16); tSTG = T()
            MST = SB("mst", (128, NPAIR, 2, 128), BF16); tMST = T()
            lamre = SB("lamre_sb", (128, 2, 64)); lamim = SB("lamim_sb", (128, 2, 64)); logdt = SB("logdt_sb", (128, 2, 64))
            masks = SB("masks_sb", (128, 2, 128)); expv = SB("expv_sb", (128, 17)); sgn = SB("sgn_sb", (128, 2))
            idx = SB("idx_sb", (128, 2, NCH)); drep = SB("drep_sb", (128, 2, 32))
            tI = T()
            dl = DmaSem(fw, "ssmld")
            for dst, src in ((lamre, "lamre"), (lamim, "lamim"), (logdt, "logdt"), (masks, "masks"), (expv, "expv"),
                             (sgn, "sgn"), (idx, "idx"), (drep, "d_rep")):
                self.dma("sp", dst[:], dr[src], [], [tI], dl)
            fw.wait_dma("dve", [dl]); fw.wait_dma("act", [dl]); fw.wait_dma("pe", [dl])
            DT = SB("dt_sb", (128, 64)); A_ = SB("a_sb", (128, 64)); TH = SB("th_sb", (128, 64))
            AE = SB("ae_sb", (128, 64, 17)); TE = SB("te_sb", (128, 64, 17))
            CS = SB("cs2_sb", (128, 64, 17)); LY = SB("ly_sb", (128, 64, 17))
            KK = LY
            tL = T(); tK = tL
            sm = [SB(f"sm{i}", (128, 64)) for i in range(8)]
            tS = T()
            bx = SB("bx_sb", (128, 64, 16)); by = SB("by_sb", (128, 64, 16))
            cx = SB("cx_sb", (128, 64, 16)); cy = SB("cy_sb", (128, 64, 16))
            BX = SB("BX_sb", (128, 64, 16)); BY = SB("BY_sb", (128, 64, 16))
            tBC = T()
            PHQ = SB("phq_sb", (128, 2, NPAIR)); tPH = T()
            dbc = DmaSem(fw, "bcld")
            STG2, tSTG2 = STG, tSTG
            sgn_m = sgn[:, 0:1]
            sgn_pm = sgn[:, 1:2]
            pq = [0]

            def psq():
                i = pq[0] % 6
                pq[0] += 1
                if not hasattr(self, "_pqT"):
                    self._pqT = [T(True) for _ in range(6)]
                return self.pps[:, 2 + i, 0:128], self._pqT[i]

            try:
              for l in range(2):
                lr = lamre[:, l, :]; li = lamim[:, l, :]
                self.dma("sp", bx[:], dr["b_x"][:, l], [], [tBC], dbc)
                self.dma("sp", by[:], dr["b_y"][:, l], [], [tBC], dbc)
                self.dma("sp", cx[:], dr["c_x"][:, l], [], [tBC], dbc)
                self.dma("sp", cy[:], dr["c_y"][:, l], [], [tBC], dbc)
                self.act(DT[:], logdt[:, l, :], AF.Exp, [tI, tS], [tS])
                self.tt("dve", A_[:], lr, DT[:], ALU.mult, [tI, tS], [tS])
                self.tt("dve", TH[:], li, DT[:], ALU.mult, [tI, tS], [tS])
                a_bc = fr(A_[:, 0:1], [[1, 64], [0, 17]])
                th_bc = fr(TH[:, 0:1], [[1, 64], [0, 17]])
                e_bc = fr(expv[:, 0:1], [[0, 64], [1, 17]])
                self.tt("dve", AE[:], a_bc, e_bc, ALU.mult, [tS, tI], [tL])
                self.act(AE[:], AE[:], AF.Exp, [tL], [tL])
                self.tt("dve", TE[:], th_bc, e_bc, ALU.mult, [tS, tI], [tL])
                self.rr(TE[:], TE[:], KK[:], [tL], [tL], tK)
                self.tsc(CS[:], TE[:], PI / 2, None, ALU.add, None, [tL], [tL])
                self.rr(CS[:], CS[:], KK[:], [tL], [tL], tK)
                self.act(TE[:], TE[:], AF.Sin, [tL], [tL])
                self.act(CS[:], CS[:], AF.Sin, [tL], [tL])
                self.tt("dve", CS[:], CS[:], AE[:], ALU.mult, [tL], [tL])
                self.tt("dve", TE[:], TE[:], AE[:], ALU.mult, [tL], [tL])
                self.tsc(LY[:], TE[:], sgn_m, None, ALU.mult, None, [tL, tI], [tL])
                LX = CS
                LI = TE
                self.dump("LX", LX[:], [tL]); self.dump("LI", LI[:], [tL])
                self._chk("ssmA")
                lbr = fr(LX[:, 0:1, 9:10], [[17, 64]])
                lbi = fr(LI[:, 0:1, 9:10], [[17, 64]])
                xr, den, t1, t2, br_, bi_, bis, e8 = [s_[:] for s_ in sm]
                self.tsc(xr, lbr, -1.0, None, ALU.add, None, [tL], [tS])
                self.tt("dve", den, lr, lr, ALU.mult, [tI], [tS])
                self.tt("dve", t1, li, li, ALU.mult, [tI], [tS])
                self.tt("dve", den, den, t1, ALU.add, [tS], [tS])
                fw.op("dve", lambda h: h.reciprocal(den, den), [tS], [tS])
                self.tt("dve", t1, xr, lr, ALU.mult, [tS, tI], [tS])
                self.tt("dve", t2, lbi, li, ALU.mult, [tL, tI], [tS])
                self.tt("dve", t1, t1, t2, ALU.add, [tS], [tS])
                self.tt("dve", br_, t1, den, ALU.mult, [tS], [tS])
                self.tt("dve", t1, lbi, lr, ALU.mult, [tL, tI], [tS])
                self.tt("dve", t2, xr, li, ALU.mult, [tS, tI], [tS])
                self.tt("dve", t1, t1, t2, ALU.subtract, [tS], [tS])
                self.tt("dve", bi_, t1, den, ALU.mult, [tS], [tS])
                self.tsc(bis, bi_, sgn_m, None, ALU.mult, None, [tS, tI], [tS])
                br_bc = fr(br_[:, 0:1], [[1, 64], [0, 16]])
                bis_bc = fr(bis[:, 0:1], [[1, 64], [0, 16]])
                w1 = fr(BIG[0][:, 0:1], [[16, 64], [1, 16]])
                w2 = fr(BIG[1][:, 0:1], [[16, 64], [1, 16]])
                fw.wait_dma("dve", [dbc])
                self.tt("dve", w1, bx[:], br_bc, ALU.mult, [tBC, tS], [tB[0]])
                self.tt("dve", w2, by[:], bis_bc, ALU.mult, [tBC, tS], [tB[1]])
                self.tt("dve", BX[:], w1, w2, ALU.add, [tB[0], tB[1]], [tBC])
                self.tt("dve", w1, by[:], br_bc, ALU.mult, [tBC, tS], [tB[0]])
                self.tt("dve", w2, bx[:], bis_bc, ALU.mult, [tBC, tS], [tB[1]])
                self.tt("dve", BY[:], w1, w2, ALU.subtract, [tB[0], tB[1]], [tBC])
                self.tsc(cx[:], cx[:], sgn_pm, None, ALU.mult, None, [tBC, tI], [tBC])
                self.tsc(cy[:], cy[:], sgn_pm, None, ALU.mult, None, [tBC, tI], [tBC])
                self.act(e8, A_[:], AF.Exp, [tS], [tS], scale=8.0)
                for gq in range(2):
                    self.cp("dve", self.RHO[gq * 64:(gq + 1) * 64, l], fr(e8[gq * 64:(gq + 1) * 64, gq:gq + 1], [[32, 2], [2, NPAIR]]), [tS], [tC])
                self.tsc(t1, TH[:], 8.0, None, ALU.mult, None, [tS], [tS])
                self.rr(t1, t1, t2, [tS], [tS], tS)
                for gq in range(2):
                    self.cp("dve", PHQ[gq * 64:(gq + 1) * 64], fr(t1[gq * 64:(gq + 1) * 64, gq:gq + 1], [[32, 2], [2, NPAIR]]), [tS], [tPH])

                self.dump("BX", BX[:], [tBC]); self.dump("cx", cx[:], [tBC]); self.dump("RHO", self.RHO[:], [tC]); self.dump("PHQ", PHQ[:], [tPH])
                self._chk("ssmB")
                for k in range(2):
                    if k == 0:
                        iB, sB, iBp, sBp, iC, sC = 15, -1, 7, -1, 9, 1
                    else:
                        iB, sB, iBp, sBp, iC, sC = 8, 1, 0, 1, 16, -1

                    def build(dst, i0, stp, X, Y, tsrc):
                        lx = fr(LX[:, k * 32:k * 32 + 1, i0:i0 + 1], [[17, 32], [stp, 8], [0, 16]])
                        ly = fr(LY[:, k * 32:k * 32 + 1, i0:i0 + 1], [[17, 32], [stp, 8], [0, 16]])
                        xv = fr(X[:, k * 32:k * 32 + 1, 0:1], [[16, 32], [0, 8], [1, 16]])
                        yv = fr(Y[:, k * 32:k * 32 + 1, 0:1], [[16, 32], [0, 8], [1, 16]])
                        o4 = lambda b: fr(b[:, 0:1], [[128, 32], [16, 8], [1, 16]])
                        self.tt("dve", o4(BIG[0]), lx, xv, ALU.mult, [tL, tsrc], [tB[0]])
                        self.tt("dve", o4(BIG[1]), ly, yv, ALU.mult, [tL, tsrc], [tB[1]])
                        self.tt("dve", o4(BIG[dst]), o4(BIG[0]), o4(BIG[1]), ALU.add, [tB[0], tB[1]], [tB[dst]])

                    build(2, iB, sB, BX, BY, tBC)
                    build(3, iBp, sBp, BX, BY, tBC)
                    build(4, iC, sC, cx, cy, tBC)
                    Bt, Bpt, Ct = BIG[2], BIG[3], BIG[4]
                    self.dump("Bt", Bt[:], [tB[2]]); self.dump("Ct", Ct[:], [tB[4]])
                    self._chk("ssmC")
                    stg = STG
                    fw.op("dve", lambda h: h.memset(STG[:], 0.0), [], [tSTG])
                    for g in range(NG):
                        q, gq = g // 2, g % 2
                        pa, pt = psq()
                        self.tr(pa, Bt[:, g * 128:(g + 1) * 128], self.ident[:], [tB[2], tC], [pt])
                        self.cp("act", stg[:, q, gq, gq * 64:(gq + 1) * 64], pa[:, 0:64], [pt], [tSTG])
                        self.cp("act", stg[:, q, 2 + gq, gq * 64:(gq + 1) * 64], pa[:, 64:128], [pt], [tSTG])
                        pm, pmt = psq()
                        self.mm(pm, Bpt[:, g * 128:(g + 1) * 128], Ct[:, g * 128:(g + 1) * 128], True, True, [tB[3], tB[4]], [pmt])
                        if k == 0:
                            self.tt("dve", MTMP[:, g, :], pm, masks[:, 0, :], ALU.mult, [pmt, tI], [tMT])
                            self.stt(MTMP[:, g, :], self.ident[:], drep[:, l, g:g + 1], MTMP[:, g, :], ALU.mult, ALU.add, [tC, tI, tMT], [tMT])
                        else:
                            self.tt("dve", BIG[0][:, 0:128], pm, masks[:, 1, :], ALU.mult, [pmt, tI], [tB[0]])
                            self.tt("dve", MST[:, q, gq, :], BIG[0][:, 0:128], MTMP[:, g, :], ALU.add, [tB[0], tMT], [tMST])
                        if g == 0:
                            self._chk("ssmD1")
                    self._chk("ssmD2")
                    c0 = 256 + k * 512
                    dview = dr["ssmw"][l].rearrange("q p c -> p q c")
                    self.dma("sp", dview[:, :, c0:c0 + 512], stg[:].rearrange("p q v c -> p q (v c)"), [tSTG], [], self.dsem_s)
                    self._chk("ssmD")
                    fw.op("dve", lambda h: h.memset(STG[:], 0.0), [], [tSTG])
                    for v, (r0, gq) in enumerate(((0, 0), (0, 1), (64, 0), (64, 1))):
                        ro = 64 * gq
                        src = fr(Ct[r0:r0 + 64, gq * 128:gq * 128 + 1], [[256, NPAIR], [1, 128]])
                        self.cp("dve", STG2[ro:ro + 64, :, v, :], src, [tB[4]], [tSTG2])
                    c0 = 1280 + k * 512
                    self.dma("sp", dview[:, :, c0:c0 + 512], STG2[:].rearrange("p q v c -> p q (v c)"), [tSTG2], [], self.dsem_s)
                    self._chk("ssmE")
                    ARG, K2, SINT, COST, NSIN = [fr(b[:, 0:1], [[NCH, NPAIR], [1, NCH]]) for b in BIG]
                    ph_bc = fr(PHQ[:, k, 0:1], [[1, NPAIR], [0, NCH]])
                    ix_bc = fr(idx[:, k, 0:1], [[0, NPAIR], [1, NCH]])
                    self.tt("dve", ARG, ph_bc, ix_bc, ALU.mult, [tPH, tI], [tB[0]])
                    self.rr(ARG, ARG, K2, [tB[0]], [tB[0]], tB[1])
                    self.tsc(COST, ARG, PI / 2, None, ALU.add, None, [tB[0]], [tB[3]])
                    self.rr(COST, COST, K2, [tB[3]], [tB[3]], tB[1])
                    self.act(SINT, ARG, AF.Sin, [tB[0]], [tB[2]])
                    self.act(COST, COST, AF.Sin, [tB[3]], [tB[3]])
                    self.tsc(NSIN, SINT, -1.0, None, ALU.mult, None, [tB[2]], [tB[4]])
                    tview = dr["tab"][l, k].rearrange("q p c -> p q c")
                    s_re, s_im = (SINT, NSIN) if k == 0 else (NSIN, SINT)
                    t_re, t_im = (tB[2], tB[4]) if k == 0 else (tB[4], tB[2])
                    self.dma("sp", tview[:, :, 0:NCH], COST, [tB[3]], [], self.dsem_s)
                    self.dma("sp", tview[:, :, NCH:2 * NCH], COST, [tB[3]], [], self.dsem_s)
                    self.dma("sp", tview[:, :, 2 * NCH:3 * NCH], s_re, [t_re], [], self.dsem_s)
                    self.dma("sp", tview[:, :, 3 * NCH:4 * NCH], s_im, [t_im], [], self.dsem_s)
                dview = dr["ssmw"][l].rearrange("q p c -> p q c")
                self.dma("sp", dview[:, :, 0:256], MST[:].rearrange("p q g c -> p q (g c)"), [tMST], [], self.dsem_s)
            except _Stop:
                pass
            fw.barrier([self.dsem_s])
            fw.emit()

    def main(self):
        nc, fw, dr, nb, st = self.nc, self.fw, self.dr, self.nb, self.st
        SB = lambda n, s, d=F32: self.SB(st, n, s, d)
        self.XT = SB("XT", (128, KT, NT))
        self.XTb = Buf(self.XT, KT, NT, 128)
        self.HT = SB("HT", (128, KT, NT), BF16)
        self.HTb = Buf(self.HT, KT, NT, 128)
        self.HTflat = self.HT[:].rearrange("p a b -> p (a b)")
        self.PH = SB("PH", (128, 23040), BF16)
        self.SCf = SB("SCf", (128, 4096))
        self.SCb = SB("SCb", (128, 4096), BF16)
        self.WR = SB("WR", (128, 2, 2816), BF16)
        self.tWR = [T(), T()]
        self.dWR = [DmaSem(fw, "wr0"), DmaSem(fw, "wr1")]
        self.wn = 0
        self.ps = st.enter_context(nc.psum_tensor("ps", [128, 8, 512], F32))
        self.psT = [T(True) for _ in range(8)]
        self.psn = 0
        self.psqn = 0
        self.psxn = 0
        self.ps_banks = list(range(8))
        PH = self.PH
        self.UA = PH[:, 0:4608].rearrange("p (m n) -> p m n", m=2)
        self.UAb = Buf(self.UA, 2, NT, 128)
        self.V = PH[:, 4608:9216].rearrange("p (c f) -> p c f", c=18)
        self.tV = [T() for _ in range(18)]
        self.US = PH[:, 9216:18432].rearrange("p (m s c) -> p m s c", m=4, s=8)
        self.GT = PH[:, 9216:18432].rearrange("p (m n) -> p m n", m=4)
        self.tUS = [T() for _ in range(4)]
        self.GTb = Buf(self.GT, 4, NT, 128)
        self.PP = PH[:, 18432:23040].rearrange("p (m n) -> p m n", m=2)
        self.PPb = Buf(self.PP, 2, NT, 128)
        self.dsem_x = [DmaSem(fw, f"x{i}") for i in range(4)]
        self.dsem_o = [DmaSem(fw, f"o{i}") for i in range(4)]
        self.dsem_t = [DmaSem(fw, "t0"), DmaSem(fw, "t1")]
        self.dsem_wo = DmaSem(fw, "wo")
        self.tIO = [T() for _ in range(4)]
        fw.wait_dma("sp", [self.dsem_w, self.dsem_s, self.dsem_c, self.dsem_c2])
        for bi in range(nb):
            self.load_x(bi)
            if self.stop == "load":
                break
            for l in range(self.nlayers):
                self.layer(l, bi)
                if self.stop is not None:
                    break
            if self.stop is not None:
                break
            self.store_out(bi)
            fw.barrier()
        fw.wait_dma("sp", self.dsem_o + [self.dsem_dbg])

    def bank(self):
        b = self.ps_banks[self.psn % len(self.ps_banks)]
        self.psn += 1
        return self.ps[:, b, :], self.psT[b]

    def bank67(self):
        b = 4 + self.psqn % 4
        self.psqn += 1
        return self.ps[:, b, :], self.psT[b]

    def wslot(self):
        s = self.wn % 2
        self.wn += 1
        return s

    def load_x(self, bi):
        fw, dr = self.fw, self.dr
        self.ps_banks = list(range(8))
        XIN = [self.SCf[:, k * 1024:(k + 1) * 1024] for k in range(4)]
        tX = self.tIO
        for i in range(18):
            s = i % 4
            src = dr["ctx"][bi, i * 128:(i + 1) * 128, :] if i < 2 else dr["x"][bi, (i - 2) * 128:(i - 1) * 128, :]
            self.dma("sp", XIN[s], src, [], [tX[s]], self.dsem_x[s])
            c0 = i * 128
            for half in range(2):
                bk, bt = self.bank()
                for jj in range(4):
                    j = half * 4 + jj
                    fw.op("pe", (lambda o, a: (lambda h: h.transpose(o, a, self.ident[:])))(bk[:, jj * 128:(jj + 1) * 128], XIN[s][:, j * 128:(j + 1) * 128]),
                          [tX[s], self.tC], [bt], inc=(jj == 3))
                wr = self.XTb.ts(half * 4, half * 4 + 4, c0, c0 + 128)
                if i < 2:
                    self.cp("dve", self.XT[:, half * 4:half * 4 + 4, c0:c0 + 128], bk.rearrange("p (a b) -> p a b", a=4), [bt], wr)
                else:
                    r0 = 2 * (i - 2)
                    o4 = fr(self.XT[:, half * 4:half * 4 + 1, c0:c0 + 1], [[NT, 4], [64, 2], [1, 64]])
                    i4 = fr(bk[:, 0:1], [[128, 4], [64, 2], [1, 64]])
                    if half == 0:
                        pe = fr(self.erT[:, 0:1, r0:r0 + 1], [[32, 4], [1, 2], [0, 64]])
                    else:
                        pe = fr(self.ecT[:, 0:1, 0:1], [[64, 4], [0, 2], [1, 64]])
                    self.tt("dve", o4, i4, pe, ALU.add, [bt, self.tC], wr)
        self.dump(f"xt0_{bi}", self.XT[:], self.XTb.ts(0, KT, 0, NT))

    def store_out(self, bi):
        fw, dr = self.fw, self.dr
        self.ps_banks = list(range(8))
        OS = [self.SCf[:, k * 1024:(k + 1) * 1024] for k in range(4)]
        tO = self.tIO
        for i in range(16):
            s = i % 4
            c0 = NCTX + i * 128
            for half in range(2):
                bk, bt = self.bank()
                for jj in range(4):
                    j = half * 4 + jj
                    fw.op("pe", (lambda o, a: (lambda h: h.transpose(o, a, self.ident[:])))(bk[:, jj * 128:(jj + 1) * 128], self.XT[:, j, c0:c0 + 128]),
                          self.XTb.ts(j, j + 1, c0, c0 + 128) + [self.tC], [bt], inc=(jj == 3))
                self.cp("act" if half == 0 else "dve", OS[s][:, half * 512:(half + 1) * 512], bk, [bt], [tO[s]])
            self.dma("sp", dr["out"][bi, i * 128:(i + 1) * 128, :], OS[s], [tO[s]], [], self.dsem_o[s])

    def rstd_from_bank(self, bk, bt, w, RS, tRS):
        self.act(RS[:, 0:w], bk[:, 0:w], AF.Sqrt, [bt], [tRS], bias=EPS, scale=1.0 / D)
        self.fw.op("dve", lambda h: h.reciprocal(RS[:, 0:w], RS[:, 0:w]), [tRS], [tRS])

    def prenorm(self, l, a, b, qa, qb, bi):
        w = b - a
        pb = self.nb if a < NCTX else bi
        SQ = self.SCb[:, 0:4096].rearrange("p (j n) -> p j n", j=KT)
        RS = self.SCf[:, 0:512]
        TMP = [self.SCf[:, 512:1024], self.SCf[:, 1024:1536]]
        st = self._pn
        self.act(SQ[:, :, 0:w], self.XT[:, :, a:b], AF.Square, self.XTb.ts(0, KT, a, b), [st["sq"]])
        bk, bt = self.bank()
        for j in range(KT):
            self.mm(bk[:, 0:w], self.ones[:], SQ[:, j, 0:w], j == 0, j == KT - 1, [st["sq"], self.tC], [bt])
        self.rstd_from_bank(bk, bt, w, RS, st["rs"])
        for j in range(KT):
            r = j % 2
            self.stt(TMP[r][:, 0:w], self.XT[:, j, a:b], self.PAR[:, l, qa, j, pb:pb + 1], RS[:, 0:w], ALU.mult, ALU.mult,
                     self.XTb.ts(j, j + 1, a, b) + [st["rs"], self.tC], [st["tmp"][r]])
            self.act(self.HT[:, j, a:b], TMP[r][:, 0:w], AF.Identity, [st["tmp"][r], self.tC], self.HTb.ts(j, j + 1, a, b),
                     bias=self.PAR[:, l, qb, j, pb:pb + 1])

    def postnorm_update(self, l, a, b, q, bi, MT, tMT, slot=None):
        w = b - a
        pb = self.nb if a < NCTX else bi
        if slot is None:
            SQ = self.SCb[:, 0:4096].rearrange("p (j n) -> p j n", j=KT)
            RS = self.SCf[:, 0:512]
            TMP = [self.SCf[:, 512:1024], self.SCf[:, 1024:1536]]
            st = self._pn
        else:
            SQ = self.SCb[:, slot * 2048:(slot + 1) * 2048].rearrange("p (j n) -> p j n", j=KT)
            RS = self.SCf[:, slot * 256:(slot + 1) * 256]
            TMP = [self.SCf[:, 512 + (2 * slot + i) * 256:512 + (2 * slot + i + 1) * 256] for i in range(2)]
            st = self._pn2[slot]
        for m in range(KT):
            self.tt("pool", SQ[:, m, 0:w], MT[:, m, 0:w], MT[:, m, 0:w], ALU.mult, [tMT[m]], [st["sqm"][m]])
        bk, bt = self.bank()
        for m in range(KT):
            self.mm(bk[:, 0:w], self.ones[:], SQ[:, m, 0:w], m == 0, m == KT - 1, [st["sqm"][m], self.tC], [bt])
        self.rstd_from_bank(bk, bt, w, RS, st["rs"])
        for m in range(KT):
            r = m % 2
            self.tt("dve", TMP[r][:, 0:w], MT[:, m, 0:w], RS[:, 0:w], ALU.mult, [tMT[m], st["rs"]], [st["tmp"][r]])
            xt = self.XTb.ts(m, m + 1, a, b)
            self.stt(self.XT[:, m, a:b], TMP[r][:, 0:w], self.PAR[:, l, q, m, pb:pb + 1], self.XT[:, m, a:b], ALU.mult, ALU.add,
                     [st["tmp"][r], self.tC] + xt, xt)

    def layer(self, l, bi):
        fw = self.fw
        last = (l == 1)
        self._pn = {"sq": T(), "rs": T(), "tmp": [T(), T()], "sqm": [T() for _ in range(KT)], "vf": [T(), T()], "ln": T()}
        fw.barrier()
        self.ps_banks = list(range(8))
        lat_tiles = [(NCTX + 512 * i, NCTX + 512 * (i + 1)) for i in range(4)]
        tiles = [(0, NCTX)] + lat_tiles
        self.prenorm(l, tiles[0][0], tiles[0][1], 0, 1, bi)
        self._win_seq = [(ti, pair) for ti, (a, b) in enumerate(tiles) for pair in range(5)
                         if not (last and a < NCTX and pair not in (2, 3))]
        self._win_pos = 0
        self._win_slots = {}
        self.w_in_prefetch(l)
        for ti, (a, b) in enumerate(tiles):
            if ti + 1 < len(tiles):
                self.prenorm(l, tiles[ti + 1][0], tiles[ti + 1][1], 0, 1, bi)
            self.w_in(l, [(a, b)], last, ti)
        self.dump(f"ua_{l}_{bi}", self.UA, self.UAb.ts(0, 2, 0, NT), BF16)
        self.dump(f"v_{l}_{bi}", self.V, self.tV, BF16)
        self.dump(f"us_{l}_{bi}", self.US, self.tUS, BF16)
        self.dump(f"pp_{l}_{bi}", self.PP, self.PPb.ts(0, 2, 0, NT), BF16)
        fw.barrier()
        if self.stop == "A":
            return
        self.ps_banks = [0, 1, 2, 3]
        self.sgu(l, last)
        self.ssm(l, last)
        self.pool(l, last)
        self.glu(l, last)
        self.dump(f"a_{l}_{bi}", self.UA, self.UAb.ts(0, 2, 0, NT), BF16)
        self.dump(f"s_{l}_{bi}", self.GT, self.GTb.ts(0, 4, 0, NT) + self.tUS, BF16)
        self.dump(f"p_{l}_{bi}", self.PP, self.PPb.ts(0, 2, 0, NT), BF16)
        fw.barrier()
        if self.stop == "B":
            return
        self.ps_banks = list(range(8))
        self.w_out(l, bi, last)
        self.dump(f"xmid_{l}_{bi}", self.XT[:], self.XTb.ts(0, KT, 0, NT))
        fw.barrier()
        if self.stop == "C":
            return
        self.ffn(l, bi, last)
        self.dump(f"xend_{l}_{bi}", self.XT[:], self.XTb.ts(0, KT, 0, NT))

    def w_in_prefetch(self, l):
        if self._win_pos >= len(self._win_seq):
            return
        key = self._win_seq[self._win_pos]
        self._win_pos += 1
        s = self.wslot()
        dst = self.WR[:, s, 0:2048].rearrange("p (s k c) -> p s k c", s=2, k=KT)
        self.dma("sp", dst, self.dr["wsc_in"][l, 2 * key[1]:2 * key[1] + 2].rearrange("s p k c -> p s k c"), [], [self.tWR[s]], self.dWR[s])
        self._win_slots[key] = s

    def w_in(self, l, tiles, last, ti):
        fw, dr = self.fw, self.dr
        WRv = lambda s: self.WR[:, s, 0:2048].rearrange("p (s k c) -> p s k c", s=2, k=KT)
        VF = [self.SCf[:, 1536:1792], self.SCf[:, 1792:2048]]
        VC = self.SCf[:, 2048:2304]
        VS = self.SCf[:, 2304:2560]
        S1 = self.SCf[:, 2560:2564]
        S2 = self.SCf[:, 2564:2568]
        tVF = self._pn["vf"]
        tLN = self._pn["ln"]
        nvf = 0
        for pair in range(5):
            if (ti, pair) not in self._win_slots:
                continue
            s = self._win_slots.pop((ti, pair))
            self.w_in_prefetch(l)
            W = WRv(s)
            tW = self.tWR[s]
            for (a, b) in tiles:
                w = b - a
                isctx = a < NCTX
                hts = self.HTb.ts(0, KT, a, b)
                if pair == 1:
                    for t0 in range(a, b, 128):
                        ck = t0 // 128
                        bk, bt = self.bank()
                        for kt in range(KT):
                            rhs = fr(W[:, 0:1, kt:kt + 1, 0:1], [[KT * 128, 2], [1, 128]])
                            self.mm(bk[:, 0:256], self.HT[:, kt, t0:t0 + 128], rhs, kt == 0, kt == KT - 1, hts + [tW], [bt])
                        r = nvf % 2
                        nvf += 1
                        vf = VF[r]
                        self.act(vf, bk[:, 0:256], AF.Gelu, [bt], [tVF[r]])
                        v3 = lambda ap: ap.rearrange("p (h d) -> p h d", h=4)
                        bc = lambda ap: fr(ap[:, 0:1], [[1, 4], [0, 64]])
                        fw.op("dve", (lambda o, i: (lambda h: h.tensor_reduce(o, i, AX.X, ALU.add)))(S1, v3(vf)), [tVF[r]], [tLN])
                        self.tsc(S1, S1, -1.0 / 64, None, ALU.mult, None, [tLN], [tLN])
                        self.tt("dve", v3(VC), v3(vf), bc(S1), ALU.add, [tVF[r], tLN], [tLN])
                        self.tt("dve", VS, VC, VC, ALU.mult, [tLN], [tLN])
                        fw.op("dve", (lambda o, i: (lambda h: h.tensor_reduce(o, i, AX.X, ALU.add)))(S2, v3(VS)), [tLN], [tLN])
                        self.act(S2, S2, AF.Sqrt, [tLN], [tLN], bias=EPS, scale=1.0 / 64)
                        fw.op("dve", (lambda o: (lambda h: h.reciprocal(o, o)))(S2), [tLN], [tLN])
                        self.tt("dve", v3(self.V[:, ck, :]), v3(VC), bc(S2), ALU.mult, [tLN], [self.tV[ck]])
                    continue
                for mm_ in range(2):
                    bk, bt = self.bank()
                    for kt in range(KT):
                        self.mm(bk[:, 0:w], W[:, mm_, kt, :], self.HT[:, kt, a:b], kt == 0, kt == KT - 1, hts + [tW], [bt])
                    if pair == 0:
                        self.act(self.UA[:, mm_, a:b], bk[:, 0:w], AF.Gelu, [bt], self.UAb.ts(mm_, mm_ + 1, a, b))
                    elif pair in (2, 3):
                        m = (pair - 2) * 2 + mm_
                        c0, nc_ = a // 8, w // 8
                        o3 = fr(self.US[:, m:m + 1, 0:1, c0:c0 + 1], [[NCH, 8], [1, nc_]])
                        i3 = fr(bk[:, 0:1], [[1, 8], [8, nc_]])
                        self.cp("act", o3, i3, [bt], [self.tUS[m]] + self.GTb.ts(m, m + 1, 0, NT))
                    else:
                        self.cp("dve", self.PP[:, mm_, a:b], bk[:, 0:w], [bt], self.PPb.ts(mm_, mm_ + 1, a, b))

    def sgu(self, l, last):
        for ck in range(2 if last else 0, 18):
            c0 = ck * 128
            for hp in range(2):
                qs = []
                bk, qt = self.bank67()
                for hh in range(2):
                    hd = 2 * hp + hh
                    qa = bk[:, hh * 128:(hh + 1) * 128]
                    self.mm(qa, self.V[:, ck, hp * 128:(hp + 1) * 128], self.sguTb[:, l, hd, :], True, False, [self.tV[ck], self.tC], [qt], inc=False)
                    o = (l * 4 + hd) * 128
                    self.mm(qa, self.ones[0:1, 0:128], self.sgubb[0:1, o:o + 128], False, True, [self.tC], [qt], inc=(hh == 1))
                    qs.append((qa, qt))
                for hh in range(2):
                    r0 = 64 * hh
                    ts_ = self.UAb.ts(hp, hp + 1, c0, c0 + 128)
                    self.tt("dve", self.UA[r0:r0 + 64, hp, c0:c0 + 128], qs[hh][0][r0:r0 + 64, :], self.UA[r0:r0 + 64, hp, c0:c0 + 128], ALU.mult,
                            [qs[hh][1]] + ts_, ts_)

    def pool(self, l, last):
        QT = [self.SCb[:, i * 128:(i + 1) * 128] for i in range(8)]
        tQ = [T() for _ in range(8)]

        def first(ti, m, slot):
            c0 = ti * 128
            bk, qt = self.bank67()
            qa = bk[:, 0:128]
            self.mm(qa, self.PP[:, m, c0:c0 + 128], self.poolwb[:, l, m, :], True, True, self.PPb.ts(m, m + 1, c0, c0 + 128) + [self.tC], [qt])
            self.cp("act", QT[slot], qa, [qt], [tQ[slot]])

        def finish(ti, m, rhs_list):
            c0 = ti * 128
            bk, qt = self.bank67()
            for gq in range(2):
                qa = bk[:, gq * 128:(gq + 1) * 128]
                n = len(rhs_list[gq])
                for i, (slot, rhs) in enumerate(rhs_list[gq]):
                    self.mm(qa, QT[slot], rhs, i == 0, i == n - 1, [tQ[slot], self.tC], [qt], inc=(gq == 1 and i == n - 1))
            for gq in range(2):
                qa = bk[:, gq * 128:(gq + 1) * 128]
                r0 = 64 * gq
                self.tsc(self.PP[r0:r0 + 64, m, c0:c0 + 128], qa[r0:r0 + 64, :], self.pscale[r0:r0 + 64, l, m:m + 1], None, ALU.mult, None,
                         [qt, self.tC], self.PPb.ts(m, m + 1, c0, c0 + 128))

        if not last:
            for m in range(2):
                for ti in range(2):
                    first(ti, m, 4 + 2 * m + ti)
            for m in range(2):
                for tt_ in range(2):
                    finish(tt_, m, [[(4 + 2 * m + st_, self.avgCb[:, 2 * m + gq, st_, tt_, :]) for st_ in range(2)] for gq in range(2)])
        n = 0
        for ti in range(2, 18):
            for m in range(2):
                slot = n % 4
                n += 1
                first(ti, m, slot)
                finish(ti, m, [[(slot, self.avgLb[:, 2 * m + gq, :])] for gq in range(2)])

    def glu(self, l, last):
        dr = self.dr
        s = self.wslot()
        GW = self.WR[:, s, 0:2048].rearrange("p (k c) -> p k c", k=4)
        self.dma("sp", GW, dr["wsc_glu"][l], [], [self.tWR[s]], self.dWR[s])
        SG = [self.SCb[:, 1024:1536], self.SCb[:, 1536:2048], self.SCb[:, 2048:2560], self.SCb[:, 2560:3072]]
        tSG = [T() for _ in range(4)]
        tiles = [(NCTX + 512 * i, NCTX + 512 * (i + 1)) for i in range(4)]
        if not last:
            tiles = [(0, NCTX)] + tiles
        for (a, b) in tiles:
            w = b - a
            gts = self.GTb.ts(0, 4, a, b)
            bks = []
            for m in range(4):
                bk, bt = self.bank()
                for kt in range(4):
                    self.mm(bk[:, 0:w], GW[:, kt, m * 128:(m + 1) * 128], self.GT[:, kt, a:b], kt == 0, kt == 3, gts + [self.tWR[s]], [bt])
                bks.append((bk, bt))
            for m in range(4):
                bk, bt = bks[m]
                self.act(SG[m][:, 0:w], bk[:, 0:w], AF.Sigmoid, [bt, self.tC], [tSG[m]], bias=self.glub[:, l, m:m + 1])
            for m in range(4):
                ts_ = self.GTb.ts(m, m + 1, a, b)
                self.tt("pool", self.GT[:, m, a:b], self.GT[:, m, a:b], SG[m][:, 0:w], ALU.mult, [tSG[m]] + ts_, ts_)

    def ssm(self, l, last):
        fw, dr = self.fw, self.dr
        HF = self.HTflat
        UTs = [HF[:, r * 2304:(r + 1) * 2304].rearrange("p (g c) -> p g c", g=8) for r in range(2)]
        GYs = [HF[:, 4608 + r * 2304:4608 + (r + 1) * 2304].rearrange("p (g c) -> p g c", g=8) for r in range(2)]
        Hbs = [HF[:, 9216 + i * 576:9216 + (i + 1) * 576].rearrange("p (r c) -> p r c", r=2) for i in range(4)]
        tUTs = [[T() for _ in range(8)] for _ in range(2)]
        tGYs = [[T() for _ in range(8)] for _ in range(2)]
        tHs = [T() for _ in range(4)]
        TB = [self.SCf[:, i * 1152:(i + 1) * 1152].rearrange("p (t r c) -> p t r c", t=2, r=2) for i in range(2)]
        tTB = [T(), T()]
        F2 = HF[:, 11520:14976].bitcast(F32)
        sets = [[self.SCf[:, 2304 + i * 576:2304 + (i + 1) * 576].rearrange("p (r c) -> p r c", r=2) for i in range(3)],
                [F2[:, i * 576:(i + 1) * 576].rearrange("p (r c) -> p r c", r=2) for i in range(3)]]
        tsets = [[T(), T(), T()], [T(), T(), T()]]
        nunit = 0

        def swapped(ap3, stride):
            return fr(ap3[:, 1:2, 0:1], [[-stride, 2], [1, NCH]])

        def sel_in(j):
            UT, tUT = UTs[j % 2], tUTs[j % 2]
            for gp in range(8):
                bk, bt = self.bank()
                for s in range(8):
                    self.mm(bk[:, 0:NCH], self.dselb[:, gp, 112 - 16 * s:240 - 16 * s], self.US[:, j, s, :], s == 0, s == 7,
                            [self.tUS[j], self.tC], [bt])
                self.cp("act", UT[:, gp, :], bk[:, 0:NCH], [bt], [tUT[gp]])

        def front(j, qq):
            nonlocal nunit
            UT, tUT = UTs[j % 2], tUTs[j % 2]
            q = j * 4 + qq
            s = self.wslot()
            SW = self.WR[:, s, 0:SSMW_COLS]
            tSW = self.tWR[s]
            self.dma("sp", SW, dr["ssmw"][l, q], [], [tSW], self.dWR[s])
            g0, g1 = 2 * qq, 2 * qq + 1
            Hk = []
            for k in range(2):
                u = nunit
                nunit += 1
                r = u % 2
                self.dma("sp", TB[r], dr["tab"][l, k, q].rearrange("p (t r c) -> p t r c", t=2, r=2), [], [tTB[r]], self.dsem_t[r])
                COS2, SIN2 = TB[r][:, 0], TB[r][:, 1]
                XP, XQ, G = sets[r]
                tXP, tXQ, tG = tsets[r]
                Hb, tH = Hbs[u % 4], tHs[u % 4]
                Hk.append((Hb, tH))
                xb = 4 + 2 * r
                X = self.ps[:, xb:xb + 2, 0:NCH]
                tX, tX2 = self.psT[xb], self.psT[xb + 1]
                zb = lambda v: SW[:, 256 + k * 512 + v * 128:256 + k * 512 + (v + 1) * 128]
                for ri in range(2):
                    self.mm(self.ps[:, xb + ri, 0:NCH], zb(2 * ri), UT[:, g0, :], True, False, [tSW, tUT[g0]], [tX, tX2], inc=False)
                    self.mm(self.ps[:, xb + ri, 0:NCH], zb(2 * ri + 1), UT[:, g1, :], False, True, [tSW, tUT[g1]], [tX, tX2], inc=(ri == 1))
                self.tt("dve", XP, X, COS2, ALU.mult, [tX, tX2, tTB[r]], [tXP])
                self.tt("dve", XQ, swapped(X, 512), SIN2, ALU.mult, [tX, tX2, tTB[r]], [tXQ])
                self.tt("dve", XP, XP, XQ, ALU.add, [tXP, tXQ], [tXP])
                rho = fr(self.RHO[:, l, k, q:q + 1], [[0, NCH]])
                for ri in range(2):
                    if k == 0:
                        self.scan(G[:, ri, :], rho, XP[:, ri, :], 0.0, [tXP, self.tC], [tG])
                    else:
                        rho32 = fr(self.RHO[:, l, k, q:q + 1], [[0, 32]])
                        rho256 = fr(self.RHO[:, l, k, q:q + 1], [[0, 256]])
                        self.scan(G[:, ri, 31::-1], rho32, XP[:, ri, 31::-1], 0.0, [tXP, self.tC], [tG])
                        self.scan(G[:, ri, 287:31:-1], rho256, XP[:, ri, 287:31:-1], G[:, ri, 0:1], [tXP, self.tC, tG], [tG])
                self.tt("pool", XP, G, COS2, ALU.mult, [tG, tTB[r]], [tXP])
                self.tt("pool", XQ, swapped(G, NCH), SIN2, ALU.mult, [tG, tTB[r]], [tXQ])
                self.tt("pool", Hb, XP, XQ, ALU.subtract, [tXP, tXQ], [tH])
            return (SW, tSW, Hk)

        def back(j, qq, ctxt):
            SW, tSW, Hk = ctxt
            UT, tUT = UTs[j % 2], tUTs[j % 2]
            GY, tGY = GYs[j % 2], tGYs[j % 2]
            for gq in range(2):
                g = 2 * qq + gq
                bk, bt = self.bank()
                cpv = lambda k, v: SW[:, 1280 + k * 512 + v * 128:1280 + k * 512 + (v + 1) * 128]
                self.mm(bk[:, 0:NCH], SW[:, gq * 128:(gq + 1) * 128], UT[:, g, :], True, False, [tSW, tUT[g]], [bt], inc=False)
                for ri in range(2):
                    self.mm(bk[:, 1:NCH], cpv(0, 2 * ri + gq), Hk[0][0][:, ri, 0:NCH - 1], False, False, [tSW, Hk[0][1]], [bt], inc=False)
                segs = ((32, 287, 33), (287, 288, 0), (0, 31, 1))
                for si, (o0, o1, i0_) in enumerate(segs):
                    for ri in range(2):
                        lastmm = (si == 2 and ri == 1)
                        self.mm(bk[:, o0:o1], cpv(1, 2 * ri + gq), Hk[1][0][:, ri, i0_:i0_ + (o1 - o0)], False, lastmm, [tSW, Hk[1][1]], [bt], inc=lastmm)
                self.act(GY[:, g, :], bk[:, 0:NCH], AF.Gelu, [bt], [tGY[g]])

        def sel_out(j):
            GY, tGY = GYs[j % 2], tGYs[j % 2]
            for t in range(8):
                bk, bt = self.bank()
                for gp in range(8):
                    self.mm(bk[:, 0:NCH], self.dselb[:, t, 112 - 16 * gp:240 - 16 * gp], GY[:, gp, :], gp == 0, gp == 7, [tGY[gp], self.tC], [bt])
                self.cp("dve", self.GT[:, j, t:NT:8], bk[:, 0:NCH], [bt], [self.tUS[j]] + self.GTb.ts(j, j + 1, 0, NT))

        jobs = [(j, qq) for j in range(4) for qq in range(4)]
        sel_in(0)
        sel_in(1)
        ctx_next = front(*jobs[0])
        for i, (j, qq) in enumerate(jobs):
            ctx_cur = ctx_next
            if i + 1 < len(jobs):
                ctx_next = front(*jobs[i + 1])
            back(j, qq, ctx_cur)
            if qq == 3:
                sel_out(j)
                if j + 2 < 4:
                    sel_in(j + 2)

    def w_out(self, l, bi, last):
        dr = self.dr
        HF = self.HTflat
        MTs = [HF[:, r * 4096:(r + 1) * 4096].bitcast(F32).rearrange("p (m n) -> p m n", m=KT) for r in range(2)]
        WO = HF[:, 8192:16384].rearrange("p (m k c) -> p m k c", m=8, k=KT)
        tWO = T()
        self.dma("sp", WO, dr["wsc_out"][l].rearrange("m p k c -> p m k c"), [], [tWO], self.dsem_wo)
        tMTs = [[T() for _ in range(KT)] for _ in range(2)]
        self._pn2 = [{"sq": T(), "rs": T(), "tmp": [T(), T()], "sqm": [T() for _ in range(KT)]} for _ in range(2)]
        tiles = [(NCTX + 256 * i, NCTX + 256 * (i + 1)) for i in range(8)]
        if not last:
            tiles = [(0, NCTX)] + tiles
        for ti, (a, b) in enumerate(tiles):
            w = b - a
            MT, tMT = MTs[ti % 2], tMTs[ti % 2]
            for m in range(KT):
                bk, bt = self.bank()
                for kt in range(KT):
                    if kt < 2:
                        rhs, rt = self.UA[:, kt, a:b], self.UAb.ts(kt, kt + 1, a, b)
                    elif kt < 6:
                        rhs, rt = self.GT[:, kt - 2, a:b], self.GTb.ts(kt - 2, kt - 1, a, b)
                    else:
                        rhs, rt = self.PP[:, kt - 6, a:b], self.PPb.ts(kt - 6, kt - 5, a, b)
                    self.mm(bk[:, 0:w], WO[:, m, kt, :], rhs, kt == 0, kt == KT - 1, rt + [tWO], [bt])
                self.cp("act", MT[:, m, 0:w], bk[:, 0:w], [bt], [tMT[m]])
            self.postnorm_update(l, a, b, 2, bi, MT, tMT, slot=ti % 2)

    def ffn(self, l, bi, last):
        fw, dr = self.fw, self.dr
        if last:
            groups = [[(256, 768), (768, 1280)], [(1280, 1792), (1792, 2304)]]
        else:
            groups = [[(0, 256), (256, 768)], [(768, 1280), (1280, 1536)], [(1536, 2048), (2048, 2304)]]
        HF = self.HTflat
        MT = HF[:, 0:8192].bitcast(F32).rearrange("p (m n) -> p m n", m=KT)
        tMT = [T() for _ in range(KT)]
        SG = [self.SCb[:, 0:512], self.SCb[:, 512:1024], self.SCb[:, 1024:1536], self.SCb[:, 1536:2048]]
        tSG = [T() for _ in range(4)]
        nsg = 0
        tA = [[T() for _ in range(2)] for _ in range(JT)]
        self._pn = {"sq": T(), "rs": T(), "tmp": [T(), T()], "sqm": [T() for _ in range(KT)], "vf": [T(), T()], "ln": T()}
        for grp in groups:
            g0 = grp[0][0]
            glen = grp[-1][1] - g0
            A = self.PH[:, 0:JT * glen].rearrange("p (j n) -> p j n", j=JT)
            for (a, b) in grp:
                self.prenorm(l, a, b, 3, 4, bi)
            for j in range(JT):
                s = self.wslot()
                W = self.WR[:, s, 0:2048].rearrange("p (s k c) -> p s k c", s=2, k=KT)
                self.dma("sp", W[:, 0], dr["wsc_g"][l, j], [], [self.tWR[s]], self.dWR[s])
                self.dma("sp", W[:, 1], dr["wsc_u"][l, j], [], [self.tWR[s]], self.dWR[s])
                for ci, (a, b) in enumerate(grp):
                    w = b - a
                    hts = self.HTb.ts(0, KT, a, b)
                    bg, tg = self.bank()
                    for kt in range(KT):
                        self.mm(bg[:, 0:w], W[:, 0, kt, :], self.HT[:, kt, a:b], kt == 0, kt == KT - 1, hts + [self.tWR[s]], [tg])
                    bu, tu = self.bank()
                    for kt in range(KT):
                        self.mm(bu[:, 0:w], W[:, 1, kt, :], self.HT[:, kt, a:b], kt == 0, kt == KT - 1, hts + [self.tWR[s]], [tu])
                    r = nsg % 4
                    nsg += 1
                    self.act(SG[r][:, 0:w], bg[:, 0:w], AF.Silu, [tg], [tSG[r]])
                    self.tt("dve", A[:, j, a - g0:b - g0], bu[:, 0:w], SG[r][:, 0:w], ALU.mult, [tu, tSG[r]], [tA[j][ci]])
            for ci, (a, b) in enumerate(grp):
                w = b - a
                for m in range(KT):
                    s = self.wslot()
                    WD = self.WR[:, s, 0:JT * 128].rearrange("p (j c) -> p j c", j=JT)
                    self.dma("sp", WD, dr["wsc_d"][l, m], [], [self.tWR[s]], self.dWR[s])
                    bk, bt = self.bank()
                    for j in range(JT):
                        self.mm(bk[:, 0:w], WD[:, j, :], A[:, j, a - g0:b - g0], j == 0, j == JT - 1, [tA[j][ci], self.tWR[s]], [bt])
                    self.cp("act", MT[:, m, 0:w], bk[:, 0:w], [bt], [tMT[m]])
                self.postnorm_update(l, a, b, 5, bi, MT, tMT)
        fw.barrier()


_CACHE = {}


def _get_nc(nb, dbg=(), nlayers=2, skip=(), stop=None):
    key = (nb, tuple(sorted(dbg)), nlayers, tuple(sorted(skip)), stop)
    if key not in _CACHE:
        g = Gen(nb, dbg, nlayers, skip, stop)
        g.build()
        _CACHE[key] = g
    return _CACHE[key]


def run_cores(inp, batch_lists, dbg=(), nlayers=2, trace=False, skip=(), stop=None):
    nb = len(batch_lists[0])
    g = _get_nc(nb, dbg, nlayers, skip, stop)
    sh = _prep_shared(inp)
    x = np.asarray(inp["x"], np.float32)
    ctx = np.asarray(inp["ctx"], np.float32)
    c = np.asarray(inp["c"], np.float32)
    c_ctx = np.asarray(inp["c_ctx"], np.float32)
    in_maps = []
    for bl in batch_lists:
        m = dict(sh)
        m["x"] = np.ascontiguousarray(x[bl])
        m["ctx"] = np.ascontiguousarray(ctx[bl])
        cc = np.concatenate([c[bl], c_ctx[None]], axis=0)
        m["cT"] = np.ascontiguousarray(cc.reshape(nb + 1, KT, 128).transpose(2, 1, 0))
        in_maps.append(m)
    res = run_bass_kernel_spmd(g.nc, in_maps, core_ids=list(range(len(batch_lists))), trace=trace)
    return res


def kernel(**inputs):
    nb = np.asarray(inputs["x"]).shape[0] // NCORES
    batch_lists = [list(range(c * nb, (c + 1) * nb)) for c in range(NCORES)]
    res = run_cores(inputs, batch_lists)
    out = np.concatenate([np.asarray(r["out"], np.float32) for r in res.results], axis=0)
    return out
```
